# Optimizing a Trainium2 kernel written in Bass

```python
import jax, jax.numpy as jnp
from jax import lax
import numpy as np

D_MODEL = 1024
BATCH = 16
SEQ = 4096
DEPTH = 1
DEC_BATCH = 8
DEC_SEQ = 64
PAST_LEN = 2048

CHUNK = 64
N_META = 16
D_MIX = D_MODEL
RW_HEADS = 8
RW_HEAD = 64
RW_WIDTH = RW_HEADS * RW_HEAD
W_LORA = 64
A_LORA = 64
G_LORA = 128
RW_COLS = 3 * RW_WIDTH + W_LORA + A_LORA + G_LORA
MLA_HEADS = 8
NOPE = 64
ROPE = 32
VDIM = 64
Q_LORA = 256
KV_LORA = 128
MLA_COLS = Q_LORA + KV_LORA + ROPE
N_IN = RW_COLS + MLA_COLS
D_FF = 2816
Q_BLOCK = 128
ROPE_THETA = 10000.0
NORM_EPS = 1e-6
GN_EPS = 64e-5
MLA_SCALE = (NOPE + ROPE) ** -0.5

kernel_name = "hymba_rwkv7_mla_macaron_stream"


def rmsnorm(x, g):
    xf = x.astype(jnp.float32)
    y = xf * lax.rsqrt(jnp.mean(xf * xf, axis=-1, keepdims=True) + NORM_EPS)
    return (y * g.astype(jnp.float32)).astype(x.dtype)


def swiglu(h, w1, w3, w2):
    return (jax.nn.silu(h @ w1) * (h @ w3)) @ w2


def rope(x, pos):
    half = ROPE // 2
    inv = ROPE_THETA ** (-jnp.arange(half, dtype=jnp.float32) / half)
    ang = pos.astype(jnp.float32)[:, None] * inv[None, :]
    shape = (1, pos.shape[0]) + (1,) * (x.ndim - 3) + (half,)
    cos = jnp.cos(ang).reshape(shape)
    sin = jnp.sin(ang).reshape(shape)
    xf = x.astype(jnp.float32)
    x1, x2 = xf[..., :half], xf[..., half:]
    return jnp.concatenate([x1 * cos - x2 * sin, x1 * sin + x2 * cos], axis=-1).astype(x.dtype)


def chunk_id(pos):
    return jnp.where(pos < N_META, -1, (pos - N_META) // CHUNK)


def pre_mix(x, L):
    x = x + 0.5 * swiglu(rmsnorm(x, L["norm_ffn1"]), L["ffn1_w1"], L["ffn1_w3"], L["ffn1_w2"])
    p = rmsnorm(x, L["norm_mix"]) @ L["w_in"]
    return x, p


def rwkv_scan(S0, r, w, k, v, kk, kka):
    def step(S, xs):
        r_t, w_t, k_t, v_t, kk_t, kka_t = xs
        sa = jnp.einsum("bhvk,bhk->bhv", S, -kk_t)
        S = S * w_t[:, :, None, :] + sa[..., None] * kka_t[:, :, None, :] + v_t[..., :, None] * k_t[..., None, :]
        return S, jnp.einsum("bhvk,bhk->bhv", S, r_t)
    xs = tuple(jnp.moveaxis(t, 1, 0) for t in (r, w, k, v, kk, kka))
    S, ys = lax.scan(step, S0, xs)
    return S, jnp.moveaxis(ys, 0, 1)


def rwkv_mix(s, wkv0, L):
    B, T, _ = s.shape
    f = lambda name: L[name].astype(jnp.float32)
    sf = s.astype(jnp.float32)
    cuts = [RW_WIDTH, 2 * RW_WIDTH, 3 * RW_WIDTH, 3 * RW_WIDTH + W_LORA, 3 * RW_WIDTH + W_LORA + A_LORA]
    r, k, v, wl, al, gl = jnp.split(sf, cuts, axis=-1)
    logw = -jax.nn.softplus(-(f("w0") + jnp.tanh(wl) @ f("w_w2"))) - 0.5
    decay = jnp.exp(-jnp.exp(logw))
    a = jax.nn.sigmoid(f("a0") + al @ f("w_a2"))
    g = jax.nn.sigmoid(gl) @ f("w_g2")
    hd = lambda t: t.reshape(B, T, RW_HEADS, RW_HEAD)
    kk = hd(k * f("k_k"))
    kk = kk / jnp.maximum(jnp.sqrt(jnp.sum(kk * kk, axis=-1, keepdims=True)), 1e-12)
    k = k * (1.0 + (a - 1.0) * f("k_a"))
    r, k, v, decay, a = hd(r), hd(k), hd(v), hd(decay), hd(a)
    S, y = rwkv_scan(wkv0.astype(jnp.float32), r, decay, k, v, kk, kk * a)
    mu = jnp.mean(y, axis=-1, keepdims=True)
    var = jnp.mean(jnp.square(y - mu), axis=-1, keepdims=True)
    y = ((y - mu) * lax.rsqrt(var + GN_EPS)).reshape(B, T, RW_WIDTH) * f("ln_x_w") + f("ln_x_b")
    y = y + (jnp.sum(r * k * f("r_k"), axis=-1, keepdims=True) * v).reshape(B, T, RW_WIDTH)
    return (y * g).astype(s.dtype), S.astype(wkv0.dtype)


def mla_kv(p, pos, L):
    o = RW_COLS + Q_LORA
    c = rmsnorm(p[..., o:o + KV_LORA], L["kv_norm"])
    kr = rope(p[..., o + KV_LORA:o + KV_LORA + ROPE], pos)
    return c, kr


def mla_q(p, pos, L):
    cq = rmsnorm(p[..., RW_COLS:RW_COLS + Q_LORA], L["q_norm"])
    q = jnp.einsum("btr,rhd->bthd", cq, L["w_q_up"])
    q_r = rope(q[..., NOPE:], pos)
    q_lat = jnp.einsum("bthd,chd->bthc", q[..., :NOPE], L["w_uk"])
    return q_lat, q_r


def attend(q_lat, q_r, c, kr, mask):
    s = (jnp.einsum("bqhc,bkc->bhqk", q_lat, c) + jnp.einsum("bqhr,bkr->bhqk", q_r, kr)).astype(jnp.float32) * MLA_SCALE
    if mask is not None:
        s = jnp.where(mask, s, -1e30)
    pr = jax.nn.softmax(s, axis=-1).astype(c.dtype)
    return jnp.einsum("bhqk,bkc->bqhc", pr, c)


def prompt_attention(q_lat, q_r, c, kr):
    B, T = q_lat.shape[0], q_lat.shape[1]
    nb = -(-T // Q_BLOCK)
    pad = nb * Q_BLOCK - T
    def blocks(t):
        t = jnp.pad(t, ((0, 0), (0, pad), (0, 0), (0, 0)))
        return jnp.moveaxis(t.reshape((B, nb, Q_BLOCK) + t.shape[2:]), 1, 0)
    cid_k = chunk_id(jnp.arange(T))
    def one(args):
        ql, qr, i = args
        cid_q = chunk_id(i * Q_BLOCK + jnp.arange(Q_BLOCK))
        return attend(ql, qr, c, kr, cid_q[:, None] >= cid_k[None, :])
    out = lax.map(one, (blocks(q_lat), blocks(q_r), jnp.arange(nb)))
    return jnp.moveaxis(out, 0, 1).reshape(B, nb * Q_BLOCK, MLA_HEADS, KV_LORA)[:, :T]


def layer(x, L, pos, shift_prev, wkv0, prefix):
    B, T, _ = x.shape
    x, p = pre_mix(x, L)
    p_rw = p[..., :RW_COLS]
    prev = jnp.concatenate([shift_prev.astype(p.dtype), p_rw[:, :-1]], axis=1)
    rw_out, wkv = rwkv_mix(p_rw + L["mu_shift"] * (prev - p_rw), wkv0, L)
    c, kr = mla_kv(p, pos, L)
    q_lat, q_r = mla_q(p, pos, L)
    if prefix is None:
        lat = prompt_attention(q_lat, q_r, c, kr)
    else:
        c_all = jnp.concatenate([prefix[0].astype(c.dtype), c], axis=1)
        kr_all = jnp.concatenate([prefix[1].astype(kr.dtype), kr], axis=1)
        lat = attend(q_lat, q_r, c_all, kr_all, None)
    mla_out = jnp.einsum("bqhc,chd->bqhd", lat, L["w_uv"]).reshape(B, T, MLA_HEADS * VDIM)
    x = x + jnp.concatenate([rw_out, mla_out], axis=-1) @ L["w_out"]
    x = x + 0.5 * swiglu(rmsnorm(x, L["norm_ffn2"]), L["ffn2_w1"], L["ffn2_w3"], L["ffn2_w2"])
    return x, p_rw[:, -1:], wkv, c, kr


def setup_inputs(seed: int = 0) -> dict:
    key = jax.random.key(seed)
    ks = list(jax.random.split(key, 40))
    cnt = [0]
    def nk():
        cnt[0] += 1
        return ks[cnt[0] - 1]
    f32 = jnp.float32
    def nrm(shape, scale=1.0):
        return jax.random.normal(nk(), shape, f32) * scale
    def gain(shape):
        return 1.0 + nrm(shape, 0.01)
    return {
        "x_prompt": nrm((BATCH, SEQ, D_MODEL)),
        "x_sample": nrm((DEC_BATCH, DEC_SEQ, D_MODEL)),
        "cache_ckv": nrm((DEPTH, DEC_BATCH, PAST_LEN, KV_LORA)),
        "cache_krope": nrm((DEPTH, DEC_BATCH, PAST_LEN, ROPE)),
        "state_wkv": nrm((DEPTH, DEC_BATCH, RW_HEADS, RW_HEAD, RW_HEAD), 0.3),
        "state_shift": nrm((DEPTH, DEC_BATCH, 1, RW_COLS)),
        "meta_tokens": nrm((N_META, D_MODEL)),
        "norm_ffn1": gain((DEPTH, D_MODEL)),
        "ffn1_w1": nrm((DEPTH, D_MODEL, D_FF), D_MODEL ** -0.5),
        "ffn1_w3": nrm((DEPTH, D_MODEL, D_FF), D_MODEL ** -0.5),
        "ffn1_w2": nrm((DEPTH, D_FF, D_MODEL), D_FF ** -0.5),
        "norm_mix": gain((DEPTH, D_MODEL)),
        "w_in": nrm((DEPTH, D_MODEL, N_IN), D_MODEL ** -0.5),
        "mu_shift": jax.random.uniform(nk(), (DEPTH, RW_COLS), f32),
        "w0": nrm((DEPTH, RW_WIDTH), 0.5),
        "w_w2": nrm((DEPTH, W_LORA, RW_WIDTH), W_LORA ** -0.5),
        "a0": nrm((DEPTH, RW_WIDTH), 0.1),
        "w_a2": nrm((DEPTH, A_LORA, RW_WIDTH), 0.5 * A_LORA ** -0.5),
        "w_g2": nrm((DEPTH, G_LORA, RW_WIDTH), G_LORA ** -0.5),
        "k_k": 1.0 + nrm((DEPTH, RW_WIDTH), 0.1),
        "k_a": 1.0 + nrm((DEPTH, RW_WIDTH), 0.1),
        "r_k": nrm((DEPTH, RW_HEADS, RW_HEAD), 0.1),
        "ln_x_w": gain((DEPTH, RW_WIDTH)),
        "ln_x_b": nrm((DEPTH, RW_WIDTH), 0.01),
        "q_norm": gain((DEPTH, Q_LORA)),
        "w_q_up": nrm((DEPTH, Q_LORA, MLA_HEADS, NOPE + ROPE), Q_LORA ** -0.5),
        "kv_norm": gain((DEPTH, KV_LORA)),
        "w_uk": nrm((DEPTH, KV_LORA, MLA_HEADS, NOPE), KV_LORA ** -0.5),
        "w_uv": nrm((DEPTH, KV_LORA, MLA_HEADS, VDIM), KV_LORA ** -0.5),
        "w_out": nrm((DEPTH, D_MIX, D_MODEL), D_MIX ** -0.5),
        "norm_ffn2": gain((DEPTH, D_MODEL)),
        "ffn2_w1": nrm((DEPTH, D_MODEL, D_FF), D_MODEL ** -0.5),
        "ffn2_w3": nrm((DEPTH, D_MODEL, D_FF), D_MODEL ** -0.5),
        "ffn2_w2": nrm((DEPTH, D_FF, D_MODEL), D_FF ** -0.5),
        "final_norm": gain((D_MODEL,)),
    }


def reference(x_prompt, x_sample, cache_ckv, cache_krope, state_wkv, state_shift, meta_tokens,
              norm_ffn1, ffn1_w1, ffn1_w3, ffn1_w2, norm_mix, w_in, mu_shift, w0, w_w2, a0, w_a2,
              w_g2, k_k, k_a, r_k, ln_x_w, ln_x_b, q_norm, w_q_up, kv_norm, w_uk, w_uv, w_out,
              norm_ffn2, ffn2_w1, ffn2_w3, ffn2_w2, final_norm):
    P = dict(norm_ffn1=norm_ffn1, ffn1_w1=ffn1_w1, ffn1_w3=ffn1_w3, ffn1_w2=ffn1_w2, norm_mix=norm_mix,
             w_in=w_in, mu_shift=mu_shift, w0=w0, w_w2=w_w2, a0=a0, w_a2=w_a2, w_g2=w_g2, k_k=k_k,
             k_a=k_a, r_k=r_k, ln_x_w=ln_x_w, ln_x_b=ln_x_b, q_norm=q_norm, w_q_up=w_q_up,
             kv_norm=kv_norm, w_uk=w_uk, w_uv=w_uv, w_out=w_out, norm_ffn2=norm_ffn2,
             ffn2_w1=ffn2_w1, ffn2_w3=ffn2_w3, ffn2_w2=ffn2_w2)
    B, S_len, _ = x_prompt.shape
    Bd, Tn, _ = x_sample.shape
    past = cache_ckv.shape[2]
    dt = x_prompt.dtype
    meta = meta_tokens.astype(dt)
    x = jnp.concatenate([jnp.broadcast_to(meta[None], (B, N_META, D_MODEL)), x_prompt], axis=1)
    xs = x_sample
    m = meta[None]
    pos_p = jnp.arange(N_META + S_len)
    pos_m = jnp.arange(N_META)
    pos_s = N_META + past + jnp.arange(Tn)
    ckv_p, kr_p, wkv_p, sh_p = [], [], [], []
    ckv_s, kr_s, wkv_s, sh_s = [], [], [], []
    for l in range(DEPTH):
        L = {name: arr[l] for name, arr in P.items()}
        x, sh, wkv, c, kr = layer(x, L, pos_p, jnp.zeros((B, 1, RW_COLS), dt),
                                  jnp.zeros((B, RW_HEADS, RW_HEAD, RW_HEAD), dt), None)
        ckv_p.append(c); kr_p.append(kr); wkv_p.append(wkv); sh_p.append(sh)
        _, p_m = pre_mix(m, L)
        c_m, kr_m = mla_kv(p_m, pos_m, L)
        prefix = (jnp.concatenate([jnp.broadcast_to(c_m, (Bd, N_META, KV_LORA)), cache_ckv[l].astype(c_m.dtype)], axis=1),
                  jnp.concatenate([jnp.broadcast_to(kr_m, (Bd, N_META, ROPE)), cache_krope[l].astype(kr_m.dtype)], axis=1))
        xs, sh2, wkv2, c2, kr2 = layer(xs, L, pos_s, state_shift[l], state_wkv[l], prefix)
        ckv_s.append(c2); kr_s.append(kr2); wkv_s.append(wkv2); sh_s.append(sh2)
        if l + 1 < DEPTH:
            m = layer(m, L, pos_m, jnp.zeros((1, 1, RW_COLS), dt),
                      jnp.zeros((1, RW_HEADS, RW_HEAD, RW_HEAD), dt), None)[0]
    y_prompt = rmsnorm(x, final_norm)[:, N_META:]
    y_sample = rmsnorm(xs, final_norm)
    return (y_prompt, y_sample,
            jnp.stack(ckv_p), jnp.stack(kr_p), jnp.stack(wkv_p), jnp.stack(sh_p),
            jnp.stack(ckv_s), jnp.stack(kr_s), jnp.stack(wkv_s), jnp.stack(sh_s))
```

```python
import math
from contextlib import ExitStack

import numpy as np
import concourse.bass as bass
import concourse.mybir as mybir
from concourse.bass_utils import run_bass_kernel_spmd

F32 = mybir.dt.float32
BF16 = mybir.dt.bfloat16
ALU = mybir.AluOpType
AF = mybir.ActivationFunctionType
AX = mybir.AxisListType

ENGS = ('pe', 'act', 'dve', 'pool', 'sp')
SEM_LIMIT = 30000

D = 1024
DFF = 2816
NFC = 22
N_META = 16
PAST = 2048
NT = 256
C0 = math.exp(-0.5)
MLA_SCALE = 96 ** -0.5
NORM_EPS = 1e-6
GN_EPS = 64e-5


class COp:
    __slots__ = ("eng", "idx", "fn", "need", "sig")

    def __init__(self, eng, idx, fn):
        self.eng = eng
        self.idx = idx
        self.fn = fn
        self.need = False
        self.sig = None


class Sched:
    def __init__(self, nc, es):
        self.nc = nc
        self.es = es
        self.prog = {e: [] for e in ENGS}
        self.seen = {e: {} for e in ENGS}
        self.lastw = {}
        self.readers = {}
        self.nsem = 0
        self.rings = {}
        self.nops = {e: 0 for e in ENGS}
        self.cnt = {e: 0 for e in ENGS}

    def newsem(self):
        s = self.es.enter_context(self.nc.semaphore("s%d" % self.nsem))
        self.nsem += 1
        return s

    def _deps(self, eng, reads, writes):
        toks = []
        skip_own = (eng == 'pe')

        def own(t):
            return t[0] == 'c' and t[1].eng == eng

        for k in reads:
            t = self.lastw.get(k)
            if t is not None and not (skip_own and own(t)):
                toks.append(t)
        for k in writes:
            t = self.lastw.get(k)
            if t is not None and not (skip_own and own(t)):
                toks.append(t)
            for t in self.readers.get(k, ()):
                if not (skip_own and own(t)):
                    toks.append(t)
        best = {}
        for t in toks:
            if t[0] == 'c':
                key = ('c', t[1].eng)
                val = t[1].idx
            else:
                key = ('d', id(t[1]))
                val = t[2]
            if key not in best or best[key][0] < val:
                best[key] = (val, t)
        seen = self.seen[eng]
        out = []
        for key, (val, t) in best.items():
            if seen.get(key, -1) < val:
                seen[key] = val
                out.append(t)
        return out

    def _wait(self, eng, t):
        if t[0] == 'c':
            t[1].need = True
        self.prog[eng].append(('w', t))

    def _record(self, tok, reads, writes):
        for k in writes:
            self.lastw[k] = tok
            self.readers[k] = []
        for k in reads:
            self.readers.setdefault(k, []).append(tok)

    def op(self, eng, fn, reads=(), writes=(), inc=True):
        ps_r = [k for k in reads if k.startswith('ps')]
        if ps_r:
            reads = [k for k in reads if not k.startswith('ps')]
            writes = list(writes) + ps_r
        for t in self._deps(eng, reads, writes):
            self._wait(eng, t)
        o = COp(eng, self.cnt[eng], fn)
        self.cnt[eng] += 1
        self.prog[eng].append(('c', o))
        self.nops[eng] += 1
        tok = ('c', o)
        self._record(tok, reads, writes)
        return tok

    def dma(self, eng, chan, pairs, reads=(), writes=(), **kw):
        ring = self.rings.setdefault(eng, {'sems': [], 'i': 0})
        nring = 24 if eng == 'sp' else 16
        if len(ring['sems']) < nring:
            ring['sems'].append([self.newsem(), 0])
        ent = ring['sems'][ring['i'] % nring]
        ring['i'] += 1
        if ent[1] + 16 * len(pairs) >= SEM_LIMIT:
            s0, v0 = ent[0], ent[1]
            self.prog[eng].append(('w', ('d', s0, v0)))
            ent[0], ent[1] = self.newsem(), 0
        sem, cnt = ent[0], ent[1]
        key = ('d', id(sem))
        if cnt > 0 and self.seen[eng].get(key, -1) < cnt:
            self.seen[eng][key] = cnt
            self.prog[eng].append(('w', ('d', sem, cnt)))
        for t in self._deps(eng, reads, writes):
            self._wait(eng, t)
        for (o, i) in pairs:
            self.prog[eng].append(('raw', lambda h, o=o, i=i, sem=sem: h.dma_start(out=o, in_=i, **kw).then_inc(sem, 16)))
            ent[1] += 16
            self.nops[eng] += 1
        tok = ('d', sem, ent[1])
        self._record(tok, reads, writes)
        return tok

    def finish(self, eng='sp'):
        toks = []
        for k, t in self.lastw.items():
            toks.append(t)
        best = {}
        for t in toks:
            if t[0] == 'c':
                key = ('c', t[1].eng)
                val = t[1].idx
            else:
                key = ('d', id(t[1]))
                val = t[2]
            if key not in best or best[key][0] < val:
                best[key] = (val, t)
        for key, (val, t) in best.items():
            if self.seen[eng].get(key, -1) < val:
                self.seen[eng][key] = val
                self._wait(eng, t)

    def emit(self):
        nc = self.nc
        prog = self.prog
        nincs = {}
        for e in ENGS:
            sem, c = None, 0
            n = 0
            for ent in prog[e]:
                if ent[0] == 'c' and ent[1].need:
                    if sem is None or c >= SEM_LIMIT:
                        sem, c = self.newsem(), 0
                    c += 1
                    n += 1
                    ent[1].sig = (sem, c)
            nincs[e] = n
        print("incs:", nincs, flush=True)

        def run(e, h):
            for ent in prog[e]:
                k = ent[0]
                if k == 'c':
                    o = ent[1]
                    ins = o.fn(h)
                    if o.need:
                        ins.then_inc(o.sig[0], 1)
                elif k == 'w':
                    t = ent[1]
                    if t[0] == 'c':
                        h.wait_ge(t[1].sig[0], t[1].sig[1])
                    else:
                        h.wait_ge(t[1], t[2])
                else:
                    ent[1](h)

        with nc.Block() as block:
            @block.tensor
            def _(e):
                run('pe', e)

            @block.scalar
            def _(e):
                run('act', e)

            @block.vector
            def _(e):
                run('dve', e)

            @block.gpsimd
            def _(e):
                run('pool', e)

            @block.sync
            def _(e):
                run('sp', e)


class StopBuild(Exception):
    pass


class RR:
    def __init__(self, items):
        self.items = items
        self.i = 0

    def __call__(self):
        r = self.items[self.i % len(self.items)]
        self.i += 1
        return r


W_NAMES = ["norm_ffn1", "ffn1_w1", "ffn1_w3", "ffn1_w2", "norm_mix", "w_in", "mu_shift", "w0", "w_w2", "a0",
           "w_a2", "w_g2", "k_k", "k_a", "r_k", "ln_x_w", "ln_x_b", "q_norm", "w_q_up", "kv_norm", "w_uk", "w_uv",
           "w_out", "norm_ffn2", "ffn2_w1", "ffn2_w3", "ffn2_w2", "final_norm"]
W_SHAPES = {
    "norm_ffn1": [D], "ffn1_w1": [D, DFF], "ffn1_w3": [D, DFF], "ffn1_w2": [DFF, D], "norm_mix": [D],
    "w_in": [D, 2208], "mu_shift": [1792], "w0": [512], "w_w2": [64, 512], "a0": [512], "w_a2": [64, 512],
    "w_g2": [128, 512], "k_k": [512], "k_a": [512], "r_k": [512], "ln_x_w": [512], "ln_x_b": [512],
    "q_norm": [256], "w_q_up": [256, 768], "kv_norm": [128], "w_uk": [128, 512], "w_uv": [128, 512],
    "w_out": [D, D], "norm_ffn2": [D], "ffn2_w1": [D, DFF], "ffn2_w3": [D, DFF], "ffn2_w2": [DFF, D],
    "final_norm": [D],
}


def build(SEQ=4096, NSEQ=2, stop=99):
    nc = bass.Bass("TRN2", target_bir_lowering=False)
    es = ExitStack()
    with es:
        S = Sched(nc, es)

        def din(name, shape, dt=F32):
            return nc.dram_tensor(name, shape, dt, kind="ExternalInput").ap()

        def dout(name, shape):
            return nc.dram_tensor(name, shape, F32, kind="ExternalOutput").ap()

        def sb(name, shape, dt=F32):
            return es.enter_context(nc.sbuf_tensor(name, shape, dt))

        xp = din("xp", [NSEQ, SEQ, D])
        xs = din("xs", [64, D])
        cckv = din("cckv", [PAST, 128])
        ckr = din("ckr", [PAST, 32])
        swkv = din("swkv", [8, 64, 64])
        ssh = din("ssh", [1792])
        meta = din("meta", [N_META, D])
        W = {n: din(n, W_SHAPES[n]) for n in W_NAMES}
        ident_d = din("c_ident", [128, 128])
        masks_d = din("c_masks", [3, 64, 64], BF16)
        tri_d = din("c_tri", [2, 64, 64])
        tabP = din("c_tabp", [4, 32, N_META + SEQ])
        tabS = din("c_tabs", [4, 32, 64])
        tabM = din("c_tabm", [4, 32, 64])

        y_p = dout("y_p", [NSEQ, SEQ, D])
        y_s = dout("y_s", [64, D])
        ckv_p = dout("ckv_p", [NSEQ, N_META + SEQ, 128])
        kr_p = dout("kr_p", [NSEQ, N_META + SEQ, 32])
        wkv_p = dout("wkv_p", [NSEQ, 8, 64, 64])
        sh_p = dout("sh_p", [NSEQ, 1792])
        ckv_s = dout("ckv_s", [64, 128])
        kr_s = dout("kr_s", [64, 32])
        wkv_s = dout("wkv_s", [8, 64, 64])
        sh_s = dout("sh_s", [1792])

        scr_f = [nc.dram_tensor("scr_f%d" % i, [NFC, 128, 3072], BF16, kind="Internal").ap() for i in range(2)]
        scr_in = nc.dram_tensor("scr_in", [18, 128, 1024], BF16, kind="Internal").ap()
        scr_out = nc.dram_tensor("scr_out", [16, 64, 1024], BF16, kind="Internal").ap()

        ident = sb("ident", [128, 128])
        ident_bf = sb("ident_bf", [128, 128], BF16)
        ones_bf = sb("ones_bf", [128, 128], BF16)
        masks = sb("masks", [64, 3, 64], BF16)
        tri = sb("tri", [64, 2, 64])
        bc = sb("bc", [64, 3, 512])
        gains = sb("gains", [128, 4, 8])
        qn_g = sb("qn_g", [128, 2])
        kvn_g = sb("kvn_g", [128, 1])
        mu_t = sb("mu_t", [128, 26])
        hv = sb("hv", [64, 5, 8])
        rk_bf = sb("rk_bf", [64, 8], BF16)
        eps_c = sb("eps_c", [128, 1])
        w_w2_bf = sb("w_w2_bf", [64, 512], BF16)
        w_a2_bf = sb("w_a2_bf", [128, 512], BF16)
        w_g2_bf = sb("w_g2_bf", [128, 512], BF16)
        wq_bf = sb("wq_bf", [128, 2, 768], BF16)
        wqrot_bf = sb("wqrot_bf", [128, 2, 8, 32], BF16)
        wukT_bf = sb("wukT_bf", [64, 8, 128], BF16)
        wuv_bf = sb("wuv_bf", [128, 512], BF16)

        cT_c = sb("cT_c", [128, 4096], BF16)
        krT_c = sb("krT_c", [32, 4096], BF16)
        ctok_c = sb("ctok_c", [128, 32, 128], BF16)
        cT_m = sb("cT_m", [128, 16], BF16)
        krT_m = sb("krT_m", [32, 16], BF16)
        ctok_m = sb("ctok_m", [16, 128], BF16)
        ctok_d = sb("ctok_d", [64, 4, 128], BF16)

        xT = sb("xT", [128, 8, NT])
        hT = sb("hT", [128, 8, NT], BF16)
        xin = sb("xin", [128, 2, 1024])
        yout = sb("yout", [128, 1024])
        sq_p = [sb("sq%d" % i, [128, NT], BF16) for i in range(2)]
        rstd = sb("rstd", [128, NT])
        sa_p = [sb("sa%d" % i, [128, NT]) for i in range(2)]
        g_p = [sb("g%d" % i, [128, NT], BF16) for i in range(3)]
        w13_p = [sb("w13_%d" % i, [128, 2048], BF16) for i in range(3)]
        w2_p = [sb("w2_%d" % i, [128, 1024], BF16) for i in range(3)]
        wi_p = [sb("wi%d" % i, [128, 8, 128], BF16) for i in range(3)]
        wo_p = [sb("wo%d" % i, [64, 8, 128], BF16) for i in range(3)]
        pst_p = [sb("pst%d" % i, [128, 2, NT + 1]) for i in range(2)]
        dmix = sb("dmix", [128, 2, NT])
        ks_t = sb("ks_t", [64, 2, NT])
        carry = sb("carry", [128, 26])
        carry_m = sb("carry_m", [128, 26])
        tw = sb("tw", [64, NT], BF16)
        als = sb("als", [128, NT], BF16)
        sgl = sb("sgl", [128, NT], BF16)
        r_t = sb("r_t", [64, 8, NT], BF16)
        kp_t = sb("kp_t", [64, 8, NT], BF16)
        kk_t = sb("kk_t", [64, 8, NT], BF16)
        b_t = sb("b_t", [64, 8, NT], BF16)
        v_t = sb("v_t", [64, 8, NT], BF16)
        a_t = sb("a_t", [64, 8, NT], BF16)
        sg = sb("sg", [64, 512])
        gam = sb("gam", [64, 8, 64])
        ginv = sb("ginv", [64, 8, 64])
        gprev = sb("gprev", [64, 8, 64])
        gamC = sb("gamC", [64, 8])
        tmpA = gam[:].rearrange("p a b -> p (a b)")
        tmpB = ginv[:].rearrange("p a b -> p (a b)")
        tmpC = gprev[:].rearrange("p a b -> p (a b)")
        rT = sb("rT", [64, 8, 64], BF16)
        kT = sb("kT", [64, 8, 64], BF16)
        bT = sb("bT", [64, 8, 64], BF16)
        aT = sb("aT", [64, 8, 64], BF16)
        prodT = sb("prodT", [64, 8, 64], BF16)
        bt = sb("bt", [64, 512], BF16)
        kt = sb("kt", [64, 512], BF16)
        vt = sb("vt", [64, 512], BF16)
        AKT = sb("AKT", [64, 512], BF16)
        RBT = sb("RBT", [64, 512], BF16)
        RKT = sb("RKT", [64, 512], BF16)
        Pm = [sb("Pm%d" % i, [64, 512], BF16) for i in range(2)]
        Qm = [sb("Qm%d" % i, [64, 512], BF16) for i in range(2)]
        TT = [sb("TT%d" % i, [64, 512], BF16) for i in range(2)]
        Xb = sb("Xb", [64, 512], BF16)
        Ub = sb("Ub", [64, 512], BF16)
        S32 = sb("S32", [64, 8, 64])
        S32m = sb("S32m", [64, 8, 64])
        Sbf = sb("Sbf", [64, 8, 64], BF16)
        ytm = sb("ytm", [64, 8, 64])
        ysq = sb("ysq", [64, 8, 64])
        st8 = sb("st8", [64, 6, 8])
        rwo = sb("rwo", [64, 512], BF16)
        rwoT = sb("rwoT", [64, 8, NT], BF16)
        mlaT = sb("mlaT", [64, 8, NT], BF16)
        cq32 = sb("cq32", [128, 2, NT])
        cqn = sb("cqn", [128, 2, NT], BF16)
        c32 = sb("c32", [128, NT])
        kr32 = sb("kr32", [32, NT])
        tabs = sb("tabs", [32, 4, NT])
        qn_p = [sb("qn%d" % i, [64, NT], BF16) for i in range(2)]
        qlatT = sb("qlatT", [128, 8, NT], BF16)
        qrT = sb("qrT", [32, 8, NT], BF16)
        qt1 = sb("qt1", [32, NT])
        qt2 = sb("qt2", [32, NT])
        cst = sb("cst", [128, 2, 128])
        krst = sb("krst", [128, 2, 32])
        PT_p = [sb("PT%d" % i, [128, 2, NT], BF16) for i in range(3)]
        latn_p = [sb("latn%d" % i, [128, NT], BF16) for i in range(2)]

        psL = [es.enter_context(nc.psum_tensor("psL%d" % i, [128, 512], F32)) for i in range(4)]
        psR = [es.enter_context(nc.psum_tensor("psR%d" % i, [128, 512], F32)) for i in range(4)]
        Rn = RR([(psR[i], "psR%d" % i) for i in range(3)])
        Ln2 = RR([(psL[2], "psL2"), (psL[3], "psL3"), (psR[3], "psR3")])
        Rn4 = RR([(psR[i], "psR%d" % i) for i in range(4)])

        def PE(fn, r, w, inc=True):
            return S.op('pe', fn, r, w, inc)

        def ACT(fn, r, w):
            return S.op('act', fn, r, w)

        def DVE(fn, r, w):
            return S.op('dve', fn, r, w)

        def POOL(fn, r, w):
            return S.op('pool', fn, r, w)

        ew_rr = RR(['act', 'dve'])

        def EV(out, in_, r, w):
            if ew_rr() == 'act':
                return ACT(lambda h: h.activation(out=out, in_=in_, func=AF.Copy), r, w)
            return DVE(lambda h: h.tensor_copy(out=out, in_=in_), r, w)

        def ckpt(n):
            if stop <= n:
                raise StopBuild()

        try:
            S.dma('sp', 'ld_c', [(ident[:], ident_d)], writes=['ident'])
            ACT(lambda h: h.activation(out=ident_bf[:], in_=ident[:], func=AF.Copy), ['ident'], ['ident_bf'])
            POOL(lambda h: h.memset(ones_bf[:], 1.0), [], ['ones_bf'])
            POOL(lambda h: h.memset(eps_c[:], NORM_EPS), [], ['eps_c'])
            S.dma('sp', 'ld_c', [(masks[:, m, :], masks_d[m]) for m in range(3)], writes=['masks'])
            S.dma('sp', 'ld_c', [(tri[:, m, :], tri_d[m]) for m in range(2)], writes=['tri'])
            for i, nme in enumerate(["w0", "ln_x_w", "ln_x_b"]):
                S.dma('sp', 'ld_c', [(bc[:, i, :], W[nme].partition_broadcast(64))], writes=['bc'])
            for i, nme in enumerate(["norm_ffn1", "norm_mix", "norm_ffn2", "final_norm"]):
                S.dma('sp', 'ld_c', [(gains[:, i, :], W[nme].rearrange("(c p) -> p c", p=128))], writes=['gains'],
                      allow_slow_non_contiguous=True)
            S.dma('sp', 'ld_c', [(qn_g[:], W["q_norm"].rearrange("(c p) -> p c", p=128)),
                                 (kvn_g[:], W["kv_norm"].rearrange("(c p) -> p c", p=128))], writes=['qn_g', 'kvn_g'],
                  allow_slow_non_contiguous=True)
            S.dma('sp', 'ld_c', [(mu_t[0:64, 0:24], W["mu_shift"][0:1536].rearrange("(c p) -> p c", p=64)),
                                 (mu_t[:, 24:26], W["mu_shift"][1536:1792].rearrange("(c p) -> p c", p=128))],
                  writes=['mu_t'], allow_slow_non_contiguous=True)
            for i, nme in enumerate(["a0", "k_k", "k_a", "r_k"]):
                S.dma('sp', 'ld_c', [(hv[:, (i if i < 3 else 4), :], W[nme].rearrange("(c p) -> p c", p=64))], writes=['hv'],
                      allow_slow_non_contiguous=True)
            DVE(lambda h: h.tensor_scalar(out=hv[:, 3, :], in0=hv[:, 2, :], scalar1=-1.0, scalar2=1.0, op0=ALU.mult, op1=ALU.add),
                ['hv'], ['hv'])
            DVE(lambda h: h.tensor_copy(out=rk_bf[:], in_=hv[:, 4, :]), ['hv'], ['rk_bf'])

            def small_w(dst_ap, src_ap, rows, cols, slot, r0=0):
                S.dma('sp', 'ld_c', [(xin[r0:r0 + rows, slot, 0:cols], src_ap)], writes=['xin%d' % slot])
                DVE(lambda h: h.tensor_copy(out=dst_ap, in_=xin[r0:r0 + rows, slot, 0:cols]), ['xin%d' % slot], ['smallw'])

            small_w(w_w2_bf[:], W["w_w2"], 64, 512, 0)
            small_w(w_a2_bf[64:128, :], W["w_a2"], 64, 512, 1, r0=64)
            small_w(w_g2_bf[:], W["w_g2"], 128, 512, 0)
            small_w(wuv_bf[:], W["w_uv"], 128, 512, 1)
            for j in range(2):
                small_w(wq_bf[:, j, :], W["w_q_up"][j * 128:(j + 1) * 128, :], 128, 768, j)
            for j in range(2):
                for hh in range(8):
                    c0 = hh * 96 + 64
                    DVE(lambda h, j=j, hh=hh, c0=c0: h.tensor_scalar(out=wqrot_bf[:, j, hh, 0:16], in0=wq_bf[:, j, c0 + 16:c0 + 32],
                                                                      scalar1=-1.0, scalar2=None, op0=ALU.mult),
                        ['smallw'], ['smallw2'])
                    DVE(lambda h, j=j, hh=hh, c0=c0: h.tensor_copy(out=wqrot_bf[:, j, hh, 16:32], in_=wq_bf[:, j, c0:c0 + 16]),
                        ['smallw'], ['smallw2'])
            S.dma('sp', 'ld_c', [(xin[:, 0, 0:512], W["w_uk"])], writes=['xin0'])
            for hh in range(8):
                ps, pk = Rn()
                PE(lambda h, hh=hh, ps=ps: h.transpose(ps[0:64, 0:128], xin[:, 0, hh * 64:(hh + 1) * 64], ident[:]),
                   ['xin0', 'ident'], [pk])
                ACT(lambda h, hh=hh, ps=ps: h.activation(out=wukT_bf[:, hh, :], in_=ps[0:64, 0:128], func=AF.Copy), [pk], ['smallw'])

            ckpt(0)
            cv_rr = RR(['act', 'dve', 'pool'])
            yo_bf = yout[:].bitcast(BF16)
            cvi = [0]

            def cast_unit(src3, dst_flat, npart=128, pieces=None):
                sl = cvi[0] % 2
                cvi[0] += 1
                xk = 'xin%d' % sl
                yk = 'yo%d' % sl
                a, b = src3.shape[1], src3.shape[2]
                S.dma('sp', 'cv_ld', [(xin[0:npart, sl, :].rearrange("p (a b) -> p a b", a=a), src3)], writes=[xk])
                eng = cv_rr()
                if pieces is None:
                    fnc = (lambda h: h.tensor_copy(out=yo_bf[0:npart, sl * 1024:(sl + 1) * 1024], in_=xin[0:npart, sl, :])) \
                        if eng != 'act' else \
                        (lambda h: h.activation(out=yo_bf[0:npart, sl * 1024:(sl + 1) * 1024], in_=xin[0:npart, sl, :], func=AF.Copy))
                    S.op(eng, fnc, [xk], [yk])
                else:
                    pieces(sl, xk, yk)
                S.dma('pool', 'cv_st', [(dst_flat, yo_bf[0:npart, sl * 1024:(sl + 1) * 1024])], reads=[yk], writes=['scr'])

            for fi, (n1, n3, n2) in enumerate([("ffn1_w1", "ffn1_w3", "ffn1_w2"), ("ffn2_w1", "ffn2_w3", "ffn2_w2")]):
                for fc in range(NFC):
                    for wi, nme in enumerate([n1, n3]):
                        cast_unit(W[nme][:, fc * 128:(fc + 1) * 128].rearrange("(k p) f -> p k f", p=128),
                                  scr_f[fi][fc, :, wi * 1024:(wi + 1) * 1024])
                    cast_unit(W[n2][fc * 128:(fc + 1) * 128, :].rearrange("p (a b) -> p a b", a=8),
                              scr_f[fi][fc, :, 2048:3072])
            for g in range(17):
                cast_unit(W["w_in"][:, g * 128:(g + 1) * 128].rearrange("(k p) f -> p k f", p=128), scr_in[g])

            def rope_pieces(sl, xk, yk):
                xv = xin[:, sl, :].rearrange("p (k f) -> p k f", k=8)
                yv = yo_bf[:, sl * 1024:(sl + 1) * 1024].rearrange("p (k f) -> p k f", k=8)
                POOL(lambda h: h.memset(yo_bf[:, sl * 1024:(sl + 1) * 1024], 0.0), [], [yk])
                DVE(lambda h: h.tensor_copy(out=yv[:, :, 0:32], in_=xv[:, :, 0:32]), [xk], [yk])
                DVE(lambda h: h.tensor_scalar(out=yv[:, :, 32:48], in0=xv[:, :, 16:32], scalar1=-1.0, scalar2=None, op0=ALU.mult), [xk], [yk])
                DVE(lambda h: h.tensor_copy(out=yv[:, :, 48:64], in_=xv[:, :, 0:16]), [xk], [yk])

            sl = cvi[0] % 2
            POOL(lambda h, sl=sl: h.memset(xin[:, sl, :], 0.0), [], ['xin%d' % sl])
            cvi[0] += 1
            xk = 'xin%d' % sl
            yk = 'yo%d' % sl
            S.dma('sp', 'cv_ld', [(xin[:, sl, :].rearrange("p (k f) -> p k f", k=8)[:, :, 0:32],
                                   W["w_in"][:, 2176:2208].rearrange("(k p) f -> p k f", p=128))], writes=[xk])
            rope_pieces(sl, xk, yk)
            S.dma('pool', 'cv_st', [(scr_in[17], yo_bf[:, sl * 1024:(sl + 1) * 1024])], reads=[yk], writes=['scr'])
            for dch in range(8):
                for half in range(2):
                    cast_unit(W["w_out"][half * 512:(half + 1) * 512, dch * 128:(dch + 1) * 128].rearrange("(g p) m -> p g m", p=64),
                              scr_out[dch * 2 + half], npart=64)

            ckpt(1)
            def rmsnorm_to_hT(gi, ntok):
                ps, pk = Rn4()
                for dch in range(8):
                    sq = sq_p[dch % 2]
                    sk = 'sq%d' % (dch % 2)
                    ACT(lambda h, dch=dch, sq=sq: h.activation(out=sq[:, :ntok], in_=xT[:, dch, :ntok], func=AF.Square), ['xT%d' % dch], [sk])
                    PE(lambda h, dch=dch, sq=sq, ps=ps: h.matmul(ps[:, :ntok], lhsT=ones_bf[:], rhs=sq[:, :ntok], start=(dch == 0), stop=(dch == 7)),
                       [sk, 'ones_bf'], [pk], inc=(dch == 7))
                rstd_from(ps, pk, ntok, 128, 1.0 / D)
                for dch in range(8):
                    if True:
                        DVE(lambda h, dch=dch: h.scalar_tensor_tensor(out=hT[:, dch, :ntok], in0=xT[:, dch, :ntok], scalar=gains[:, gi, dch:dch + 1],
                                                                      in1=rstd[:, :ntok], op0=ALU.mult, op1=ALU.mult), ['xT%d' % dch, 'rstd', 'gains'], ['hT%d' % dch])
                    else:
                        tb = sa_p[dch % 2]
                        tk = 'sa%d' % (dch % 2)
                        POOL(lambda h, dch=dch, tb=tb: h.tensor_tensor(out=tb[:, :ntok], in0=xT[:, dch, :ntok], in1=rstd[:, :ntok], op=ALU.mult),
                             ['xT%d' % dch, 'rstd'], [tk])
                        POOL(lambda h, dch=dch, tb=tb: h.tensor_scalar(out=hT[:, dch, :ntok], in0=tb[:, :ntok], scalar1=gains[:, gi, dch:dch + 1], scalar2=None,
                                                                        op0=ALU.mult), [tk, 'gains'], ['hT%d' % dch])

            def rstd_from(ps, pk, ntok, npart, inv_n):
                ACT(lambda h: h.activation(out=rstd[0:npart, :ntok], in_=ps[0:npart, :ntok], func=AF.Sqrt, bias=eps_c[0:npart, 0:1], scale=inv_n),
                    [pk, 'eps_c'], ['rstd'])
                DVE(lambda h: h.reciprocal(out=rstd[0:npart, :ntok], in_=rstd[0:npart, :ntok]), ['rstd'], ['rstd'])

            wf_i = [0]

            def ffn(fi, ntok):
                base = wf_i[0]
                wf_i[0] += NFC

                def slot(fc):
                    return (base + fc) % 3

                def load(fc):
                    sl = slot(fc)
                    S.dma('sp', 'w13', [(w13_p[sl][:], scr_f[fi][fc, :, 0:2048])], reads=['scr'], writes=['w13_%d' % sl])
                    S.dma('sp', 'w2', [(w2_p[sl][:], scr_f[fi][fc, :, 2048:3072])], reads=['scr'], writes=['w2_%d' % sl])

                def up(fc):
                    sl = slot(fc)
                    wf = w13_p[sl]
                    ps, pk = Rn4()
                    for part in range(2):
                        for k in range(8):
                            PE(lambda h, part=part, k=k, ps=ps, wf=wf: h.matmul(ps[:, part * 256:part * 256 + ntok],
                                                                              lhsT=wf[:, part * 1024 + k * 128: part * 1024 + (k + 1) * 128],
                                                                              rhs=hT[:, k, :ntok], start=(k == 0), stop=(k == 7)),
                               ['w13_%d' % sl, 'hT%d' % k], [pk])
                    sa = sa_p[fc % 2]
                    g = g_p[fc % 3]
                    ACT(lambda h, ps=ps, sa=sa: h.activation(out=sa[:, :ntok], in_=ps[:, 0:ntok], func=AF.Silu), [pk], ['sa%d' % (fc % 2)])
                    DVE(lambda h, ps=ps, sa=sa, g=g: h.tensor_tensor(out=g[:, :ntok], in0=ps[:, 256:256 + ntok], in1=sa[:, :ntok], op=ALU.mult),
                        [pk, 'sa%d' % (fc % 2)], ['g%d' % (fc % 3)])

                def down(fc):
                    sl = slot(fc)
                    wf = w2_p[sl]
                    g = g_p[fc % 3]
                    for dch in range(8):
                        acc = psL[dch // 2]
                        PE(lambda h, dch=dch, acc=acc, wf=wf, g=g: h.matmul(acc[:, (dch % 2) * 256:(dch % 2) * 256 + ntok],
                                                                          lhsT=wf[:, dch * 128:(dch + 1) * 128],
                                                                          rhs=g[:, :ntok], start=(fc == 0 and dch % 2 == 0), stop=(fc == NFC - 1),
                                                                          skip_group_check=True),
                           ['w2_%d' % sl, 'g%d' % (fc % 3)], ['psL%d' % (dch // 2)])

                for fc in range(3):
                    load(fc)
                up(0)
                for fc in range(NFC):
                    if fc + 1 < NFC:
                        up(fc + 1)
                    down(fc)
                    if fc + 3 < NFC:
                        load(fc + 3)
                for dch in range(8):
                    acc = psL[dch // 2]
                    DVE(lambda h, dch=dch, acc=acc: h.scalar_tensor_tensor(out=xT[:, dch, :ntok], in0=acc[:, (dch % 2) * 256:(dch % 2) * 256 + ntok],
                                                                         scalar=0.5, in1=xT[:, dch, :ntok], op0=ALU.mult, op1=ALU.add),
                        ['psL%d' % (dch // 2), 'xT%d' % dch], ['xT%d' % dch])

            wi_i = [0]

            wi_st = {'order': [], 'loaded': 0, 'used': 0, 'slots': []}

            def wi_begin(order):
                wi_st['order'] = list(order)
                wi_st['loaded'] = 0
                wi_st['used'] = 0
                wi_st['slots'] = []

            def load_wi(g):
                st = wi_st
                assert st['order'][st['used']] == g, (st['order'], st['used'], g)
                while st['loaded'] < min(len(st['order']), st['used'] + 3):
                    gg = st['order'][st['loaded']]
                    sl = wi_i[0] % 3
                    wi_i[0] += 1
                    S.dma('sp', 'wi%d' % sl, [(wi_p[sl][:].rearrange("p k f -> p (k f)"), scr_in[gg])], reads=['scr'], writes=['wi%d' % sl])
                    st['slots'].append(sl)
                    st['loaded'] += 1
                sl = st['slots'][st['used']]
                st['used'] += 1
                return sl

            def proj(sl, c0, m, ps, pk, col0, ntok, part0=0):
                wi = wi_p[sl]
                for k in range(8):
                    PE(lambda h, k=k: h.matmul(ps[part0:part0 + m, col0:col0 + ntok], lhsT=wi[:, k, c0:c0 + m], rhs=hT[:, k, :ntok],
                                               start=(k == 0), stop=(k == 7)), ['wi%d' % sl, 'hT%d' % k], [pk], inc=(k == 7))

            pst_i = [0]

            def mix(ps, pk, npart, nsub, cols, ntok, outs):
                i = pst_i[0] % 2
                pst_i[0] += 1
                pst = pst_p[i]
                pk2 = 'pst%d' % i
                psv = ps[0:npart, :].rearrange("p (j t) -> p j t", j=2)[:, 0:nsub, 0:ntok]
                ACT(lambda h: h.activation(out=pst[0:npart, 0:nsub, 1:ntok + 1], in_=psv, func=AF.Copy), [pk], [pk2])
                c0 = cols[0]
                POOL(lambda h: h.tensor_copy(out=pst[0:npart, 0:nsub, 0:1], in_=carry[0:npart, c0:c0 + nsub].unsqueeze(2)), ['carry'], [pk2])
                DVE(lambda h: h.tensor_tensor(out=dmix[0:npart, 0:nsub, 0:ntok], in0=pst[0:npart, 0:nsub, 0:ntok],
                                              in1=pst[0:npart, 0:nsub, 1:ntok + 1], op=ALU.subtract), [pk2], ['dmix'])
                for j in range(nsub):
                    o, ok = outs[j]
                    DVE(lambda h, j=j, o=o: h.scalar_tensor_tensor(out=o, in0=dmix[0:npart, j, 0:ntok], scalar=mu_t[0:npart, cols[j]:cols[j] + 1],
                                                                 in1=pst[0:npart, j, 1:ntok + 1], op0=ALU.mult, op1=ALU.add),
                        ['dmix', pk2, 'mu_t'], [ok])
                POOL(lambda h: h.tensor_copy(out=carry[0:npart, c0:c0 + nsub].unsqueeze(2), in_=pst[0:npart, 0:nsub, ntok:ntok + 1]), [pk2], ['carry'])

            def tile(ntok, x_src, tab_src, key_off, full_blocks, do_attn, y_dst, c_dst, kr_dst, is_meta=False, nxt=None):
                nch = ntok // 64
                blks = [(t0, min(128, ntok - t0)) for t0 in range(0, ntok, 128)]
                for bi, (t0, n) in enumerate(blks):
                    for q4 in range(2):
                        ps, pk = Rn4()
                        for j in range(4):
                            dch = q4 * 4 + j
                            PE(lambda h, bi=bi, n=n, j=j, dch=dch, ps=ps: h.transpose(ps[:, j * 128:j * 128 + n], xin[0:n, bi, dch * 128:(dch + 1) * 128],
                                                                                 ident[0:n, 0:n]), ['xin%d' % bi, 'ident'], [pk], inc=(j == 3))
                        EV(xT[:, q4 * 4:(q4 + 1) * 4, t0:t0 + n], ps[:, :].rearrange("p (j t) -> p j t", j=4)[:, :, 0:n], [pk], ['xT%d' % d_ for d_ in range(q4 * 4, q4 * 4 + 4)])
                if nxt is not None:
                    nxt()
                S.dma('sp', 'tab', [(tabs[:, i, :ntok], tab_src[i]) for i in range(4)], writes=['tabs'])
                if not is_meta:
                    ckpt(20)
                rmsnorm_to_hT(0, ntok)
                ffn(0, ntok)
                if not is_meta:
                    ckpt(21)
                rmsnorm_to_hT(1, ntok)
                wi_begin([12, 13, 4, 5, 0, 6, 1, 7, 2, 3, 8, 9, 10, 11] + ([14, 15] if do_attn else []) + [16, 17])
                sl = load_wi(12)
                ps, pk = Rn4()
                proj(sl, 0, 128, ps, pk, 0, ntok)
                mix(ps, pk, 128, 1, [24], ntok, [(tmpWA[:, :ntok], 'c32')])
                ACT(lambda h: h.activation(out=tw[:, :ntok], in_=tmpWA[0:64, :ntok], func=AF.Tanh), ['c32'], ['tw'])
                DVE(lambda h: h.tensor_copy(out=als[64:128, :ntok], in_=tmpWA[64:128, :ntok]), ['c32'], ['als'])
                sl2 = load_wi(13)
                ps, pk = Rn4()
                proj(sl2, 0, 128, ps, pk, 0, ntok)
                mix(ps, pk, 128, 1, [25], ntok, [(tmpWA[:, :ntok], 'c32')])
                ACT(lambda h: h.activation(out=sgl[:, :ntok], in_=tmpWA[:, :ntok], func=AF.Sigmoid), ['c32'], ['sgl'])
                for hp in range(4):
                    ps, pk = Rn4()
                    for j in range(2):
                        hh = hp * 2 + j
                        PE(lambda h, hh=hh, j=j, ps=ps: h.matmul(ps[0:64, j * 256:j * 256 + ntok], lhsT=w_a2_bf[64:128, hh * 64:(hh + 1) * 64],
                                                               rhs=als[64:128, :ntok], start=True, stop=True), ['als', 'smallw'], [pk], inc=(j == 1))
                    for j in range(2):
                        hh = hp * 2 + j
                        ACT(lambda h, hh=hh, j=j, ps=ps: h.activation(out=a_t[:, hh, :ntok], in_=ps[0:64, j * 256:j * 256 + ntok], func=AF.Sigmoid,
                                                                    bias=hv[:, 0, hh:hh + 1], scale=1.0), [pk, 'hv'], ['a_t'])
                def rv_group(g):
                    sl = load_wi(g)
                    ps, pk = Rn4()
                    dst = r_t if g < 4 else v_t
                    dk = 'r_t' if g < 4 else 'v_t'
                    for j in range(2):
                        proj(sl, j * 64, 64, ps, pk, j * 256, ntok)
                    hp = g % 4
                    mix(ps, pk, 64, 2, [2 * g, 2 * g + 1], ntok, [(dst[:, hp * 2 + j, :ntok], dk) for j in range(2)])

                tA3 = gam[:].rearrange("p a b -> p (a b)").rearrange("p (j t) -> p j t", j=2)
                tB3 = ginv[:].rearrange("p a b -> p (a b)").rearrange("p (j t) -> p j t", j=2)
                tC3 = gprev[:].rearrange("p a b -> p (a b)").rearrange("p (j t) -> p j t", j=2)
                sq3 = Xb[:].rearrange("p (j t) -> p j t", j=2)
                kbufs = [(ks_t, 'ks_t'), (ysq[:].rearrange("p a b -> p (a b)").rearrange("p (j t) -> p j t", j=2), 'ysq')]

                def bcs(col, hp):
                    return hv[:, col, 2 * hp:2 * hp + 2].unsqueeze(2).to_broadcast([64, 2, ntok])

                def k_s1(g):
                    kb, kk_ = kbufs[g % 2]
                    sl = load_wi(g)
                    ps, pk = Rn4()
                    for j in range(2):
                        proj(sl, j * 64, 64, ps, pk, j * 256, ntok)
                    mix(ps, pk, 64, 2, [2 * g, 2 * g + 1], ntok, [(kb[:, j, :ntok], kk_) for j in range(2)])

                def k_s2(g):
                    kb, kk_ = kbufs[g % 2]
                    hp = g % 4
                    DVE(lambda h: h.tensor_tensor(out=tA3[:, :, :ntok], in0=kb[:, :, :ntok], in1=bcs(1, hp), op=ALU.mult), [kk_, 'hv'], ['gam'])
                    ACT(lambda h: h.activation(out=sq3[:, :, :ntok], in_=tA3[:, :, :ntok], func=AF.Square), ['gam'], ['Xb'])

                def k_s3(g):
                    kb, kk_ = kbufs[g % 2]
                    hp = g % 4
                    ps2, pk2 = Rn4()
                    if ntok == 256:
                        PE(lambda h: h.matmul(ps2[0:64, :], lhsT=ones_bf[0:64, 0:64], rhs=Xb[:], start=True, stop=True), ['Xb', 'ones_bf'], [pk2])
                    else:
                        for j in range(2):
                            PE(lambda h, j=j: h.matmul(ps2[0:64, j * 256:j * 256 + ntok], lhsT=ones_bf[0:64, 0:64], rhs=sq3[:, j, :ntok], start=True, stop=True),
                               ['Xb', 'ones_bf'], [pk2])
                    p3 = ps2[0:64, :].rearrange("p (j t) -> p j t", j=2)[:, :, :ntok]
                    ACT(lambda h: h.activation(out=tB3[:, :, :ntok], in_=p3, func=AF.Sqrt), [pk2], ['ginv'])
                    DVE(lambda h: h.tensor_scalar(out=tB3[:, :, :ntok], in0=tB3[:, :, :ntok], scalar1=1e-12, scalar2=None, op0=ALU.max), ['ginv'], ['ginv'])
                    DVE(lambda h: h.reciprocal(out=tB3[:, :, :ntok], in_=tB3[:, :, :ntok]), ['ginv'], ['ginv'])
                    DVE(lambda h: h.tensor_tensor(out=kk_t[:, 2 * hp:2 * hp + 2, :ntok], in0=tA3[:, :, :ntok], in1=tB3[:, :, :ntok], op=ALU.mult),
                        ['gam', 'ginv'], ['kk_t'])
                    POOL(lambda h: h.tensor_tensor(out=b_t[:, 2 * hp:2 * hp + 2, :ntok], in0=kk_t[:, 2 * hp:2 * hp + 2, :ntok], in1=a_t[:, 2 * hp:2 * hp + 2, :ntok], op=ALU.mult),
                         ['kk_t', 'a_t'], ['b_t'])
                    DVE(lambda h: h.tensor_tensor(out=tC3[:, :, :ntok], in0=a_t[:, 2 * hp:2 * hp + 2, :ntok], in1=bcs(2, hp), op=ALU.mult), ['a_t', 'hv'], ['gprev'])
                    DVE(lambda h: h.tensor_tensor(out=tC3[:, :, :ntok], in0=tC3[:, :, :ntok], in1=bcs(3, hp), op=ALU.add), ['gprev', 'hv'], ['gprev'])
                    DVE(lambda h: h.tensor_tensor(out=kp_t[:, 2 * hp:2 * hp + 2, :ntok], in0=kb[:, :, :ntok], in1=tC3[:, :, :ntok], op=ALU.mult),
                        [kk_, 'gprev'], ['kp_t'])

                k_s1(4)
                k_s2(4)
                k_s1(5)
                rv_group(0)
                k_s3(4)
                k_s2(5)
                k_s1(6)
                rv_group(1)
                k_s3(5)
                k_s2(6)
                k_s1(7)
                rv_group(2)
                k_s3(6)
                k_s2(7)
                rv_group(3)
                k_s3(7)
                for g in range(8, 12):
                    rv_group(g)
                if not is_meta:
                    ckpt(22)
                if do_attn:
                    sl = load_wi(14)
                    ps, pk = Rn4()
                    proj(sl, 0, 128, ps, pk, 0, ntok)
                    sl2 = load_wi(15)
                    proj(sl2, 0, 128, ps, pk, 256, ntok)
                    ACT(lambda h, ps=ps: h.activation(out=cq32[:, :, :ntok], in_=ps[:, :].rearrange("p (j t) -> p j t", j=2)[:, :, 0:ntok], func=AF.Copy),
                        [pk], ['cq32'])
                    ps2, pk2 = Rn4()
                    for j in range(2):
                        ACT(lambda h, j=j: h.activation(out=sq_p[j][:, :ntok], in_=cq32[:, j, :ntok], func=AF.Square), ['cq32'], ['sq%d' % j])
                        PE(lambda h, j=j, ps2=ps2: h.matmul(ps2[:, :ntok], lhsT=ones_bf[:], rhs=sq_p[j][:, :ntok], start=(j == 0), stop=(j == 1)),
                           ['sq%d' % j, 'ones_bf'], [pk2], inc=(j == 1))
                    rstd_from(ps2, pk2, ntok, 128, 1.0 / 256)
                    for j in range(2):
                        DVE(lambda h, j=j: h.scalar_tensor_tensor(out=cqn[:, j, :ntok], in0=cq32[:, j, :ntok], scalar=qn_g[:, j:j + 1], in1=rstd[:, :ntok],
                                                                  op0=ALU.mult, op1=ALU.mult), ['cq32', 'rstd', 'qn_g'], ['cqn'])
                if not is_meta:
                    ckpt(22.1)
                sl = load_wi(16)
                ps, pk = Rn4()
                proj(sl, 0, 128, ps, pk, 0, ntok)
                sl2 = load_wi(17)
                ACT(lambda h, ps=ps: h.activation(out=c32[:, :ntok], in_=ps[:, 0:ntok], func=AF.Copy), [pk], ['c32'])
                ACT(lambda h: h.activation(out=sq_p[0][:, :ntok], in_=c32[:, :ntok], func=AF.Square), ['c32'], ['sq0'])
                ps2, pk2 = Rn4()
                PE(lambda h, ps2=ps2: h.matmul(ps2[:, :ntok], lhsT=ones_bf[:], rhs=sq_p[0][:, :ntok], start=True, stop=True), ['sq0', 'ones_bf'], [pk2])
                rstd_from(ps2, pk2, ntok, 128, 1.0 / 128)
                DVE(lambda h: h.scalar_tensor_tensor(out=c32[:, :ntok], in0=c32[:, :ntok], scalar=kvn_g[:, 0:1], in1=rstd[:, :ntok],
                                                     op0=ALU.mult, op1=ALU.mult), ['c32', 'rstd', 'kvn_g'], ['c32'])
                if is_meta:
                    ACT(lambda h: h.activation(out=cT_m[:], in_=c32[:, 48:64], func=AF.Copy), ['c32'], ['cT_m'])
                else:
                    ACT(lambda h: h.activation(out=cT_c[:, key_off:key_off + ntok], in_=c32[:, :ntok], func=AF.Copy), ['c32'], ['cT_c'])
                if not is_meta:
                    ckpt(22.2)
                ps, pk = Rn4()
                proj(sl2, 0, 32, ps, pk, 0, ntok)
                proj(sl2, 32, 32, ps, pk, 256, ntok)
                DVE(lambda h, ps=ps: h.tensor_tensor(out=qt1[:, :ntok], in0=ps[0:32, 0:ntok], in1=tabs[:, 0, :ntok], op=ALU.mult), [pk, 'tabs'], ['qt1'])
                DVE(lambda h, ps=ps: h.tensor_tensor(out=kr32[:, :ntok], in0=ps[0:32, 256:256 + ntok], in1=tabs[:, 1, :ntok], op=ALU.mult), [pk, 'tabs'], ['kr32'])
                DVE(lambda h: h.tensor_tensor(out=kr32[:, :ntok], in0=kr32[:, :ntok], in1=qt1[:, :ntok], op=ALU.add), ['kr32', 'qt1'], ['kr32'])
                if is_meta:
                    ACT(lambda h: h.activation(out=krT_m[:], in_=kr32[:, 48:64], func=AF.Copy), ['kr32'], ['krT_m'])
                else:
                    ACT(lambda h: h.activation(out=krT_c[:, key_off:key_off + ntok], in_=kr32[:, :ntok], func=AF.Copy), ['kr32'], ['krT_c'])
                if not is_meta:
                    ckpt(22.3)
                for bi, (t0, n) in enumerate(blks):
                    ps, pk = Rn4()
                    PE(lambda h, t0=t0, n=n, ps=ps: h.transpose(ps[0:n, 0:128], c32[:, t0:t0 + n], ident[:]), ['c32', 'ident'], [pk], inc=False)
                    PE(lambda h, t0=t0, n=n, ps=ps: h.transpose(ps[0:n, 128:160], kr32[:, t0:t0 + n], ident[0:32, 0:32]), ['kr32', 'ident'], [pk])
                    ACT(lambda h, bi=bi, n=n, ps=ps: h.activation(out=cst[0:n, bi, :], in_=ps[0:n, 0:128], func=AF.Copy), [pk], ['cst'])
                    DVE(lambda h, bi=bi, n=n, ps=ps: h.tensor_copy(out=krst[0:n, bi, :], in_=ps[0:n, 128:160]), [pk], ['krst'])
                    if (not is_meta) and n == 128:
                        DVE(lambda h, bi=bi, ps=ps: h.tensor_copy(out=ctok_c[:, (key_off + bi * 128) // 128, :], in_=ps[:, 0:128]), [pk], ['ctok_c'])
                if not is_meta:
                    ckpt(22.4)
                if is_meta:
                    ps, pk = Rn4()
                    PE(lambda h, ps=ps: h.matmul(ps[0:16, 0:128], lhsT=cT_m[:], rhs=ident_bf[:], start=True, stop=True), ['cT_m', 'ident_bf'], [pk])
                    ACT(lambda h, ps=ps: h.activation(out=ctok_m[:], in_=ps[0:16, 0:128], func=AF.Copy), [pk], ['ctok_m'])
                    for si in range(NSEQ):
                        S.dma('pool', 'st_c', [(ckv_p[si, 0:16, :], cst[48:64, 0, :]), (kr_p[si, 0:16, :], krst[48:64, 0, :])],
                              reads=['cst', 'krst'], writes=['o_ckv'])
                else:
                    for bi, (t0, n) in enumerate(blks):
                        S.dma('pool', 'st_c', [(c_dst[t0:t0 + n, :], cst[0:n, bi, :]), (kr_dst[t0:t0 + n, :], krst[0:n, bi, :])],
                              reads=['cst', 'krst'], writes=['o_ckv'])
                    ckpt(22.5)
                    for cch in range(nch):
                        ps, pk = Rn4()
                        PE(lambda h, cch=cch, ps=ps: h.matmul(ps[0:64, 0:128], lhsT=cT_c[:, key_off + cch * 64:key_off + (cch + 1) * 64],
                                                            rhs=ident_bf[:], start=True, stop=True), ['cT_c', 'ident_bf'], [pk])
                        ACT(lambda h, cch=cch, ps=ps: h.activation(out=ctok_d[:, cch, :], in_=ps[0:64, 0:128], func=AF.Copy), [pk], ['ctok_d'])

                if not is_meta:
                    ckpt(23)
                def rwkv_thread():
                    decay_front(0)
                    for cch in range(nch):
                        yield from rwkv_chunk(cch, ntok, need_y=do_attn, nxt=(cch + 1 if cch + 1 < nch else None))
                    if rw_tail[0] is not None:
                        rw_tail[0]()
                        rw_tail[0] = None

                if not do_attn:
                    for _ in rwkv_thread():
                        pass
                    return
                ckpt(40)
                for hh in range(8):
                    psn, pkn = Rn4()
                    for j in range(2):
                        PE(lambda h, j=j, hh=hh, psn=psn: h.matmul(psn[0:64, :ntok], lhsT=wq_bf[:, j, hh * 96:hh * 96 + 64], rhs=cqn[:, j, :ntok],
                                                                 start=(j == 0), stop=(j == 1)), ['cqn', 'smallw'], [pkn], inc=(j == 1))
                    psr, pkr = Rn4()
                    for j in range(2):
                        PE(lambda h, j=j, hh=hh, psr=psr: h.matmul(psr[0:32, 0:ntok], lhsT=wq_bf[:, j, hh * 96 + 64:hh * 96 + 96], rhs=cqn[:, j, :ntok],
                                                                 start=(j == 0), stop=(j == 1)), ['cqn', 'smallw'], [pkr], inc=False)
                    for j in range(2):
                        PE(lambda h, j=j, hh=hh, psr=psr: h.matmul(psr[0:32, 256:256 + ntok], lhsT=wqrot_bf[:, j, hh, :], rhs=cqn[:, j, :ntok],
                                                                 start=(j == 0), stop=(j == 1)), ['cqn', 'smallw2'], [pkr], inc=(j == 1))
                    qn = qn_p[hh % 2]
                    qk = 'qn%d' % (hh % 2)
                    ACT(lambda h, psn=psn, qn=qn: h.activation(out=qn[:, :ntok], in_=psn[0:64, :ntok], func=AF.Copy), [pkn], [qk])
                    psl, pkl = Rn4()
                    PE(lambda h, hh=hh, psl=psl, qn=qn: h.matmul(psl[:, :ntok], lhsT=wukT_bf[:, hh, :], rhs=qn[:, :ntok], start=True, stop=True),
                       [qk, 'smallw'], [pkl])
                    ACT(lambda h, hh=hh, psl=psl: h.activation(out=qlatT[:, hh, :ntok], in_=psl[:, :ntok], func=AF.Copy, scale=MLA_SCALE), [pkl], ['qlatT'])
                    DVE(lambda h, psr=psr: h.tensor_tensor(out=qt1[:, :ntok], in0=psr[0:32, 0:ntok], in1=tabs[:, 2, :ntok], op=ALU.mult), [pkr, 'tabs'], ['qt1'])
                    DVE(lambda h, psr=psr: h.tensor_tensor(out=qt2[:, :ntok], in0=psr[0:32, 256:256 + ntok], in1=tabs[:, 3, :ntok], op=ALU.mult), [pkr, 'tabs'], ['qt2'])
                    DVE(lambda h, hh=hh: h.tensor_tensor(out=qrT[:, hh, :ntok], in0=qt1[:, :ntok], in1=qt2[:, :ntok], op=ALU.add), ['qt1', 'qt2'], ['qrT'])
                ckpt(41)
                blocks = [(cT_m[:], krT_m[:], ctok_m[:], 16, 0, ['cT_m', 'krT_m', 'ctok_m'])]
                for (a1, a2, a3, nk) in full_blocks:
                    blocks.append((a1, a2, a3, nk, 0, ['cT_c', 'krT_c', 'ctok_c']))
                for cch in range(nch):
                    blocks.append((cT_c[:, key_off + cch * 64:key_off + (cch + 1) * 64], krT_c[:, key_off + cch * 64:key_off + (cch + 1) * 64],
                                   ctok_d[:, cch, :], 64, cch * 64, ['cT_c', 'krT_c', 'ctok_d']))
                pt_i = [0]
                v2 = lambda t, np_: t[0:np_, :].rearrange("p (j t) -> p j t", j=2)

                def score(hp, bi, pend):
                    a1, a2, a3, nk, q0, keys = blocks[bi]
                    ps, pk = Ln2()
                    o = v2(ps, nk)[:, :, q0:ntok]
                    if q0 == 0 and ntok == 256:
                        PE(lambda h: h.matmul(ps[0:nk, :], lhsT=a1, rhs=qlatT[:, 2 * hp:2 * hp + 2, :].rearrange("p a b -> p (a b)"), start=True, stop=False),
                           keys[0:1] + ['qlatT'], [pk])
                        PE(lambda h: h.matmul(ps[0:nk, :], lhsT=a2, rhs=qrT[:, 2 * hp:2 * hp + 2, :].rearrange("p a b -> p (a b)"), start=False, stop=True),
                           keys[1:2] + ['qrT'], [pk])
                    else:
                        for j in range(2):
                            oj = ps[0:nk, j * 256 + q0:j * 256 + ntok]
                            PE(lambda h, j=j, oj=oj: h.matmul(oj, lhsT=a1, rhs=qlatT[:, 2 * hp + j, q0:ntok], start=True, stop=False),
                               keys[0:1] + ['qlatT'], [pk])
                            PE(lambda h, j=j, oj=oj: h.matmul(oj, lhsT=a2, rhs=qrT[:, 2 * hp + j, q0:ntok], start=False, stop=True),
                               keys[1:2] + ['qrT'], [pk])
                    pi = pt_i[0] % 3
                    pt_i[0] += 1
                    PT = PT_p[pi]
                    ACT(lambda h: h.activation(out=PT[0:nk, :, q0:ntok], in_=o, func=AF.Exp), [pk], ['PT%d' % pi])
                    pend.append((bi, pi))

                def pv(hp, pend, nb):
                    bi, pi = pend.pop(0)
                    a1, a2, a3, nk, q0, keys = blocks[bi]
                    PT = PT_p[pi]
                    if q0 == 0 and ntok == 256:
                        PE(lambda h: h.matmul(psL[0][:, :], lhsT=a3, rhs=PT[0:nk, :, :].rearrange("p a b -> p (a b)"), start=(bi == 0), stop=(bi == nb - 1),
                                              skip_group_check=True), keys[2:3] + ['PT%d' % pi], ['psL0'])
                        PE(lambda h: h.matmul(psL[1][:, :], lhsT=ones_bf[0:nk, :], rhs=PT[0:nk, :, :].rearrange("p a b -> p (a b)"), start=(bi == 0), stop=(bi == nb - 1),
                                              skip_group_check=True), ['ones_bf', 'PT%d' % pi], ['psL1'])
                    else:
                        for j in range(2):
                            PE(lambda h, j=j: h.matmul(psL[0][:, j * 256 + q0:j * 256 + ntok], lhsT=a3, rhs=PT[0:nk, j, q0:ntok], start=(bi == 0 and j == 0), stop=(bi == nb - 1),
                                                       skip_group_check=True), keys[2:3] + ['PT%d' % pi], ['psL0'])
                            PE(lambda h, j=j: h.matmul(psL[1][:, j * 256 + q0:j * 256 + ntok], lhsT=ones_bf[0:nk, :], rhs=PT[0:nk, j, q0:ntok], start=(bi == 0 and j == 0), stop=(bi == nb - 1),
                                                       skip_group_check=True), ['ones_bf', 'PT%d' % pi], ['psL1'])

                def head_norm(hp):
                    for j in range(2):
                        latn = latn_p[j]
                        DVE(lambda h, j=j: h.reciprocal(out=rstd[:, :ntok], in_=psL[1][:, j * 256:j * 256 + ntok]), ['psL1'], ['rstd'])
                        DVE(lambda h, j=j, latn=latn: h.tensor_tensor(out=latn[:, :ntok], in0=psL[0][:, j * 256:j * 256 + ntok], in1=rstd[:, :ntok], op=ALU.mult),
                            ['psL0', 'rstd'], ['latn%d' % j])

                def head_tail(hp):
                    for j in range(2):
                        hh = 2 * hp + j
                        latn = latn_p[j]
                        ps, pk = Ln2()
                        PE(lambda h, hh=hh, latn=latn, ps=ps: h.matmul(ps[0:64, :ntok], lhsT=wuv_bf[:, hh * 64:(hh + 1) * 64], rhs=latn[:, :ntok], start=True, stop=True),
                           ['latn%d' % j, 'smallw'], [pk])
                        ACT(lambda h, hh=hh, ps=ps: h.activation(out=mlaT[:, hh, :ntok], in_=ps[0:64, :ntok], func=AF.Copy), [pk], ['mlaT'])

                def attn_thread():
                    nb = len(blocks)
                    ptail = None
                    for hp in range(4):
                        pend = []
                        score(hp, 0, pend)
                        score(hp, 1, pend)
                        for bi in range(nb):
                            if bi + 2 < nb:
                                score(hp, bi + 2, pend)
                            pv(hp, pend, nb)
                            if bi == 1 and ptail is not None:
                                head_tail(ptail)
                                ptail = None
                            yield
                        if ptail is not None:
                            head_tail(ptail)
                        head_norm(hp)
                        ptail = hp
                        yield
                    head_tail(ptail)
                    yield

                wo_slots = {}

                def wo_load(u):
                    sl = wo_ctr[0] % 3
                    wo_ctr[0] += 1
                    wo_slots[u] = sl
                    S.dma('sp', 'wo%d' % sl, [(wo_p[sl][:].rearrange("p g m -> p (g m)"), scr_out[u])], reads=['scr'], writes=['wo%d' % sl])

                for u in range(3):
                    wo_load(u)
                threads = [attn_thread(), rwkv_thread()]
                import os as _os
                if _os.environ.get("NOINT") == "1":
                    for t in threads:
                        for _ in t:
                            pass
                    threads = []
                while threads:
                    for t in list(threads):
                        try:
                            next(t)
                        except StopIteration:
                            threads.remove(t)
                ckpt(42)
                for dch in range(8):
                    ps, pk = Rn4()
                    for half in range(2):
                        u = dch * 2 + half
                        sl = wo_slots[u]
                        for g8 in range(8):
                            src = rwoT if half == 0 else mlaT
                            PE(lambda h, sl=sl, g8=g8, src=src, ps=ps, half=half: h.matmul(ps[:, :ntok], lhsT=wo_p[sl][:, g8, :], rhs=src[:, g8, :ntok],
                                                                                       start=(half == 0 and g8 == 0), stop=(half == 1 and g8 == 7)),
                               ['wo%d' % sl, 'rwoT' if half == 0 else 'mlaT'], [pk], inc=(half == 1 and g8 == 7))
                        if u + 3 < 16:
                            wo_load(u + 3)
                    DVE(lambda h, dch=dch, ps=ps: h.tensor_tensor(out=xT[:, dch, :ntok], in0=ps[:, :ntok], in1=xT[:, dch, :ntok], op=ALU.add), [pk, 'xT%d' % dch], ['xT%d' % dch])
                ckpt(43)
                rmsnorm_to_hT(2, ntok)
                ffn(1, ntok)
                ps, pk = Rn4()
                for dch in range(8):
                    sq = sq_p[dch % 2]
                    sk = 'sq%d' % (dch % 2)
                    ACT(lambda h, dch=dch, sq=sq: h.activation(out=sq[:, :ntok], in_=xT[:, dch, :ntok], func=AF.Square), ['xT%d' % dch], [sk])
                    PE(lambda h, dch=dch, sq=sq, ps=ps: h.matmul(ps[:, :ntok], lhsT=ones_bf[:], rhs=sq[:, :ntok], start=(dch == 0), stop=(dch == 7)),
                       [sk, 'ones_bf'], [pk], inc=(dch == 7))
                rstd_from(ps, pk, ntok, 128, 1.0 / D)
                for dch in range(8):
                    if True:
                        DVE(lambda h, dch=dch: h.scalar_tensor_tensor(out=xT[:, dch, :ntok], in0=xT[:, dch, :ntok], scalar=gains[:, 3, dch:dch + 1],
                                                                      in1=rstd[:, :ntok], op0=ALU.mult, op1=ALU.mult), ['xT%d' % dch, 'rstd', 'gains'], ['xT%d' % dch])
                    else:
                        POOL(lambda h, dch=dch: h.tensor_tensor(out=xT[:, dch, :ntok], in0=xT[:, dch, :ntok], in1=rstd[:, :ntok], op=ALU.mult),
                             ['xT%d' % dch, 'rstd'], ['xT%d' % dch])
                        POOL(lambda h, dch=dch: h.tensor_scalar(out=xT[:, dch, :ntok], in0=xT[:, dch, :ntok], scalar1=gains[:, 3, dch:dch + 1], scalar2=None,
                                                                op0=ALU.mult), ['xT%d' % dch, 'gains'], ['xT%d' % dch])
                for bi, (t0, n) in enumerate(blks):
                    for q4 in range(2):
                        ps, pk = Rn4()
                        for j in range(4):
                            dch = q4 * 4 + j
                            PE(lambda h, t0=t0, n=n, j=j, dch=dch, ps=ps: h.transpose(ps[0:n, j * 128:(j + 1) * 128], xT[:, dch, t0:t0 + n], ident[:]),
                               ['xT%d' % dch, 'ident'], [pk], inc=(j == 3))
                        EV(yout[0:n, q4 * 512:(q4 + 1) * 512], ps[0:n, :], [pk], ['yo%d' % q4])
                        S.dma('sp', 'st_y', [(y_dst[t0:t0 + n, q4 * 512:(q4 + 1) * 512], yout[0:n, q4 * 512:(q4 + 1) * 512])],
                              reads=['yo%d' % q4], writes=['o_y%d' % q4])

            rw_tail = [None]
            wo_ctr = [0]

            v3 = lambda t: t[0:64, :].rearrange("p (a b) -> p a b", a=8)

            def decay_front(cch):
                cs = slice(cch * 64, (cch + 1) * 64)
                ps, pk = Rn()
                PE(lambda h, ps=ps: h.matmul(ps[0:64, :], lhsT=tw[:, cs], rhs=w_w2_bf[:], start=True, stop=True), ['tw', 'smallw'], [pk])
                DVE(lambda h, ps=ps: h.tensor_tensor(out=sg[:], in0=ps[0:64, :], in1=bc[:, 0, :], op=ALU.add), [pk, 'bc'], ['sg'])
                ACT(lambda h: h.activation(out=sg[:], in_=sg[:], func=AF.Sigmoid), ['sg'], ['sg'])
                pcl, kcl = Rn()
                pce, kce = Rn()
                for hh in range(8):
                    PE(lambda h, hh=hh, pcl=pcl: h.matmul(pcl[0:64, hh * 64:(hh + 1) * 64], lhsT=sg[:, hh * 64:(hh + 1) * 64], rhs=tri[:, 0, :], start=True, stop=True),
                       ['sg', 'tri'], [kcl], inc=(hh == 7))
                for hh in range(8):
                    PE(lambda h, hh=hh, pce=pce: h.matmul(pce[0:64, hh * 64:(hh + 1) * 64], lhsT=sg[:, hh * 64:(hh + 1) * 64], rhs=tri[:, 1, :], start=True, stop=True),
                       ['sg', 'tri'], [kce], inc=(hh == 7))
                ACT(lambda h: h.activation(out=gam[:], in_=v3(pcl), func=AF.Exp, scale=-C0), [kcl], ['gam'])
                ACT(lambda h: h.activation(out=ginv[:], in_=v3(pcl), func=AF.Exp, scale=C0), [kcl], ['ginv'])
                ACT(lambda h: h.activation(out=gprev[:], in_=v3(pce), func=AF.Exp, scale=-C0), [kce], ['gprev'])

            def rwkv_chunk(cch, ntok, need_y, nxt=None):
                cs = slice(cch * 64, (cch + 1) * 64)
                if need_y:
                    ckpt(30)
                yield
                DVE(lambda h: h.tensor_tensor(out=rT[:], in0=r_t[:, :, cs], in1=gam[:], op=ALU.mult), ['r_t', 'gam'], ['rT'])
                POOL(lambda h: h.tensor_tensor(out=kT[:], in0=kp_t[:, :, cs], in1=ginv[:], op=ALU.mult), ['kp_t', 'ginv'], ['kT'])
                POOL(lambda h: h.tensor_tensor(out=bT[:], in0=b_t[:, :, cs], in1=ginv[:], op=ALU.mult), ['b_t', 'ginv'], ['bT'])
                DVE(lambda h: h.scalar_tensor_tensor(out=aT[:], in0=kk_t[:, :, cs], scalar=-1.0, in1=gprev[:], op0=ALU.mult, op1=ALU.mult),
                    ['kk_t', 'gprev'], ['aT'])
                if need_y:
                    POOL(lambda h: h.tensor_tensor(out=prodT[:], in0=r_t[:, :, cs], in1=kp_t[:, :, cs], op=ALU.mult), ['r_t', 'kp_t'], ['prodT'])
                DVE(lambda h: h.tensor_copy(out=gamC[:].unsqueeze(2), in_=gam[:, :, 63:64]), ['gam'], ['gamC'])
                if nxt is not None:
                    yield
                    decay_front(nxt)

                def tr8(src_fn, dst, dkey, rkeys):
                    ps, pk = Rn()
                    for hh in range(8):
                        PE(lambda h, hh=hh, ps=ps: h.matmul(ps[0:64, hh * 64:(hh + 1) * 64], lhsT=src_fn(hh), rhs=ident_bf[0:64, 0:64], start=True, stop=True),
                           rkeys + ['ident_bf'], [pk], inc=(hh == 7))
                    EV(dst[:], ps[0:64, :], [pk], [dkey])

                yield
                tr8(lambda hh: bT[:, hh, :], bt, 'bt', ['bT'])
                yield
                tr8(lambda hh: kT[:, hh, :], kt, 'kt', ['kT'])
                yield
                tr8(lambda hh: v_t[:, hh, cs], vt, 'vt', ['v_t'])
                yield

                def sc8(l_fn, r_fn, rkeys, mask_i, dst, dkey):
                    ps, pk = Rn()
                    for hh in range(8):
                        PE(lambda h, hh=hh, ps=ps: h.matmul(ps[0:64, hh * 64:(hh + 1) * 64], lhsT=l_fn(hh), rhs=r_fn(hh), start=True, stop=True),
                           rkeys, [pk], inc=(hh == 7))
                    DVE(lambda h, ps=ps: h.tensor_tensor(out=dst[:].rearrange("p (a b) -> p a b", a=8), in0=v3(ps),
                                                         in1=masks[:, mask_i, :].unsqueeze(1).to_broadcast([64, 8, 64]), op=ALU.mult),
                        [pk, 'masks'], [dkey])

                sc8(lambda hh: bT[:, hh, :], lambda hh: aT[:, hh, :], ['bT', 'aT'], 0, Qm[0], 'Qm0')
                yield
                sc8(lambda hh: aT[:, hh, :], lambda hh: bT[:, hh, :], ['bT', 'aT'], 1, Pm[0], 'Pm0')
                yield
                sc8(lambda hh: kT[:, hh, :], lambda hh: aT[:, hh, :], ['kT', 'aT'], 0, AKT, 'AKT')
                yield
                if need_y:
                    sc8(lambda hh: bT[:, hh, :], lambda hh: rT[:, hh, :], ['bT', 'rT'], 2, RBT, 'RBT')
                    yield
                    sc8(lambda hh: kT[:, hh, :], lambda hh: rT[:, hh, :], ['kT', 'rT'], 2, RKT, 'RKT')
                    yield
                if need_y:
                    ckpt(31)
                if rw_tail[0] is not None:
                    rw_tail[0]()
                    rw_tail[0] = None
                    yield
                POOL(lambda h: h.tensor_tensor(out=TT[0][:].rearrange("p (a b) -> p a b", a=8), in0=Qm[0][:].rearrange("p (a b) -> p a b", a=8),
                                               in1=ident_bf[0:64, 0:64].unsqueeze(1).to_broadcast([64, 8, 64]), op=ALU.add), ['Qm0', 'ident_bf'], ['TT0'])
                cur = 0
                tcur = 0

                def mm8(l, lk, r, rk, evac):
                    ps, pk = Rn()
                    for hh in range(8):
                        PE(lambda h, hh=hh, ps=ps: h.matmul(ps[0:64, hh * 64:(hh + 1) * 64], lhsT=l[:, hh * 64:(hh + 1) * 64], rhs=r[:, hh * 64:(hh + 1) * 64],
                                                            start=True, stop=True), [lk, rk], [pk], inc=(hh == 7))
                    evac(ps, pk)

                for j in range(5):
                    yield
                    nx = 1 - cur
                    mm8(Qm[cur], 'Qm%d' % cur, Pm[cur], 'Pm%d' % cur,
                        lambda ps, pk, nx=nx: ACT(lambda h: h.activation(out=Pm[nx][:], in_=ps[0:64, :], func=AF.Copy), [pk], ['Pm%d' % nx]))
                    if j < 4:
                        mm8(Pm[cur], 'Pm%d' % cur, Qm[cur], 'Qm%d' % cur,
                            lambda ps, pk, nx=nx: ACT(lambda h: h.activation(out=Qm[nx][:], in_=ps[0:64, :], func=AF.Copy), [pk], ['Qm%d' % nx]))
                    if j >= 1:
                        tn = 1 - tcur
                        mm8(Pm[cur], 'Pm%d' % cur, TT[tcur], 'TT%d' % tcur,
                            lambda ps, pk, tn=tn, tc=tcur: DVE(lambda h: h.tensor_tensor(out=TT[tn][:], in0=ps[0:64, :], in1=TT[tc][:], op=ALU.add),
                                                               [pk, 'TT%d' % tc], ['TT%d' % tn]))
                        tcur = tn
                    cur = nx
                yield
                tn = 1 - tcur
                mm8(Pm[cur], 'Pm%d' % cur, TT[tcur], 'TT%d' % tcur,
                    lambda ps, pk, tn=tn, tc=tcur: DVE(lambda h: h.tensor_tensor(out=TT[tn][:], in0=ps[0:64, :], in1=TT[tc][:], op=ALU.add),
                                                       [pk, 'TT%d' % tc], ['TT%d' % tn]))
                tcur = tn
                Tf = TT[tcur]
                Tk = 'TT%d' % tcur
                if need_y:
                    ckpt(32)
                yield
                ps, pk = Rn()
                for hh in range(8):
                    hs = slice(hh * 64, (hh + 1) * 64)
                    PE(lambda h, hh=hh, hs=hs, ps=ps: h.matmul(ps[0:64, hs], lhsT=aT[:, hh, :], rhs=Sbf[:, hh, :], start=True, stop=False), ['aT', 'Sbf'], [pk], inc=False)
                    PE(lambda h, hh=hh, hs=hs, ps=ps: h.matmul(ps[0:64, hs], lhsT=AKT[:, hs], rhs=vt[:, hs], start=False, stop=True), ['AKT', 'vt'], [pk], inc=(hh == 7))
                ACT(lambda h, ps=ps: h.activation(out=Xb[:], in_=ps[0:64, :], func=AF.Copy), [pk], ['Xb'])
                yield
                ps, pk = Rn()
                for hh in range(8):
                    hs = slice(hh * 64, (hh + 1) * 64)
                    PE(lambda h, hs=hs, ps=ps: h.matmul(ps[0:64, hs], lhsT=Tf[:, hs], rhs=Xb[:, hs], start=True, stop=True), [Tk, 'Xb'], [pk], inc=(hh == 7))
                DVE(lambda h, ps=ps: h.tensor_copy(out=Ub[:], in_=ps[0:64, :]), [pk], ['Ub'])
                yield
                if need_y:
                    psy, pky = Rn()
                    for hh in range(8):
                        hs = slice(hh * 64, (hh + 1) * 64)
                        PE(lambda h, hh=hh, hs=hs: h.matmul(psy[0:64, hs], lhsT=rT[:, hh, :], rhs=Sbf[:, hh, :], start=True, stop=False), ['rT', 'Sbf'], [pky], inc=False)
                        PE(lambda h, hs=hs: h.matmul(psy[0:64, hs], lhsT=RBT[:, hs], rhs=Ub[:, hs], start=False, stop=False), ['RBT', 'Ub'], [pky], inc=False)
                        PE(lambda h, hs=hs: h.matmul(psy[0:64, hs], lhsT=RKT[:, hs], rhs=vt[:, hs], start=False, stop=True), ['RKT', 'vt'], [pky], inc=(hh == 7))
                if need_y:
                    ckpt(34)
                yield
                pss, pks = Rn()
                for hh in range(8):
                    hs = slice(hh * 64, (hh + 1) * 64)
                    PE(lambda h, hs=hs: h.matmul(pss[0:64, hs], lhsT=bt[:, hs], rhs=Ub[:, hs], start=True, stop=False), ['bt', 'Ub'], [pks], inc=False)
                    PE(lambda h, hs=hs: h.matmul(pss[0:64, hs], lhsT=kt[:, hs], rhs=vt[:, hs], start=False, stop=True), ['kt', 'vt'], [pks], inc=(hh == 7))
                DVE(lambda h: h.tensor_tensor(out=S32[:], in0=v3(pss), in1=S32[:], op=ALU.add), [pks, 'S32'], ['S32'])
                DVE(lambda h: h.tensor_tensor(out=S32[:], in0=S32[:], in1=gamC[:].unsqueeze(2).to_broadcast([64, 8, 64]), op=ALU.mult), ['S32', 'gamC'], ['S32'])
                ACT(lambda h: h.activation(out=Sbf[:], in_=S32[:], func=AF.Copy), ['S32'], ['Sbf'])
                yield
                if not need_y:
                    return
                ckpt(35)
                ACT(lambda h: h.activation(out=ytm[:], in_=v3(psy), func=AF.Copy), [pky], ['ytm'])
                ACT(lambda h: h.activation(out=ysq[:], in_=v3(psy), func=AF.Square), [pky], ['ysq'])
                DVE(lambda h: h.tensor_reduce(out=st8[:, 0, :], in_=ytm[:], axis=AX.X, op=ALU.add), ['ytm'], ['st8a'])
                DVE(lambda h: h.tensor_reduce(out=st8[:, 1, :], in_=ysq[:], axis=AX.X, op=ALU.add), ['ysq'], ['st8b'])
                DVE(lambda h: h.tensor_scalar(out=st8[:, 2, :], in0=st8[:, 0, :], scalar1=1.0 / 64, scalar2=None, op0=ALU.mult), ['st8a'], ['st8c'])
                DVE(lambda h: h.tensor_tensor(out=st8[:, 5, :], in0=st8[:, 2, :], in1=st8[:, 2, :], op=ALU.mult), ['st8c'], ['st8f'])
                DVE(lambda h: h.scalar_tensor_tensor(out=st8[:, 3, :], in0=st8[:, 1, :], scalar=1.0 / 64, in1=st8[:, 5, :], op0=ALU.mult, op1=ALU.subtract),
                    ['st8b', 'st8f'], ['st8d'])
                DVE(lambda h: h.tensor_scalar(out=st8[:, 3, :], in0=st8[:, 3, :], scalar1=GN_EPS, scalar2=None, op0=ALU.add), ['st8d'], ['st8d'])
                ACT(lambda h: h.activation(out=st8[:, 3, :], in_=st8[:, 3, :], func=AF.Sqrt), ['st8d'], ['st8d'])
                DVE(lambda h: h.reciprocal(out=st8[:, 3, :], in_=st8[:, 3, :]), ['st8d'], ['st8d'])
                DVE(lambda h: h.tensor_tensor(out=ytm[:], in0=ytm[:], in1=st8[:, 2, :].unsqueeze(2).to_broadcast([64, 8, 64]), op=ALU.subtract), ['ytm', 'st8c'], ['ytm'])
                DVE(lambda h: h.tensor_tensor(out=ytm[:], in0=ytm[:], in1=st8[:, 3, :].unsqueeze(2).to_broadcast([64, 8, 64]), op=ALU.mult), ['ytm', 'st8d'], ['ytm'])
                yf = ytm[:].rearrange("p a b -> p (a b)")
                POOL(lambda h: h.tensor_tensor(out=yf, in0=yf, in1=bc[:, 1, :], op=ALU.mult), ['ytm', 'bc'], ['ytm'])
                POOL(lambda h: h.tensor_tensor(out=yf, in0=yf, in1=bc[:, 2, :], op=ALU.add), ['ytm', 'bc'], ['ytm'])
                yield
                ps, pk = Rn()
                for hh in range(8):
                    PE(lambda h, hh=hh, ps=ps: h.matmul(ps[0:64, hh:hh + 1], lhsT=prodT[:, hh, :], rhs=rk_bf[:, hh:hh + 1], start=True, stop=True),
                       ['prodT', 'rk_bf'], [pk], inc=(hh == 7))
                ACT(lambda h, ps=ps: h.activation(out=st8[:, 4, :], in_=ps[0:64, 0:8], func=AF.Copy), [pk], ['st8e'])
                DVE(lambda h: h.tensor_tensor(out=ysq[:], in0=vt[:].rearrange("p (a b) -> p a b", a=8), in1=st8[:, 4, :].unsqueeze(2).to_broadcast([64, 8, 64]), op=ALU.mult),
                    ['vt', 'st8e'], ['ysq'])
                POOL(lambda h: h.tensor_tensor(out=ytm[:], in0=ytm[:], in1=ysq[:], op=ALU.add), ['ytm', 'ysq'], ['ytm'])
                ckpt(36)
                yield
                ps, pk = Rn()
                PE(lambda h, ps=ps: h.matmul(ps[0:64, :], lhsT=sgl[:, cs], rhs=w_g2_bf[:], start=True, stop=True), ['sgl', 'smallw'], [pk])
                DVE(lambda h, ps=ps: h.tensor_tensor(out=rwo[:], in0=ps[0:64, :], in1=yf, op=ALU.mult), [pk, 'ytm'], ['rwo'])
                def tail():
                    ps, pk = Rn()
                    for hh in range(8):
                        hs = slice(hh * 64, (hh + 1) * 64)
                        PE(lambda h, hs=hs, ps=ps: h.matmul(ps[0:64, hs], lhsT=rwo[:, hs], rhs=ident_bf[0:64, 0:64], start=True, stop=True), ['rwo', 'ident_bf'], [pk], inc=(hh == 7))
                    ACT(lambda h, ps=ps: h.activation(out=rwoT[:, :, cs], in_=v3(ps), func=AF.Copy), [pk], ['rwoT'])

                rw_tail[0] = tail

            tmpWA = c32

            POOL(lambda h: h.memset(xin[0:64, 0, :], 0.0), ['xin0'], ['xin0'])
            S.dma('sp', 'ld_x', [(xin[48:64, 0, :], meta)], reads=['xin0'], writes=['xin0'])
            POOL(lambda h: h.memset(carry[:], 0.0), [], ['carry'])
            POOL(lambda h: h.memset(S32[:], 0.0), [], ['S32'])
            POOL(lambda h: h.memset(Sbf[:], 0.0), [], ['Sbf'])
            POOL(lambda h: h.memset(als[:], 0.0), [], ['als'])

            def load_x_prompt(si, ti):
                def f():
                    S.dma('sp', 'ld_x', [(xin[:, b, :], xp[si, ti * NT + b * 128: ti * NT + (b + 1) * 128, :]) for b in range(2)],
                          writes=['xin0', 'xin1'])
                return f

            def load_x_sample():
                S.dma('sp', 'ld_x', [(xin[0:64, 0, :], xs)], writes=['xin0'])

            ntile = SEQ // NT
            tile(64, None, [tabM[i] for i in range(4)], 0, [], False, None, None, None, is_meta=True,
                 nxt=(load_x_prompt(0, 0) if NSEQ > 0 else load_x_sample))
            ckpt(10)
            DVE(lambda h: h.tensor_copy(out=S32m[:], in_=S32[:]), ['S32'], ['S32m'])
            DVE(lambda h: h.tensor_copy(out=carry_m[:], in_=carry[:]), ['carry'], ['carry_m'])

            def store_state(wkv_dst, sh_dst):
                for half in range(2):
                    ps, pk = Rn()
                    for j in range(4):
                        hh = half * 4 + j
                        PE(lambda h, hh=hh, j=j, ps=ps: h.transpose(ps[0:64, j * 64:(j + 1) * 64], S32[:, hh, :], ident[0:64, 0:64]), ['S32', 'ident'], [pk], inc=(j == 3))
                    ACT(lambda h, half=half, ps=ps: h.activation(out=ytm[:, half * 4:(half + 1) * 4, :], in_=ps[0:64, 0:256].rearrange("p (a b) -> p a b", a=4), func=AF.Copy),
                        [pk], ['ytm'])
                S.dma('pool', 'st_s', [(wkv_dst.rearrange("h v k -> v h k"), ytm[:])], reads=['ytm'], writes=['o_s'])
                S.dma('pool', 'st_s', [(sh_dst[0:1536].rearrange("(c p) -> p c", p=64), carry[0:64, 0:24]),
                                       (sh_dst[1536:1792].rearrange("(c p) -> p c", p=128), carry[:, 24:26])],
                      reads=['carry'], writes=['o_s'], allow_slow_non_contiguous=True)

            for si in range(NSEQ):
                DVE(lambda h: h.tensor_copy(out=S32[:], in_=S32m[:]), ['S32m'], ['S32'])
                ACT(lambda h: h.activation(out=Sbf[:], in_=S32m[:], func=AF.Copy), ['S32m'], ['Sbf'])
                DVE(lambda h: h.tensor_copy(out=carry[:], in_=carry_m[:]), ['carry_m'], ['carry'])
                for ti in range(ntile):
                    if ti + 1 < ntile:
                        nxt = load_x_prompt(si, ti + 1)
                    elif si + 1 < NSEQ:
                        nxt = load_x_prompt(si + 1, 0)
                    else:
                        nxt = load_x_sample
                    p0 = N_META + ti * NT
                    fb = [(cT_c[:, b * 128:(b + 1) * 128], krT_c[:, b * 128:(b + 1) * 128], ctok_c[:, b, :], 128) for b in range(ti * NT // 128)]
                    tile(NT, None, [tabP[i, :, p0:p0 + NT] for i in range(4)], ti * NT, fb, True,
                         y_p[si, ti * NT:(ti + 1) * NT, :], ckv_p[si, p0:p0 + NT, :], kr_p[si, p0:p0 + NT, :], nxt=nxt)
                store_state(wkv_p[si], sh_p[si])

            ckpt(50)
            S.dma('pool', 'ld_s', [(dmix[:, :, :].rearrange("p a b -> p (a b)")[:, 0:512].rearrange("p (a b) -> p a b", a=16), ckr.rearrange("(b p) r -> p b r", p=128))],
                  writes=['dmix'])
            S.dma('pool', 'ld_s', [(ytm[:], swkv.rearrange("h v k -> v h k"))], writes=['ytm'])
            krv = dmix[:, :, :].rearrange("p a b -> p (a b)")[:, 0:512].rearrange("p (a b) -> p a b", a=16)
            ckvv = None

            def sample_cache_c():
                pass

            S.dma('sp', 'ld_x', [(xin[:, 1, :].rearrange("p (b c) -> p b c", b=8), cckv[0:1024, :].rearrange("(b p) c -> p b c", p=128))], writes=['xin1'])
            for half in range(2):
                if half == 1:
                    S.dma('sp', 'ld_x', [(xin[:, 1, :].rearrange("p (b c) -> p b c", b=8), cckv[1024:2048, :].rearrange("(b p) c -> p b c", p=128))],
                          reads=['xin1'], writes=['xin1'])
                xv = xin[:, 1, :].rearrange("p (b c) -> p b c", b=8)
                DVE(lambda h, half=half, xv=xv: h.tensor_copy(out=ctok_c[:, half * 8:(half + 1) * 8, :], in_=xv), ['xin1'], ['ctok_c'])
                for b in range(8):
                    ps, pk = Rn()
                    PE(lambda h, b=b, ps=ps, xv=xv: h.transpose(ps[:, 0:128], xv[:, b, :], ident[:]), ['xin1', 'ident'], [pk])
                    EV(cT_c[:, (half * 8 + b) * 128:(half * 8 + b + 1) * 128], ps[:, 0:128], [pk], ['cT_c'])
            for b in range(16):
                ps, pk = Rn()
                PE(lambda h, b=b, ps=ps: h.transpose(ps[0:32, 0:128], krv[:, b, :], ident[:]), ['dmix', 'ident'], [pk])
                EV(krT_c[:, b * 128:(b + 1) * 128], ps[0:32, 0:128], [pk], ['krT_c'])
            swv = ytm
            for half in range(2):
                ps, pk = Rn()
                for j in range(4):
                    hh = half * 4 + j
                    PE(lambda h, hh=hh, j=j, ps=ps: h.transpose(ps[0:64, j * 64:(j + 1) * 64], swv[:, hh, :], ident[0:64, 0:64]), ['ytm', 'ident'], [pk], inc=(j == 3))
                DVE(lambda h, half=half, ps=ps: h.tensor_copy(out=S32[:, half * 4:(half + 1) * 4, :], in_=ps[0:64, 0:256].rearrange("p (a b) -> p a b", a=4)), [pk], ['S32'])
            ACT(lambda h: h.activation(out=Sbf[:], in_=S32[:], func=AF.Copy), ['S32'], ['Sbf'])
            S.dma('sp', 'ld_x', [(carry[0:64, 0:24], ssh[0:1536].rearrange("(c p) -> p c", p=64)),
                                 (carry[:, 24:26], ssh[1536:1792].rearrange("(c p) -> p c", p=128))], reads=['carry'], writes=['carry'],
                  allow_slow_non_contiguous=True)
            if NSEQ == 0:
                pass
            fb = [(cT_c[:, b * 128:(b + 1) * 128], krT_c[:, b * 128:(b + 1) * 128], ctok_c[:, b, :], 128) for b in range(16)]
            tile(64, None, [tabS[i] for i in range(4)], PAST, fb, True, y_s, ckv_s, kr_s)
            store_state(wkv_s, sh_s)


        except StopBuild:
            pass
        S.finish('sp')
        S.emit()
        print("ops:", S.nops, "sems:", S.nsem, flush=True)
    return nc


def make_consts(SEQ):
    ident = np.eye(128, dtype=np.float32)
    i = np.arange(64)[:, None]
    j = np.arange(64)[None, :]
    import ml_dtypes
    masks = np.stack([(i < j), (i > j), (i <= j)]).astype(np.float32).astype(ml_dtypes.bfloat16)
    tri = np.stack([(i <= j), (i < j)]).astype(np.float32)
    half = 16
    inv = (10000.0 ** (-np.arange(half, dtype=np.float32) / half)).astype(np.float32)

    def tab(pos):
        ang = pos.astype(np.float32)[None, :] * inv[:, None]
        cos = np.cos(ang).astype(np.float32)
        sin = np.sin(ang).astype(np.float32)
        c2 = np.concatenate([cos, cos], 0)
        s2 = np.concatenate([sin, sin], 0)
        sc = np.float32(MLA_SCALE)
        return np.stack([c2, s2, c2 * sc, s2 * sc]).astype(np.float32)

    tabp = tab(np.arange(N_META + SEQ))
    tabs = tab(N_META + PAST + np.arange(64))
    tabm = tab(np.concatenate([np.zeros(48), np.arange(16)]))
    return dict(c_ident=ident, c_masks=masks, c_tri=tri, c_tabp=tabp, c_tabs=tabs, c_tabm=tabm)


_CACHE = {}


def run(inputs, SEQ=4096, NSEQ=2, ncores=8, stop=99):
    key = (SEQ, NSEQ, stop)
    if key not in _CACHE:
        _CACHE[key] = build(SEQ, NSEQ, stop)
    nc = _CACHE[key]
    consts = make_consts(SEQ)
    f32 = lambda a: np.ascontiguousarray(np.asarray(a, dtype=np.float32))
    wmap = {}
    for n in W_NAMES:
        a = f32(inputs[n])
        if n != "final_norm":
            a = a[0]
        wmap[n] = np.ascontiguousarray(a.reshape(W_SHAPES[n]))
    in_maps = []
    for c in range(ncores):
        m = dict(wmap)
        m.update(consts)
        m["xp"] = f32(inputs["x_prompt"][c * NSEQ:(c + 1) * NSEQ, :SEQ])
        m["xs"] = f32(inputs["x_sample"][c])
        m["cckv"] = f32(inputs["cache_ckv"][0, c])
        m["ckr"] = f32(inputs["cache_krope"][0, c])
        m["swkv"] = f32(inputs["state_wkv"][0, c])
        m["ssh"] = f32(inputs["state_shift"][0, c, 0])
        m["meta"] = f32(inputs["meta_tokens"])
        in_maps.append(m)
    res = run_bass_kernel_spmd(nc, in_maps, core_ids=list(range(ncores)))
    R = res.results
    cat = lambda k: np.concatenate([np.asarray(r[k]) for r in R], axis=0)
    stk = lambda k: np.stack([np.asarray(r[k]) for r in R], axis=0)
    y_p = cat("y_p")
    y_s = stk("y_s")
    outs = (y_p, y_s,
            cat("ckv_p")[None], cat("kr_p")[None], cat("wkv_p")[None], cat("sh_p")[None, :, None, :],
            stk("ckv_s")[None], stk("kr_s")[None], stk("wkv_s")[None], stk("sh_s")[None, :, None, :])
    return tuple(np.ascontiguousarray(o.astype(np.float32)) for o in outs)


def kernel(**inputs):
    return run(inputs, SEQ=4096, NSEQ=2, ncores=8)
```

```python
import math
from contextlib import ExitStack

import numpy as np
import concourse.bass as bass
import concourse.mybir as mybir
from concourse.bass_utils import run_bass_kernel_spmd

F32 = mybir.dt.float32
BF16 = mybir.dt.bfloat16
ALU = mybir.AluOpType
AF = mybir.ActivationFunctionType
AX = mybir.AxisListType

ENGS = ('pe', 'act', 'dve', 'pool', 'sp')
SEM_LIMIT = 30000

D = 1024
DFF = 2816
NFC = 22
N_META = 16
PAST = 2048
NT = 256
C0 = math.exp(-0.5)
MLA_SCALE = 96 ** -0.5
NORM_EPS = 1e-6
GN_EPS = 64e-5


class COp:
    __slots__ = ("eng", "idx", "fn", "need", "sig")

    def __init__(self, eng, idx, fn):
        self.eng = eng
        self.idx = idx
        self.fn = fn
        self.need = False
        self.sig = None


class Sched:
    def __init__(self, nc, es):
        self.nc = nc
        self.es = es
        self.prog = {e: [] for e in ENGS}
        self.seen = {e: {} for e in ENGS}
        self.lastw = {}
        self.readers = {}
        self.nsem = 0
        self.rings = {}
        self.nops = {e: 0 for e in ENGS}
        self.cnt = {e: 0 for e in ENGS}

    def newsem(self):
        s = self.es.enter_context(self.nc.semaphore("s%d" % self.nsem))
        self.nsem += 1
        return s

    def _deps(self, eng, reads, writes):
        toks = []
        skip_own = (eng == 'pe')

        def own(t):
            return t[0] == 'c' and t[1].eng == eng

        for k in reads:
            t = self.lastw.get(k)
            if t is not None and not (skip_own and own(t)):
                toks.append(t)
        for k in writes:
            t = self.lastw.get(k)
            if t is not None and not (skip_own and own(t)):
                toks.append(t)
            for t in self.readers.get(k, ()):
                if not (skip_own and own(t)):
                    toks.append(t)
        best = {}
        for t in toks:
            if t[0] == 'c':
                key = ('c', t[1].eng)
                val = t[1].idx
            else:
                key = ('d', id(t[1]))
                val = t[2]
            if key not in best or best[key][0] < val:
                best[key] = (val, t)
        seen = self.seen[eng]
        out = []
        for key, (val, t) in best.items():
            if seen.get(key, -1) < val:
                seen[key] = val
                out.append(t)
        return out

    def _wait(self, eng, t):
        if t[0] == 'c':
            t[1].need = True
        self.prog[eng].append(('w', t))

    def _record(self, tok, reads, writes):
        for k in writes:
            self.lastw[k] = tok
            self.readers[k] = []
        for k in reads:
            self.readers.setdefault(k, []).append(tok)

    def op(self, eng, fn, reads=(), writes=(), inc=True):
        ps_r = [k for k in reads if k.startswith('ps')]
        if ps_r:
            reads = [k for k in reads if not k.startswith('ps')]
            writes = list(writes) + ps_r
        for t in self._deps(eng, reads, writes):
            self._wait(eng, t)
        o = COp(eng, self.cnt[eng], fn)
        self.cnt[eng] += 1
        self.prog[eng].append(('c', o))
        self.nops[eng] += 1
        tok = ('c', o)
        self._record(tok, reads, writes)
        return tok

    def dma(self, eng, chan, pairs, reads=(), writes=(), **kw):
        ring = self.rings.setdefault(eng, {'sems': [], 'i': 0})
        nring = 24 if eng == 'sp' else 16
        if len(ring['sems']) < nring:
            ring['sems'].append([self.newsem(), 0])
        ent = ring['sems'][ring['i'] % nring]
        ring['i'] += 1
        if ent[1] + 16 * len(pairs) >= SEM_LIMIT:
            s0, v0 = ent[0], ent[1]
            self.prog[eng].append(('w', ('d', s0, v0)))
            ent[0], ent[1] = self.newsem(), 0
        sem, cnt = ent[0], ent[1]
        key = ('d', id(sem))
        if cnt > 0 and self.seen[eng].get(key, -1) < cnt:
            self.seen[eng][key] = cnt
            self.prog[eng].append(('w', ('d', sem, cnt)))
        for t in self._deps(eng, reads, writes):
            self._wait(eng, t)
        for (o, i) in pairs:
            self.prog[eng].append(('raw', lambda h, o=o, i=i, sem=sem: h.dma_start(out=o, in_=i, **kw).then_inc(sem, 16)))
            ent[1] += 16
            self.nops[eng] += 1
        tok = ('d', sem, ent[1])
        self._record(tok, reads, writes)
        return tok

    def finish(self, eng='sp'):
        toks = []
        for k, t in self.lastw.items():
            toks.append(t)
        best = {}
        for t in toks:
            if t[0] == 'c':
                key = ('c', t[1].eng)
                val = t[1].idx
            else:
                key = ('d', id(t[1]))
                val = t[2]
            if key not in best or best[key][0] < val:
                best[key] = (val, t)
        for key, (val, t) in best.items():
            if self.seen[eng].get(key, -1) < val:
                self.seen[eng][key] = val
                self._wait(eng, t)

    def emit(self):
        nc = self.nc
        prog = self.prog
        nincs = {}
        for e in ENGS:
            sem, c = None, 0
            n = 0
            for ent in prog[e]:
                if ent[0] == 'c' and ent[1].need:
                    if sem is None or c >= SEM_LIMIT:
                        sem, c = self.newsem(), 0
                    c += 1
                    n += 1
                    ent[1].sig = (sem, c)
            nincs[e] = n
        print("incs:", nincs, flush=True)

        def run(e, h):
            for ent in prog[e]:
                k = ent[0]
                if k == 'c':
                    o = ent[1]
                    ins = o.fn(h)
                    if o.need:
                        ins.then_inc(o.sig[0], 1)
                elif k == 'w':
                    t = ent[1]
                    if t[0] == 'c':
                        h.wait_ge(t[1].sig[0], t[1].sig[1])
                    else:
                        h.wait_ge(t[1], t[2])
                else:
                    ent[1](h)

        with nc.Block() as block:
            @block.tensor
            def _(e):
                run('pe', e)

            @block.scalar
            def _(e):
                run('act', e)

            @block.vector
            def _(e):
                run('dve', e)

            @block.gpsimd
            def _(e):
                run('pool', e)

            @block.sync
            def _(e):
                run('sp', e)


class StopBuild(Exception):
    pass


class RR:
    def __init__(self, items):
        self.items = items
        self.i = 0

    def __call__(self):
        r = self.items[self.i % len(self.items)]
        self.i += 1
        return r


W_NAMES = ["norm_ffn1", "ffn1_w1", "ffn1_w3", "ffn1_w2", "norm_mix", "w_in", "mu_shift", "w0", "w_w2", "a0",
           "w_a2", "w_g2", "k_k", "k_a", "r_k", "ln_x_w", "ln_x_b", "q_norm", "w_q_up", "kv_norm", "w_uk", "w_uv",
           "w_out", "norm_ffn2", "ffn2_w1", "ffn2_w3", "ffn2_w2", "final_norm"]
W_SHAPES = {
    "norm_ffn1": [D], "ffn1_w1": [D, DFF], "ffn1_w3": [D, DFF], "ffn1_w2": [DFF, D], "norm_mix": [D],
    "w_in": [D, 2208], "mu_shift": [1792], "w0": [512], "w_w2": [64, 512], "a0": [512], "w_a2": [64, 512],
    "w_g2": [128, 512], "k_k": [512], "k_a": [512], "r_k": [512], "ln_x_w": [512], "ln_x_b": [512],
    "q_norm": [256], "w_q_up": [256, 768], "kv_norm": [128], "w_uk": [128, 512], "w_uv": [128, 512],
    "w_out": [D, D], "norm_ffn2": [D], "ffn2_w1": [D, DFF], "ffn2_w3": [D, DFF], "ffn2_w2": [DFF, D],
    "final_norm": [D],
}


def build(SEQ=4096, NSEQ=2, stop=99):
    nc = bass.Bass("TRN2", target_bir_lowering=False)
    es = ExitStack()
    with es:
        S = Sched(nc, es)

        def din(name, shape, dt=F32):
            return nc.dram_tensor(name, shape, dt, kind="ExternalInput").ap()

        def dout(name, shape):
            return nc.dram_tensor(name, shape, F32, kind="ExternalOutput").ap()

        def sb(name, shape, dt=F32):
            return es.enter_context(nc.sbuf_tensor(name, shape, dt))

        xp = din("xp", [NSEQ, SEQ, D])
        xs = din("xs", [64, D])
        cckv = din("cckv", [PAST, 128])
        ckr = din("ckr", [PAST, 32])
        swkv = din("swkv", [8, 64, 64])
        ssh = din("ssh", [1792])
        meta = din("meta", [N_META, D])
        W = {n: din(n, W_SHAPES[n]) for n in W_NAMES}
        ident_d = din("c_ident", [128, 128])
        masks_d = din("c_masks", [3, 64, 64], BF16)
        tri_d = din("c_tri", [2, 64, 64])
        tabP = din("c_tabp", [4, 32, N_META + SEQ])
        tabS = din("c_tabs", [4, 32, 64])
        tabM = din("c_tabm", [4, 32, 64])

        y_p = dout("y_p", [NSEQ, SEQ, D])
        y_s = dout("y_s", [64, D])
        ckv_p = dout("ckv_p", [NSEQ, N_META + SEQ, 128])
        kr_p = dout("kr_p", [NSEQ, N_META + SEQ, 32])
        wkv_p = dout("wkv_p", [NSEQ, 8, 64, 64])
        sh_p = dout("sh_p", [NSEQ, 1792])
        ckv_s = dout("ckv_s", [64, 128])
        kr_s = dout("kr_s", [64, 32])
        wkv_s = dout("wkv_s", [8, 64, 64])
        sh_s = dout("sh_s", [1792])

        scr_f = [nc.dram_tensor("scr_f%d" % i, [NFC, 128, 3072], BF16, kind="Internal").ap() for i in range(2)]
        scr_in = nc.dram_tensor("scr_in", [18, 128, 1024], BF16, kind="Internal").ap()
        scr_out = nc.dram_tensor("scr_out", [16, 64, 1024], BF16, kind="Internal").ap()

        ident = sb("ident", [128, 128])
        ident_bf = sb("ident_bf", [128, 128], BF16)
        ones_bf = sb("ones_bf", [128, 128], BF16)
        masks = sb("masks", [64, 3, 64], BF16)
        tri = sb("tri", [64, 2, 64])
        bc = sb("bc", [64, 3, 512])
        gains = sb("gains", [128, 4, 8])
        qn_g = sb("qn_g", [128, 2])
        kvn_g = sb("kvn_g", [128, 1])
        mu_t = sb("mu_t", [128, 26])
        hv = sb("hv", [64, 5, 8])
        rk_bf = sb("rk_bf", [64, 8], BF16)
        eps_c = sb("eps_c", [128, 1])
        w_w2_bf = sb("w_w2_bf", [64, 512], BF16)
        w_a2_bf = sb("w_a2_bf", [128, 512], BF16)
        w_g2_bf = sb("w_g2_bf", [128, 512], BF16)
        wq_bf = sb("wq_bf", [128, 2, 768], BF16)
        wqrot_bf = sb("wqrot_bf", [128, 2, 8, 32], BF16)
        wukT_bf = sb("wukT_bf", [64, 8, 128], BF16)
        wuv_bf = sb("wuv_bf", [128, 512], BF16)

        cT_c = sb("cT_c", [128, 4096], BF16)
        krT_c = sb("krT_c", [32, 4096], BF16)
        ctok_c = sb("ctok_c", [128, 32, 128], BF16)
        cT_m = sb("cT_m", [128, 16], BF16)
        krT_m = sb("krT_m", [32, 16], BF16)
        ctok_m = sb("ctok_m", [16, 128], BF16)
        ctok_d = sb("ctok_d", [64, 4, 128], BF16)

        xT = sb("xT", [128, 8, NT])
        hT = sb("hT", [128, 8, NT], BF16)
        xin = sb("xin", [128, 2, 1024])
        yout = sb("yout", [128, 1024])
        sq_p = [sb("sq%d" % i, [128, NT], BF16) for i in range(2)]
        rstd = sb("rstd", [128, NT])
        sa_p = [sb("sa%d" % i, [128, NT]) for i in range(2)]
        g_p = [sb("g%d" % i, [128, NT], BF16) for i in range(3)]
        w13_p = [sb("w13_%d" % i, [128, 2048], BF16) for i in range(3)]
        w2_p = [sb("w2_%d" % i, [128, 1024], BF16) for i in range(3)]
        wi_p = [sb("wi%d" % i, [128, 8, 128], BF16) for i in range(3)]
        wo_p = [sb("wo%d" % i, [64, 8, 128], BF16) for i in range(3)]
        pst_p = [sb("pst%d" % i, [128, 2, NT + 1]) for i in range(2)]
        dmix = sb("dmix", [128, 2, NT])
        ks_t = sb("ks_t", [64, 2, NT])
        carry = sb("carry", [128, 26])
        carry_m = sb("carry_m", [128, 26])
        tw = sb("tw", [64, NT], BF16)
        als = sb("als", [128, NT], BF16)
        sgl = sb("sgl", [128, NT], BF16)
        r_t = sb("r_t", [64, 8, NT], BF16)
        kp_t = sb("kp_t", [64, 8, NT], BF16)
        kk_t = sb("kk_t", [64, 8, NT], BF16)
        b_t = sb("b_t", [64, 8, NT], BF16)
        v_t = sb("v_t", [64, 8, NT], BF16)
        a_t = sb("a_t", [64, 8, NT], BF16)
        sg = sb("sg", [64, 512])
        gam = sb("gam", [64, 8, 64])
        ginv = sb("ginv", [64, 8, 64])
        gprev = sb("gprev", [64, 8, 64])
        tmpA = gam[:].rearrange("p a b -> p (a b)")
        tmpB = ginv[:].rearrange("p a b -> p (a b)")
        tmpC = gprev[:].rearrange("p a b -> p (a b)")
        rT = sb("rT", [64, 8, 64], BF16)
        kT = sb("kT", [64, 8, 64], BF16)
        bT = sb("bT", [64, 8, 64], BF16)
        aT = sb("aT", [64, 8, 64], BF16)
        prodT = sb("prodT", [64, 8, 64], BF16)
        bt = sb("bt", [64, 512], BF16)
        kt = sb("kt", [64, 512], BF16)
        vt = sb("vt", [64, 512], BF16)
        AKT = sb("AKT", [64, 512], BF16)
        RBT = sb("RBT", [64, 512], BF16)
        RKT = sb("RKT", [64, 512], BF16)
        Pm = [sb("Pm%d" % i, [64, 512], BF16) for i in range(2)]
        Qm = [sb("Qm%d" % i, [64, 512], BF16) for i in range(2)]
        TT = [sb("TT%d" % i, [64, 512], BF16) for i in range(2)]
        Xb = sb("Xb", [64, 512], BF16)
        Ub = sb("Ub", [64, 512], BF16)
        S32 = sb("S32", [64, 8, 64])
        S32m = sb("S32m", [64, 8, 64])
        Sbf = sb("Sbf", [64, 8, 64], BF16)
        ytm = sb("ytm", [64, 8, 64])
        ysq = sb("ysq", [64, 8, 64])
        st8 = sb("st8", [64, 6, 8])
        rwo = sb("rwo", [64, 512], BF16)
        rwoT = sb("rwoT", [64, 8, NT], BF16)
        mlaT = sb("mlaT", [64, 8, NT], BF16)
        cq32 = sb("cq32", [128, 2, NT])
        cqn = sb("cqn", [128, 2, NT], BF16)
        c32 = sb("c32", [128, NT])
        kr32 = sb("kr32", [32, NT])
        tabs = sb("tabs", [32, 4, NT])
        qn_p = [sb("qn%d" % i, [64, NT], BF16) for i in range(2)]
        qlatT = sb("qlatT", [128, 8, NT], BF16)
        qrT = sb("qrT", [32, 8, NT], BF16)
        qt1 = sb("qt1", [32, NT])
        qt2 = sb("qt2", [32, NT])
        cst = sb("cst", [128, 2, 128])
        krst = sb("krst", [128, 2, 32])
        PT_p = [sb("PT%d" % i, [128, 2, NT], BF16) for i in range(3)]
        latn_p = [sb("latn%d" % i, [128, NT], BF16) for i in range(2)]

        psL = [es.enter_context(nc.psum_tensor("psL%d" % i, [128, 512], F32)) for i in range(4)]
        psR = [es.enter_context(nc.psum_tensor("psR%d" % i, [128, 512], F32)) for i in range(4)]
        Rn = RR([(psR[i], "psR%d" % i) for i in range(3)])
        Ln2 = RR([(psL[2], "psL2"), (psL[3], "psL3"), (psR[3], "psR3")])
        Rn4 = RR([(psR[i], "psR%d" % i) for i in range(4)])

        def PE(fn, r, w, inc=True):
            return S.op('pe', fn, r, w, inc)

        def ACT(fn, r, w):
            return S.op('act', fn, r, w)

        def DVE(fn, r, w):
            return S.op('dve', fn, r, w)

        def POOL(fn, r, w):
            return S.op('pool', fn, r, w)

        ew_rr = RR(['act', 'dve'])

        def EV(out, in_, r, w):
            if ew_rr() == 'act':
                return ACT(lambda h: h.activation(out=out, in_=in_, func=AF.Copy), r, w)
            return DVE(lambda h: h.tensor_copy(out=out, in_=in_), r, w)

        def ckpt(n):
            if stop <= n:
                raise StopBuild()

        try:
            S.dma('sp', 'ld_c', [(ident[:], ident_d)], writes=['ident'])
            ACT(lambda h: h.activation(out=ident_bf[:], in_=ident[:], func=AF.Copy), ['ident'], ['ident_bf'])
            POOL(lambda h: h.memset(ones_bf[:], 1.0), [], ['ones_bf'])
            POOL(lambda h: h.memset(eps_c[:], NORM_EPS), [], ['eps_c'])
            S.dma('sp', 'ld_c', [(masks[:, m, :], masks_d[m]) for m in range(3)], writes=['masks'])
            S.dma('sp', 'ld_c', [(tri[:, m, :], tri_d[m]) for m in range(2)], writes=['tri'])
            for i, nme in enumerate(["w0", "ln_x_w", "ln_x_b"]):
                S.dma('sp', 'ld_c', [(bc[:, i, :], W[nme].partition_broadcast(64))], writes=['bc'])
            for i, nme in enumerate(["norm_ffn1", "norm_mix", "norm_ffn2", "final_norm"]):
                S.dma('sp', 'ld_c', [(gains[:, i, :], W[nme].rearrange("(c p) -> p c", p=128))], writes=['gains'],
                      allow_slow_non_contiguous=True)
            S.dma('sp', 'ld_c', [(qn_g[:], W["q_norm"].rearrange("(c p) -> p c", p=128)),
                                 (kvn_g[:], W["kv_norm"].rearrange("(c p) -> p c", p=128))], writes=['qn_g', 'kvn_g'],
                  allow_slow_non_contiguous=True)
            S.dma('sp', 'ld_c', [(mu_t[0:64, 0:24], W["mu_shift"][0:1536].rearrange("(c p) -> p c", p=64)),
                                 (mu_t[:, 24:26], W["mu_shift"][1536:1792].rearrange("(c p) -> p c", p=128))],
                  writes=['mu_t'], allow_slow_non_contiguous=True)
            for i, nme in enumerate(["a0", "k_k", "k_a", "r_k"]):
                S.dma('sp', 'ld_c', [(hv[:, (i if i < 3 else 4), :], W[nme].rearrange("(c p) -> p c", p=64))], writes=['hv'],
                      allow_slow_non_contiguous=True)
            DVE(lambda h: h.tensor_scalar(out=hv[:, 3, :], in0=hv[:, 2, :], scalar1=-1.0, scalar2=1.0, op0=ALU.mult, op1=ALU.add),
                ['hv'], ['hv'])
            DVE(lambda h: h.tensor_copy(out=rk_bf[:], in_=hv[:, 4, :]), ['hv'], ['rk_bf'])

            def small_w(dst_ap, src_ap, rows, cols, slot, r0=0):
                S.dma('sp', 'ld_c', [(xin[r0:r0 + rows, slot, 0:cols], src_ap)], writes=['xin%d' % slot])
                DVE(lambda h: h.tensor_copy(out=dst_ap, in_=xin[r0:r0 + rows, slot, 0:cols]), ['xin%d' % slot], ['smallw'])

            small_w(w_w2_bf[:], W["w_w2"], 64, 512, 0)
            small_w(w_a2_bf[64:128, :], W["w_a2"], 64, 512, 1, r0=64)
            small_w(w_g2_bf[:], W["w_g2"], 128, 512, 0)
            small_w(wuv_bf[:], W["w_uv"], 128, 512, 1)
            for j in range(2):
                small_w(wq_bf[:, j, :], W["w_q_up"][j * 128:(j + 1) * 128, :], 128, 768, j)
            for j in range(2):
                for hh in range(8):
                    c0 = hh * 96 + 64
                    DVE(lambda h, j=j, hh=hh, c0=c0: h.tensor_scalar(out=wqrot_bf[:, j, hh, 0:16], in0=wq_bf[:, j, c0 + 16:c0 + 32],
                                                                      scalar1=-1.0, scalar2=None, op0=ALU.mult),
                        ['smallw'], ['smallw2'])
                    DVE(lambda h, j=j, hh=hh, c0=c0: h.tensor_copy(out=wqrot_bf[:, j, hh, 16:32], in_=wq_bf[:, j, c0:c0 + 16]),
                        ['smallw'], ['smallw2'])
            S.dma('sp', 'ld_c', [(xin[:, 0, 0:512], W["w_uk"])], writes=['xin0'])
            for hh in range(8):
                ps, pk = Rn()
                PE(lambda h, hh=hh, ps=ps: h.transpose(ps[0:64, 0:128], xin[:, 0, hh * 64:(hh + 1) * 64], ident[:]),
                   ['xin0', 'ident'], [pk])
                ACT(lambda h, hh=hh, ps=ps: h.activation(out=wukT_bf[:, hh, :], in_=ps[0:64, 0:128], func=AF.Copy), [pk], ['smallw'])

            ckpt(0)
            cv_rr = RR(['act', 'dve', 'pool'])
            yo_bf = yout[:].bitcast(BF16)
            cvi = [0]

            def cast_unit(src3, dst_flat, npart=128, pieces=None):
                sl = cvi[0] % 2
                cvi[0] += 1
                xk = 'xin%d' % sl
                yk = 'yo%d' % sl
                a, b = src3.shape[1], src3.shape[2]
                S.dma('sp', 'cv_ld', [(xin[0:npart, sl, :].rearrange("p (a b) -> p a b", a=a), src3)], writes=[xk])
                eng = cv_rr()
                if pieces is None:
                    fnc = (lambda h: h.tensor_copy(out=yo_bf[0:npart, sl * 1024:(sl + 1) * 1024], in_=xin[0:npart, sl, :])) \
                        if eng != 'act' else \
                        (lambda h: h.activation(out=yo_bf[0:npart, sl * 1024:(sl + 1) * 1024], in_=xin[0:npart, sl, :], func=AF.Copy))
                    S.op(eng, fnc, [xk], [yk])
                else:
                    pieces(sl, xk, yk)
                S.dma('pool', 'cv_st', [(dst_flat, yo_bf[0:npart, sl * 1024:(sl + 1) * 1024])], reads=[yk], writes=['scr'])

            for fi, (n1, n3, n2) in enumerate([("ffn1_w1", "ffn1_w3", "ffn1_w2"), ("ffn2_w1", "ffn2_w3", "ffn2_w2")]):
                for fc in range(NFC):
                    for wi, nme in enumerate([n1, n3]):
                        cast_unit(W[nme][:, fc * 128:(fc + 1) * 128].rearrange("(k p) f -> p k f", p=128),
                                  scr_f[fi][fc, :, wi * 1024:(wi + 1) * 1024])
                    cast_unit(W[n2][fc * 128:(fc + 1) * 128, :].rearrange("p (a b) -> p a b", a=8),
                              scr_f[fi][fc, :, 2048:3072])
            for g in range(17):
                cast_unit(W["w_in"][:, g * 128:(g + 1) * 128].rearrange("(k p) f -> p k f", p=128), scr_in[g])

            def rope_pieces(sl, xk, yk):
                xv = xin[:, sl, :].rearrange("p (k f) -> p k f", k=8)
                yv = yo_bf[:, sl * 1024:(sl + 1) * 1024].rearrange("p (k f) -> p k f", k=8)
                POOL(lambda h: h.memset(yo_bf[:, sl * 1024:(sl + 1) * 1024], 0.0), [], [yk])
                DVE(lambda h: h.tensor_copy(out=yv[:, :, 0:32], in_=xv[:, :, 0:32]), [xk], [yk])
                DVE(lambda h: h.tensor_scalar(out=yv[:, :, 32:48], in0=xv[:, :, 16:32], scalar1=-1.0, scalar2=None, op0=ALU.mult), [xk], [yk])
                DVE(lambda h: h.tensor_copy(out=yv[:, :, 48:64], in_=xv[:, :, 0:16]), [xk], [yk])

            sl = cvi[0] % 2
            POOL(lambda h, sl=sl: h.memset(xin[:, sl, :], 0.0), [], ['xin%d' % sl])
            cvi[0] += 1
            xk = 'xin%d' % sl
            yk = 'yo%d' % sl
            S.dma('sp', 'cv_ld', [(xin[:, sl, :].rearrange("p (k f) -> p k f", k=8)[:, :, 0:32],
                                   W["w_in"][:, 2176:2208].rearrange("(k p) f -> p k f", p=128))], writes=[xk])
            rope_pieces(sl, xk, yk)
            S.dma('pool', 'cv_st', [(scr_in[17], yo_bf[:, sl * 1024:(sl + 1) * 1024])], reads=[yk], writes=['scr'])
            for dch in range(8):
                for half in range(2):
                    cast_unit(W["w_out"][half * 512:(half + 1) * 512, dch * 128:(dch + 1) * 128].rearrange("(g p) m -> p g m", p=64),
                              scr_out[dch * 2 + half], npart=64)

            ckpt(1)
            def rmsnorm_to_hT(gi, ntok):
                ps, pk = Rn4()
                for dch in range(8):
                    sq = sq_p[dch % 2]
                    sk = 'sq%d' % (dch % 2)
                    ACT(lambda h, dch=dch, sq=sq: h.activation(out=sq[:, :ntok], in_=xT[:, dch, :ntok], func=AF.Square), ['xT%d' % dch], [sk])
                    PE(lambda h, dch=dch, sq=sq, ps=ps: h.matmul(ps[:, :ntok], lhsT=ones_bf[:], rhs=sq[:, :ntok], start=(dch == 0), stop=(dch == 7)),
                       [sk, 'ones_bf'], [pk], inc=(dch == 7))
                rstd_from(ps, pk, ntok, 128, 1.0 / D)
                for dch in range(8):
                    if True:
                        DVE(lambda h, dch=dch: h.scalar_tensor_tensor(out=hT[:, dch, :ntok], in0=xT[:, dch, :ntok], scalar=gains[:, gi, dch:dch + 1],
                                                                      in1=rstd[:, :ntok], op0=ALU.mult, op1=ALU.mult), ['xT%d' % dch, 'rstd', 'gains'], ['hT%d' % dch])
                    else:
                        tb = sa_p[dch % 2]
                        tk = 'sa%d' % (dch % 2)
                        POOL(lambda h, dch=dch, tb=tb: h.tensor_tensor(out=tb[:, :ntok], in0=xT[:, dch, :ntok], in1=rstd[:, :ntok], op=ALU.mult),
                             ['xT%d' % dch, 'rstd'], [tk])
                        POOL(lambda h, dch=dch, tb=tb: h.tensor_scalar(out=hT[:, dch, :ntok], in0=tb[:, :ntok], scalar1=gains[:, gi, dch:dch + 1], scalar2=None,
                                                                        op0=ALU.mult), [tk, 'gains'], ['hT%d' % dch])

            def rstd_from(ps, pk, ntok, npart, inv_n):
                ACT(lambda h: h.activation(out=rstd[0:npart, :ntok], in_=ps[0:npart, :ntok], func=AF.Sqrt, bias=eps_c[0:npart, 0:1], scale=inv_n),
                    [pk, 'eps_c'], ['rstd'])
                DVE(lambda h: h.reciprocal(out=rstd[0:npart, :ntok], in_=rstd[0:npart, :ntok]), ['rstd'], ['rstd'])

            wf_i = [0]

            def ffn(fi, ntok):
                base = wf_i[0]
                wf_i[0] += NFC

                def slot(fc):
                    return (base + fc) % 3

                def load(fc):
                    sl = slot(fc)
                    S.dma('sp', 'w13', [(w13_p[sl][:], scr_f[fi][fc, :, 0:2048])], reads=['scr'], writes=['w13_%d' % sl])
                    S.dma('sp', 'w2', [(w2_p[sl][:], scr_f[fi][fc, :, 2048:3072])], reads=['scr'], writes=['w2_%d' % sl])

                def up(fc):
                    sl = slot(fc)
                    wf = w13_p[sl]
                    ps, pk = Rn4()
                    for part in range(2):
                        for k in range(8):
                            PE(lambda h, part=part, k=k, ps=ps, wf=wf: h.matmul(ps[:, part * 256:part * 256 + ntok],
                                                                              lhsT=wf[:, part * 1024 + k * 128: part * 1024 + (k + 1) * 128],
                                                                              rhs=hT[:, k, :ntok], start=(k == 0), stop=(k == 7)),
                               ['w13_%d' % sl, 'hT%d' % k], [pk])
                    sa = sa_p[fc % 2]
                    g = g_p[fc % 3]
                    ACT(lambda h, ps=ps, sa=sa: h.activation(out=sa[:, :ntok], in_=ps[:, 0:ntok], func=AF.Silu), [pk], ['sa%d' % (fc % 2)])
                    DVE(lambda h, ps=ps, sa=sa, g=g: h.tensor_tensor(out=g[:, :ntok], in0=ps[:, 256:256 + ntok], in1=sa[:, :ntok], op=ALU.mult),
                        [pk, 'sa%d' % (fc % 2)], ['g%d' % (fc % 3)])

                def down(fc):
                    sl = slot(fc)
                    wf = w2_p[sl]
                    g = g_p[fc % 3]
                    for dch in range(8):
                        acc = psL[dch // 2]
                        PE(lambda h, dch=dch, acc=acc, wf=wf, g=g: h.matmul(acc[:, (dch % 2) * 256:(dch % 2) * 256 + ntok],
                                                                          lhsT=wf[:, dch * 128:(dch + 1) * 128],
                                                                          rhs=g[:, :ntok], start=(fc == 0 and dch % 2 == 0), stop=(fc == NFC - 1),
                                                                          skip_group_check=True),
                           ['w2_%d' % sl, 'g%d' % (fc % 3)], ['psL%d' % (dch // 2)])

                for fc in range(3):
                    load(fc)
                up(0)
                for fc in range(NFC):
                    if fc + 1 < NFC:
                        up(fc + 1)
                    down(fc)
                    if fc + 3 < NFC:
                        load(fc + 3)
                for dch in range(8):
                    acc = psL[dch // 2]
                    DVE(lambda h, dch=dch, acc=acc: h.scalar_tensor_tensor(out=xT[:, dch, :ntok], in0=acc[:, (dch % 2) * 256:(dch % 2) * 256 + ntok],
                                                                         scalar=0.5, in1=xT[:, dch, :ntok], op0=ALU.mult, op1=ALU.add),
                        ['psL%d' % (dch // 2), 'xT%d' % dch], ['xT%d' % dch])

            wi_i = [0]

            wi_st = {'order': [], 'loaded': 0, 'used': 0, 'slots': []}

            def wi_begin(order):
                wi_st['order'] = list(order)
                wi_st['loaded'] = 0
                wi_st['used'] = 0
                wi_st['slots'] = []

            def load_wi(g):
                st = wi_st
                assert st['order'][st['used']] == g, (st['order'], st['used'], g)
                while st['loaded'] < min(len(st['order']), st['used'] + 3):
                    gg = st['order'][st['loaded']]
                    sl = wi_i[0] % 3
                    wi_i[0] += 1
                    S.dma('sp', 'wi%d' % sl, [(wi_p[sl][:].rearrange("p k f -> p (k f)"), scr_in[gg])], reads=['scr'], writes=['wi%d' % sl])
                    st['slots'].append(sl)
                    st['loaded'] += 1
                sl = st['slots'][st['used']]
                st['used'] += 1
                return sl

            def proj(sl, c0, m, ps, pk, col0, ntok, part0=0):
                wi = wi_p[sl]
                for k in range(8):
                    PE(lambda h, k=k: h.matmul(ps[part0:part0 + m, col0:col0 + ntok], lhsT=wi[:, k, c0:c0 + m], rhs=hT[:, k, :ntok],
                                               start=(k == 0), stop=(k == 7)), ['wi%d' % sl, 'hT%d' % k], [pk], inc=(k == 7))

            pst_i = [0]

            def mix(ps, pk, npart, nsub, cols, ntok, outs):
                i = pst_i[0] % 2
                pst_i[0] += 1
                pst = pst_p[i]
                pk2 = 'pst%d' % i
                psv = ps[0:npart, :].rearrange("p (j t) -> p j t", j=2)[:, 0:nsub, 0:ntok]
                ACT(lambda h: h.activation(out=pst[0:npart, 0:nsub, 1:ntok + 1], in_=psv, func=AF.Copy), [pk], [pk2])
                c0 = cols[0]
                POOL(lambda h: h.tensor_copy(out=pst[0:npart, 0:nsub, 0:1], in_=carry[0:npart, c0:c0 + nsub].unsqueeze(2)), ['carry'], [pk2])
                DVE(lambda h: h.tensor_tensor(out=dmix[0:npart, 0:nsub, 0:ntok], in0=pst[0:npart, 0:nsub, 0:ntok],
                                              in1=pst[0:npart, 0:nsub, 1:ntok + 1], op=ALU.subtract), [pk2], ['dmix'])
                for j in range(nsub):
                    o, ok = outs[j]
                    DVE(lambda h, j=j, o=o: h.scalar_tensor_tensor(out=o, in0=dmix[0:npart, j, 0:ntok], scalar=mu_t[0:npart, cols[j]:cols[j] + 1],
                                                                 in1=pst[0:npart, j, 1:ntok + 1], op0=ALU.mult, op1=ALU.add),
                        ['dmix', pk2, 'mu_t'], [ok])
                POOL(lambda h: h.tensor_copy(out=carry[0:npart, c0:c0 + nsub].unsqueeze(2), in_=pst[0:npart, 0:nsub, ntok:ntok + 1]), [pk2], ['carry'])

            def tile(ntok, x_src, tab_src, key_off, full_blocks, do_attn, y_dst, c_dst, kr_dst, is_meta=False, nxt=None):
                nch = ntok // 64
                blks = [(t0, min(128, ntok - t0)) for t0 in range(0, ntok, 128)]
                for bi, (t0, n) in enumerate(blks):
                    for q4 in range(2):
                        ps, pk = Rn4()
                        for j in range(4):
                            dch = q4 * 4 + j
                            PE(lambda h, bi=bi, n=n, j=j, dch=dch, ps=ps: h.transpose(ps[:, j * 128:j * 128 + n], xin[0:n, bi, dch * 128:(dch + 1) * 128],
                                                                                 ident[0:n, 0:n]), ['xin%d' % bi, 'ident'], [pk], inc=(j == 3))
                        EV(xT[:, q4 * 4:(q4 + 1) * 4, t0:t0 + n], ps[:, :].rearrange("p (j t) -> p j t", j=4)[:, :, 0:n], [pk], ['xT%d' % d_ for d_ in range(q4 * 4, q4 * 4 + 4)])
                if nxt is not None:
                    nxt()
                S.dma('sp', 'tab', [(tabs[:, i, :ntok], tab_src[i]) for i in range(4)], writes=['tabs'])
                if not is_meta:
                    ckpt(20)
                rmsnorm_to_hT(0, ntok)
                ffn(0, ntok)
                if not is_meta:
                    ckpt(21)
                rmsnorm_to_hT(1, ntok)
                wi_begin([12, 13, 4, 5, 0, 6, 1, 7, 2, 3, 8, 9, 10, 11] + ([14, 15] if do_attn else []) + [16, 17])
                sl = load_wi(12)
                ps, pk = Rn4()
                proj(sl, 0, 128, ps, pk, 0, ntok)
                mix(ps, pk, 128, 1, [24], ntok, [(tmpWA[:, :ntok], 'c32')])
                ACT(lambda h: h.activation(out=tw[:, :ntok], in_=tmpWA[0:64, :ntok], func=AF.Tanh), ['c32'], ['tw'])
                DVE(lambda h: h.tensor_copy(out=als[64:128, :ntok], in_=tmpWA[64:128, :ntok]), ['c32'], ['als'])
                sl2 = load_wi(13)
                ps, pk = Rn4()
                proj(sl2, 0, 128, ps, pk, 0, ntok)
                mix(ps, pk, 128, 1, [25], ntok, [(tmpWA[:, :ntok], 'c32')])
                ACT(lambda h: h.activation(out=sgl[:, :ntok], in_=tmpWA[:, :ntok], func=AF.Sigmoid), ['c32'], ['sgl'])
                for hp in range(4):
                    ps, pk = Rn4()
                    for j in range(2):
                        hh = hp * 2 + j
                        PE(lambda h, hh=hh, j=j, ps=ps: h.matmul(ps[0:64, j * 256:j * 256 + ntok], lhsT=w_a2_bf[64:128, hh * 64:(hh + 1) * 64],
                                                               rhs=als[64:128, :ntok], start=True, stop=True), ['als', 'smallw'], [pk], inc=(j == 1))
                    for j in range(2):
                        hh = hp * 2 + j
                        ACT(lambda h, hh=hh, j=j, ps=ps: h.activation(out=a_t[:, hh, :ntok], in_=ps[0:64, j * 256:j * 256 + ntok], func=AF.Sigmoid,
                                                                    bias=hv[:, 0, hh:hh + 1], scale=1.0), [pk, 'hv'], ['a_t'])
                def rv_group(g):
                    sl = load_wi(g)
                    ps, pk = Rn4()
                    dst = r_t if g < 4 else v_t
                    dk = 'r_t' if g < 4 else 'v_t'
                    for j in range(2):
                        proj(sl, j * 64, 64, ps, pk, j * 256, ntok)
                    hp = g % 4
                    mix(ps, pk, 64, 2, [2 * g, 2 * g + 1], ntok, [(dst[:, hp * 2 + j, :ntok], dk) for j in range(2)])

                tA3 = gam[:].rearrange("p a b -> p (a b)").rearrange("p (j t) -> p j t", j=2)
                tB3 = ginv[:].rearrange("p a b -> p (a b)").rearrange("p (j t) -> p j t", j=2)
                tC3 = gprev[:].rearrange("p a b -> p (a b)").rearrange("p (j t) -> p j t", j=2)
                sq3 = Xb[:].rearrange("p (j t) -> p j t", j=2)
                kbufs = [(ks_t, 'ks_t'), (ysq[:].rearrange("p a b -> p (a b)").rearrange("p (j t) -> p j t", j=2), 'ysq')]

                def bcs(col, hp):
                    return hv[:, col, 2 * hp:2 * hp + 2].unsqueeze(2).to_broadcast([64, 2, ntok])

                def k_s1(g):
                    kb, kk_ = kbufs[g % 2]
                    sl = load_wi(g)
                    ps, pk = Rn4()
                    for j in range(2):
                        proj(sl, j * 64, 64, ps, pk, j * 256, ntok)
                    mix(ps, pk, 64, 2, [2 * g, 2 * g + 1], ntok, [(kb[:, j, :ntok], kk_) for j in range(2)])

                def k_s2(g):
                    kb, kk_ = kbufs[g % 2]
                    hp = g % 4
                    DVE(lambda h: h.tensor_tensor(out=tA3[:, :, :ntok], in0=kb[:, :, :ntok], in1=bcs(1, hp), op=ALU.mult), [kk_, 'hv'], ['gam'])
                    ACT(lambda h: h.activation(out=sq3[:, :, :ntok], in_=tA3[:, :, :ntok], func=AF.Square), ['gam'], ['Xb'])

                def k_s3(g):
                    kb, kk_ = kbufs[g % 2]
                    hp = g % 4
                    ps2, pk2 = Rn4()
                    if ntok == 256:
                        PE(lambda h: h.matmul(ps2[0:64, :], lhsT=ones_bf[0:64, 0:64], rhs=Xb[:], start=True, stop=True), ['Xb', 'ones_bf'], [pk2])
                    else:
                        for j in range(2):
                            PE(lambda h, j=j: h.matmul(ps2[0:64, j * 256:j * 256 + ntok], lhsT=ones_bf[0:64, 0:64], rhs=sq3[:, j, :ntok], start=True, stop=True),
                               ['Xb', 'ones_bf'], [pk2])
                    p3 = ps2[0:64, :].rearrange("p (j t) -> p j t", j=2)[:, :, :ntok]
                    ACT(lambda h: h.activation(out=tB3[:, :, :ntok], in_=p3, func=AF.Sqrt), [pk2], ['ginv'])
                    DVE(lambda h: h.tensor_scalar(out=tB3[:, :, :ntok], in0=tB3[:, :, :ntok], scalar1=1e-12, scalar2=None, op0=ALU.max), ['ginv'], ['ginv'])
                    DVE(lambda h: h.reciprocal(out=tB3[:, :, :ntok], in_=tB3[:, :, :ntok]), ['ginv'], ['ginv'])
                    DVE(lambda h: h.tensor_tensor(out=kk_t[:, 2 * hp:2 * hp + 2, :ntok], in0=tA3[:, :, :ntok], in1=tB3[:, :, :ntok], op=ALU.mult),
                        ['gam', 'ginv'], ['kk_t'])
                    POOL(lambda h: h.tensor_tensor(out=b_t[:, 2 * hp:2 * hp + 2, :ntok], in0=kk_t[:, 2 * hp:2 * hp + 2, :ntok], in1=a_t[:, 2 * hp:2 * hp + 2, :ntok], op=ALU.mult),
                         ['kk_t', 'a_t'], ['b_t'])
                    DVE(lambda h: h.tensor_tensor(out=tC3[:, :, :ntok], in0=a_t[:, 2 * hp:2 * hp + 2, :ntok], in1=bcs(2, hp), op=ALU.mult), ['a_t', 'hv'], ['gprev'])
                    DVE(lambda h: h.tensor_tensor(out=tC3[:, :, :ntok], in0=tC3[:, :, :ntok], in1=bcs(3, hp), op=ALU.add), ['gprev', 'hv'], ['gprev'])
                    DVE(lambda h: h.tensor_tensor(out=kp_t[:, 2 * hp:2 * hp + 2, :ntok], in0=kb[:, :, :ntok], in1=tC3[:, :, :ntok], op=ALU.mult),
                        [kk_, 'gprev'], ['kp_t'])

                k_s1(4)
                k_s2(4)
                k_s1(5)
                rv_group(0)
                k_s3(4)
                k_s2(5)
                k_s1(6)
                rv_group(1)
                k_s3(5)
                k_s2(6)
                k_s1(7)
                rv_group(2)
                k_s3(6)
                k_s2(7)
                rv_group(3)
                k_s3(7)
                for g in range(8, 12):
                    rv_group(g)
                if not is_meta:
                    ckpt(22)
                if do_attn:
                    sl = load_wi(14)
                    ps, pk = Rn4()
                    proj(sl, 0, 128, ps, pk, 0, ntok)
                    sl2 = load_wi(15)
                    proj(sl2, 0, 128, ps, pk, 256, ntok)
                    ACT(lambda h, ps=ps: h.activation(out=cq32[:, :, :ntok], in_=ps[:, :].rearrange("p (j t) -> p j t", j=2)[:, :, 0:ntok], func=AF.Copy),
                        [pk], ['cq32'])
                    ps2, pk2 = Rn4()
                    for j in range(2):
                        ACT(lambda h, j=j: h.activation(out=sq_p[j][:, :ntok], in_=cq32[:, j, :ntok], func=AF.Square), ['cq32'], ['sq%d' % j])
                        PE(lambda h, j=j, ps2=ps2: h.matmul(ps2[:, :ntok], lhsT=ones_bf[:], rhs=sq_p[j][:, :ntok], start=(j == 0), stop=(j == 1)),
                           ['sq%d' % j, 'ones_bf'], [pk2], inc=(j == 1))
                    rstd_from(ps2, pk2, ntok, 128, 1.0 / 256)
                    for j in range(2):
                        DVE(lambda h, j=j: h.scalar_tensor_tensor(out=cqn[:, j, :ntok], in0=cq32[:, j, :ntok], scalar=qn_g[:, j:j + 1], in1=rstd[:, :ntok],
                                                                  op0=ALU.mult, op1=ALU.mult), ['cq32', 'rstd', 'qn_g'], ['cqn'])
                if not is_meta:
                    ckpt(22.1)
                sl = load_wi(16)
                ps, pk = Rn4()
                proj(sl, 0, 128, ps, pk, 0, ntok)
                sl2 = load_wi(17)
                ACT(lambda h, ps=ps: h.activation(out=c32[:, :ntok], in_=ps[:, 0:ntok], func=AF.Copy), [pk], ['c32'])
                ACT(lambda h: h.activation(out=sq_p[0][:, :ntok], in_=c32[:, :ntok], func=AF.Square), ['c32'], ['sq0'])
                ps2, pk2 = Rn4()
                PE(lambda h, ps2=ps2: h.matmul(ps2[:, :ntok], lhsT=ones_bf[:], rhs=sq_p[0][:, :ntok], start=True, stop=True), ['sq0', 'ones_bf'], [pk2])
                rstd_from(ps2, pk2, ntok, 128, 1.0 / 128)
                DVE(lambda h: h.scalar_tensor_tensor(out=c32[:, :ntok], in0=c32[:, :ntok], scalar=kvn_g[:, 0:1], in1=rstd[:, :ntok],
                                                     op0=ALU.mult, op1=ALU.mult), ['c32', 'rstd', 'kvn_g'], ['c32'])
                if is_meta:
                    ACT(lambda h: h.activation(out=cT_m[:], in_=c32[:, 48:64], func=AF.Copy), ['c32'], ['cT_m'])
                else:
                    ACT(lambda h: h.activation(out=cT_c[:, key_off:key_off + ntok], in_=c32[:, :ntok], func=AF.Copy), ['c32'], ['cT_c'])
                if not is_meta:
                    ckpt(22.2)
                ps, pk = Rn4()
                proj(sl2, 0, 32, ps, pk, 0, ntok)
                proj(sl2, 32, 32, ps, pk, 256, ntok)
                DVE(lambda h, ps=ps: h.tensor_tensor(out=qt1[:, :ntok], in0=ps[0:32, 0:ntok], in1=tabs[:, 0, :ntok], op=ALU.mult), [pk, 'tabs'], ['qt1'])
                DVE(lambda h, ps=ps: h.tensor_tensor(out=kr32[:, :ntok], in0=ps[0:32, 256:256 + ntok], in1=tabs[:, 1, :ntok], op=ALU.mult), [pk, 'tabs'], ['kr32'])
                DVE(lambda h: h.tensor_tensor(out=kr32[:, :ntok], in0=kr32[:, :ntok], in1=qt1[:, :ntok], op=ALU.add), ['kr32', 'qt1'], ['kr32'])
                if is_meta:
                    ACT(lambda h: h.activation(out=krT_m[:], in_=kr32[:, 48:64], func=AF.Copy), ['kr32'], ['krT_m'])
                else:
                    ACT(lambda h: h.activation(out=krT_c[:, key_off:key_off + ntok], in_=kr32[:, :ntok], func=AF.Copy), ['kr32'], ['krT_c'])
                if not is_meta:
                    ckpt(22.3)
                for bi, (t0, n) in enumerate(blks):
                    ps, pk = Rn4()
                    PE(lambda h, t0=t0, n=n, ps=ps: h.transpose(ps[0:n, 0:128], c32[:, t0:t0 + n], ident[:]), ['c32', 'ident'], [pk], inc=False)
                    PE(lambda h, t0=t0, n=n, ps=ps: h.transpose(ps[0:n, 128:160], kr32[:, t0:t0 + n], ident[0:32, 0:32]), ['kr32', 'ident'], [pk])
                    ACT(lambda h, bi=bi, n=n, ps=ps: h.activation(out=cst[0:n, bi, :], in_=ps[0:n, 0:128], func=AF.Copy), [pk], ['cst'])
                    DVE(lambda h, bi=bi, n=n, ps=ps: h.tensor_copy(out=krst[0:n, bi, :], in_=ps[0:n, 128:160]), [pk], ['krst'])
                    if (not is_meta) and n == 128:
                        DVE(lambda h, bi=bi, ps=ps: h.tensor_copy(out=ctok_c[:, (key_off + bi * 128) // 128, :], in_=ps[:, 0:128]), [pk], ['ctok_c'])
                if not is_meta:
                    ckpt(22.4)
                if is_meta:
                    ps, pk = Rn4()
                    PE(lambda h, ps=ps: h.matmul(ps[0:16, 0:128], lhsT=cT_m[:], rhs=ident_bf[:], start=True, stop=True), ['cT_m', 'ident_bf'], [pk])
                    ACT(lambda h, ps=ps: h.activation(out=ctok_m[:], in_=ps[0:16, 0:128], func=AF.Copy), [pk], ['ctok_m'])
                    for si in range(NSEQ):
                        S.dma('pool', 'st_c', [(ckv_p[si, 0:16, :], cst[48:64, 0, :]), (kr_p[si, 0:16, :], krst[48:64, 0, :])],
                              reads=['cst', 'krst'], writes=['o_ckv'])
                else:
                    for bi, (t0, n) in enumerate(blks):
                        S.dma('pool', 'st_c', [(c_dst[t0:t0 + n, :], cst[0:n, bi, :]), (kr_dst[t0:t0 + n, :], krst[0:n, bi, :])],
                              reads=['cst', 'krst'], writes=['o_ckv'])
                    ckpt(22.5)
                    for cch in range(nch):
                        ps, pk = Rn4()
                        PE(lambda h, cch=cch, ps=ps: h.matmul(ps[0:64, 0:128], lhsT=cT_c[:, key_off + cch * 64:key_off + (cch + 1) * 64],
                                                            rhs=ident_bf[:], start=True, stop=True), ['cT_c', 'ident_bf'], [pk])
                        ACT(lambda h, cch=cch, ps=ps: h.activation(out=ctok_d[:, cch, :], in_=ps[0:64, 0:128], func=AF.Copy), [pk], ['ctok_d'])

                if not is_meta:
                    ckpt(23)
                def rwkv_thread():
                    for cch in range(nch):
                        yield from rwkv_chunk(cch, ntok, need_y=do_attn)
                    if rw_tail[0] is not None:
                        rw_tail[0]()
                        rw_tail[0] = None

                if not do_attn:
                    for _ in rwkv_thread():
                        pass
                    return
                ckpt(40)
                for hh in range(8):
                    psn, pkn = Rn4()
                    for j in range(2):
                        PE(lambda h, j=j, hh=hh, psn=psn: h.matmul(psn[0:64, :ntok], lhsT=wq_bf[:, j, hh * 96:hh * 96 + 64], rhs=cqn[:, j, :ntok],
                                                                 start=(j == 0), stop=(j == 1)), ['cqn', 'smallw'], [pkn], inc=(j == 1))
                    psr, pkr = Rn4()
                    for j in range(2):
                        PE(lambda h, j=j, hh=hh, psr=psr: h.matmul(psr[0:32, 0:ntok], lhsT=wq_bf[:, j, hh * 96 + 64:hh * 96 + 96], rhs=cqn[:, j, :ntok],
                                                                 start=(j == 0), stop=(j == 1)), ['cqn', 'smallw'], [pkr], inc=False)
                    for j in range(2):
                        PE(lambda h, j=j, hh=hh, psr=psr: h.matmul(psr[0:32, 256:256 + ntok], lhsT=wqrot_bf[:, j, hh, :], rhs=cqn[:, j, :ntok],
                                                                 start=(j == 0), stop=(j == 1)), ['cqn', 'smallw2'], [pkr], inc=(j == 1))
                    qn = qn_p[hh % 2]
                    qk = 'qn%d' % (hh % 2)
                    ACT(lambda h, psn=psn, qn=qn: h.activation(out=qn[:, :ntok], in_=psn[0:64, :ntok], func=AF.Copy), [pkn], [qk])
                    psl, pkl = Rn4()
                    PE(lambda h, hh=hh, psl=psl, qn=qn: h.matmul(psl[:, :ntok], lhsT=wukT_bf[:, hh, :], rhs=qn[:, :ntok], start=True, stop=True),
                       [qk, 'smallw'], [pkl])
                    ACT(lambda h, hh=hh, psl=psl: h.activation(out=qlatT[:, hh, :ntok], in_=psl[:, :ntok], func=AF.Copy, scale=MLA_SCALE), [pkl], ['qlatT'])
                    DVE(lambda h, psr=psr: h.tensor_tensor(out=qt1[:, :ntok], in0=psr[0:32, 0:ntok], in1=tabs[:, 2, :ntok], op=ALU.mult), [pkr, 'tabs'], ['qt1'])
                    DVE(lambda h, psr=psr: h.tensor_tensor(out=qt2[:, :ntok], in0=psr[0:32, 256:256 + ntok], in1=tabs[:, 3, :ntok], op=ALU.mult), [pkr, 'tabs'], ['qt2'])
                    DVE(lambda h, hh=hh: h.tensor_tensor(out=qrT[:, hh, :ntok], in0=qt1[:, :ntok], in1=qt2[:, :ntok], op=ALU.add), ['qt1', 'qt2'], ['qrT'])
                ckpt(41)
                blocks = [(cT_m[:], krT_m[:], ctok_m[:], 16, 0, ['cT_m', 'krT_m', 'ctok_m'])]
                for (a1, a2, a3, nk) in full_blocks:
                    blocks.append((a1, a2, a3, nk, 0, ['cT_c', 'krT_c', 'ctok_c']))
                for cch in range(nch):
                    blocks.append((cT_c[:, key_off + cch * 64:key_off + (cch + 1) * 64], krT_c[:, key_off + cch * 64:key_off + (cch + 1) * 64],
                                   ctok_d[:, cch, :], 64, cch * 64, ['cT_c', 'krT_c', 'ctok_d']))
                pt_i = [0]
                v2 = lambda t, np_: t[0:np_, :].rearrange("p (j t) -> p j t", j=2)

                def score(hp, bi, pend):
                    a1, a2, a3, nk, q0, keys = blocks[bi]
                    ps, pk = Ln2()
                    o = v2(ps, nk)[:, :, q0:ntok]
                    if q0 == 0 and ntok == 256:
                        PE(lambda h: h.matmul(ps[0:nk, :], lhsT=a1, rhs=qlatT[:, 2 * hp:2 * hp + 2, :].rearrange("p a b -> p (a b)"), start=True, stop=False),
                           keys[0:1] + ['qlatT'], [pk])
                        PE(lambda h: h.matmul(ps[0:nk, :], lhsT=a2, rhs=qrT[:, 2 * hp:2 * hp + 2, :].rearrange("p a b -> p (a b)"), start=False, stop=True),
                           keys[1:2] + ['qrT'], [pk])
                    else:
                        for j in range(2):
                            oj = ps[0:nk, j * 256 + q0:j * 256 + ntok]
                            PE(lambda h, j=j, oj=oj: h.matmul(oj, lhsT=a1, rhs=qlatT[:, 2 * hp + j, q0:ntok], start=True, stop=False),
                               keys[0:1] + ['qlatT'], [pk])
                            PE(lambda h, j=j, oj=oj: h.matmul(oj, lhsT=a2, rhs=qrT[:, 2 * hp + j, q0:ntok], start=False, stop=True),
                               keys[1:2] + ['qrT'], [pk])
                    pi = pt_i[0] % 3
                    pt_i[0] += 1
                    PT = PT_p[pi]
                    ACT(lambda h: h.activation(out=PT[0:nk, :, q0:ntok], in_=o, func=AF.Exp), [pk], ['PT%d' % pi])
                    pend.append((bi, pi))

                def pv(hp, pend, nb):
                    bi, pi = pend.pop(0)
                    a1, a2, a3, nk, q0, keys = blocks[bi]
                    PT = PT_p[pi]
                    if q0 == 0 and ntok == 256:
                        PE(lambda h: h.matmul(psL[0][:, :], lhsT=a3, rhs=PT[0:nk, :, :].rearrange("p a b -> p (a b)"), start=(bi == 0), stop=(bi == nb - 1),
                                              skip_group_check=True), keys[2:3] + ['PT%d' % pi], ['psL0'])
                        PE(lambda h: h.matmul(psL[1][:, :], lhsT=ones_bf[0:nk, :], rhs=PT[0:nk, :, :].rearrange("p a b -> p (a b)"), start=(bi == 0), stop=(bi == nb - 1),
                                              skip_group_check=True), ['ones_bf', 'PT%d' % pi], ['psL1'])
                    else:
                        for j in range(2):
                            PE(lambda h, j=j: h.matmul(psL[0][:, j * 256 + q0:j * 256 + ntok], lhsT=a3, rhs=PT[0:nk, j, q0:ntok], start=(bi == 0 and j == 0), stop=(bi == nb - 1),
                                                       skip_group_check=True), keys[2:3] + ['PT%d' % pi], ['psL0'])
                            PE(lambda h, j=j: h.matmul(psL[1][:, j * 256 + q0:j * 256 + ntok], lhsT=ones_bf[0:nk, :], rhs=PT[0:nk, j, q0:ntok], start=(bi == 0 and j == 0), stop=(bi == nb - 1),
                                                       skip_group_check=True), ['ones_bf', 'PT%d' % pi], ['psL1'])

                def head_norm(hp):
                    for j in range(2):
                        latn = latn_p[j]
                        DVE(lambda h, j=j: h.reciprocal(out=rstd[:, :ntok], in_=psL[1][:, j * 256:j * 256 + ntok]), ['psL1'], ['rstd'])
                        DVE(lambda h, j=j, latn=latn: h.tensor_tensor(out=latn[:, :ntok], in0=psL[0][:, j * 256:j * 256 + ntok], in1=rstd[:, :ntok], op=ALU.mult),
                            ['psL0', 'rstd'], ['latn%d' % j])

                def head_tail(hp):
                    for j in range(2):
                        hh = 2 * hp + j
                        latn = latn_p[j]
                        ps, pk = Ln2()
                        PE(lambda h, hh=hh, latn=latn, ps=ps: h.matmul(ps[0:64, :ntok], lhsT=wuv_bf[:, hh * 64:(hh + 1) * 64], rhs=latn[:, :ntok], start=True, stop=True),
                           ['latn%d' % j, 'smallw'], [pk])
                        ACT(lambda h, hh=hh, ps=ps: h.activation(out=mlaT[:, hh, :ntok], in_=ps[0:64, :ntok], func=AF.Copy), [pk], ['mlaT'])

                def attn_thread():
                    nb = len(blocks)
                    ptail = None
                    for hp in range(4):
                        pend = []
                        score(hp, 0, pend)
                        score(hp, 1, pend)
                        for bi in range(nb):
                            if bi + 2 < nb:
                                score(hp, bi + 2, pend)
                            pv(hp, pend, nb)
                            if bi == 1 and ptail is not None:
                                head_tail(ptail)
                                ptail = None
                            yield
                        if ptail is not None:
                            head_tail(ptail)
                        head_norm(hp)
                        ptail = hp
                        yield
                    head_tail(ptail)
                    yield

                wo_slots = {}

                def wo_load(u):
                    sl = wo_ctr[0] % 3
                    wo_ctr[0] += 1
                    wo_slots[u] = sl
                    S.dma('sp', 'wo%d' % sl, [(wo_p[sl][:].rearrange("p g m -> p (g m)"), scr_out[u])], reads=['scr'], writes=['wo%d' % sl])

                for u in range(3):
                    wo_load(u)
                threads = [attn_thread(), rwkv_thread()]
                import os as _os
                if _os.environ.get("NOINT") == "1":
                    for t in threads:
                        for _ in t:
                            pass
                    threads = []
                while threads:
                    for t in list(threads):
                        try:
                            next(t)
                        except StopIteration:
                            threads.remove(t)
                ckpt(42)
                for dch in range(8):
                    ps, pk = Rn4()
                    for half in range(2):
                        u = dch * 2 + half
                        sl = wo_slots[u]
                        for g8 in range(8):
                            src = rwoT if half == 0 else mlaT
                            PE(lambda h, sl=sl, g8=g8, src=src, ps=ps, half=half: h.matmul(ps[:, :ntok], lhsT=wo_p[sl][:, g8, :], rhs=src[:, g8, :ntok],
                                                                                       start=(half == 0 and g8 == 0), stop=(half == 1 and g8 == 7)),
                               ['wo%d' % sl, 'rwoT' if half == 0 else 'mlaT'], [pk], inc=(half == 1 and g8 == 7))
                        if u + 3 < 16:
                            wo_load(u + 3)
                    DVE(lambda h, dch=dch, ps=ps: h.tensor_tensor(out=xT[:, dch, :ntok], in0=ps[:, :ntok], in1=xT[:, dch, :ntok], op=ALU.add), [pk, 'xT%d' % dch], ['xT%d' % dch])
                ckpt(43)
                rmsnorm_to_hT(2, ntok)
                ffn(1, ntok)
                ps, pk = Rn4()
                for dch in range(8):
                    sq = sq_p[dch % 2]
                    sk = 'sq%d' % (dch % 2)
                    ACT(lambda h, dch=dch, sq=sq: h.activation(out=sq[:, :ntok], in_=xT[:, dch, :ntok], func=AF.Square), ['xT%d' % dch], [sk])
                    PE(lambda h, dch=dch, sq=sq, ps=ps: h.matmul(ps[:, :ntok], lhsT=ones_bf[:], rhs=sq[:, :ntok], start=(dch == 0), stop=(dch == 7)),
                       [sk, 'ones_bf'], [pk], inc=(dch == 7))
                rstd_from(ps, pk, ntok, 128, 1.0 / D)
                for dch in range(8):
                    if True:
                        DVE(lambda h, dch=dch: h.scalar_tensor_tensor(out=xT[:, dch, :ntok], in0=xT[:, dch, :ntok], scalar=gains[:, 3, dch:dch + 1],
                                                                      in1=rstd[:, :ntok], op0=ALU.mult, op1=ALU.mult), ['xT%d' % dch, 'rstd', 'gains'], ['xT%d' % dch])
                    else:
                        POOL(lambda h, dch=dch: h.tensor_tensor(out=xT[:, dch, :ntok], in0=xT[:, dch, :ntok], in1=rstd[:, :ntok], op=ALU.mult),
                             ['xT%d' % dch, 'rstd'], ['xT%d' % dch])
                        POOL(lambda h, dch=dch: h.tensor_scalar(out=xT[:, dch, :ntok], in0=xT[:, dch, :ntok], scalar1=gains[:, 3, dch:dch + 1], scalar2=None,
                                                                op0=ALU.mult), ['xT%d' % dch, 'gains'], ['xT%d' % dch])
                for bi, (t0, n) in enumerate(blks):
                    for q4 in range(2):
                        ps, pk = Rn4()
                        for j in range(4):
                            dch = q4 * 4 + j
                            PE(lambda h, t0=t0, n=n, j=j, dch=dch, ps=ps: h.transpose(ps[0:n, j * 128:(j + 1) * 128], xT[:, dch, t0:t0 + n], ident[:]),
                               ['xT%d' % dch, 'ident'], [pk], inc=(j == 3))
                        EV(yout[0:n, q4 * 512:(q4 + 1) * 512], ps[0:n, :], [pk], ['yo%d' % q4])
                        S.dma('sp', 'st_y', [(y_dst[t0:t0 + n, q4 * 512:(q4 + 1) * 512], yout[0:n, q4 * 512:(q4 + 1) * 512])],
                              reads=['yo%d' % q4], writes=['o_y%d' % q4])

            rw_tail = [None]
            wo_ctr = [0]

            def rwkv_chunk(cch, ntok, need_y):
                cs = slice(cch * 64, (cch + 1) * 64)
                ps, pk = Rn()
                PE(lambda h, ps=ps: h.matmul(ps[0:64, :], lhsT=tw[:, cs], rhs=w_w2_bf[:], start=True, stop=True), ['tw', 'smallw'], [pk])
                DVE(lambda h, ps=ps: h.tensor_tensor(out=sg[:], in0=ps[0:64, :], in1=bc[:, 0, :], op=ALU.add), [pk, 'bc'], ['sg'])
                ACT(lambda h: h.activation(out=sg[:], in_=sg[:], func=AF.Sigmoid), ['sg'], ['sg'])
                yield
                pcl, kcl = Rn()
                pce, kce = Rn()
                for hh in range(8):
                    PE(lambda h, hh=hh, pcl=pcl: h.matmul(pcl[0:64, hh * 64:(hh + 1) * 64], lhsT=sg[:, hh * 64:(hh + 1) * 64], rhs=tri[:, 0, :], start=True, stop=True),
                       ['sg', 'tri'], [kcl], inc=(hh == 7))
                for hh in range(8):
                    PE(lambda h, hh=hh, pce=pce: h.matmul(pce[0:64, hh * 64:(hh + 1) * 64], lhsT=sg[:, hh * 64:(hh + 1) * 64], rhs=tri[:, 1, :], start=True, stop=True),
                       ['sg', 'tri'], [kce], inc=(hh == 7))
                v3 = lambda t: t[0:64, :].rearrange("p (a b) -> p a b", a=8)
                ACT(lambda h: h.activation(out=gam[:], in_=v3(pcl), func=AF.Exp, scale=-C0), [kcl], ['gam'])
                ACT(lambda h: h.activation(out=ginv[:], in_=v3(pcl), func=AF.Exp, scale=C0), [kcl], ['ginv'])
                ACT(lambda h: h.activation(out=gprev[:], in_=v3(pce), func=AF.Exp, scale=-C0), [kce], ['gprev'])
                if need_y:
                    ckpt(30)
                yield
                DVE(lambda h: h.tensor_tensor(out=bT[:], in0=b_t[:, :, cs], in1=ginv[:], op=ALU.mult), ['b_t', 'ginv'], ['bT'])
                DVE(lambda h: h.tensor_tensor(out=kT[:], in0=kp_t[:, :, cs], in1=ginv[:], op=ALU.mult), ['kp_t', 'ginv'], ['kT'])
                DVE(lambda h: h.scalar_tensor_tensor(out=aT[:], in0=kk_t[:, :, cs], scalar=-1.0, in1=gprev[:], op0=ALU.mult, op1=ALU.mult),
                    ['kk_t', 'gprev'], ['aT'])
                POOL(lambda h: h.tensor_tensor(out=rT[:], in0=r_t[:, :, cs], in1=gam[:], op=ALU.mult), ['r_t', 'gam'], ['rT'])
                if need_y:
                    POOL(lambda h: h.tensor_tensor(out=prodT[:], in0=r_t[:, :, cs], in1=kp_t[:, :, cs], op=ALU.mult), ['r_t', 'kp_t'], ['prodT'])

                def tr8(src_fn, dst, dkey, rkeys):
                    ps, pk = Rn()
                    for hh in range(8):
                        PE(lambda h, hh=hh, ps=ps: h.matmul(ps[0:64, hh * 64:(hh + 1) * 64], lhsT=src_fn(hh), rhs=ident_bf[0:64, 0:64], start=True, stop=True),
                           rkeys + ['ident_bf'], [pk], inc=(hh == 7))
                    EV(dst[:], ps[0:64, :], [pk], [dkey])

                yield
                tr8(lambda hh: bT[:, hh, :], bt, 'bt', ['bT'])
                yield
                tr8(lambda hh: kT[:, hh, :], kt, 'kt', ['kT'])
                yield
                tr8(lambda hh: v_t[:, hh, cs], vt, 'vt', ['v_t'])
                yield

                def sc8(l_fn, r_fn, rkeys, mask_i, dst, dkey):
                    ps, pk = Rn()
                    for hh in range(8):
                        PE(lambda h, hh=hh, ps=ps: h.matmul(ps[0:64, hh * 64:(hh + 1) * 64], lhsT=l_fn(hh), rhs=r_fn(hh), start=True, stop=True),
                           rkeys, [pk], inc=(hh == 7))
                    DVE(lambda h, ps=ps: h.tensor_tensor(out=dst[:].rearrange("p (a b) -> p a b", a=8), in0=v3(ps),
                                                         in1=masks[:, mask_i, :].unsqueeze(1).to_broadcast([64, 8, 64]), op=ALU.mult),
                        [pk, 'masks'], [dkey])

                sc8(lambda hh: bT[:, hh, :], lambda hh: aT[:, hh, :], ['bT', 'aT'], 0, Qm[0], 'Qm0')
                yield
                sc8(lambda hh: aT[:, hh, :], lambda hh: bT[:, hh, :], ['bT', 'aT'], 1, Pm[0], 'Pm0')
                yield
                sc8(lambda hh: kT[:, hh, :], lambda hh: aT[:, hh, :], ['kT', 'aT'], 0, AKT, 'AKT')
                yield
                if need_y:
                    sc8(lambda hh: bT[:, hh, :], lambda hh: rT[:, hh, :], ['bT', 'rT'], 2, RBT, 'RBT')
                    yield
                    sc8(lambda hh: kT[:, hh, :], lambda hh: rT[:, hh, :], ['kT', 'rT'], 2, RKT, 'RKT')
                    yield
                if need_y:
                    ckpt(31)
                if rw_tail[0] is not None:
                    rw_tail[0]()
                    rw_tail[0] = None
                    yield
                POOL(lambda h: h.tensor_tensor(out=TT[0][:].rearrange("p (a b) -> p a b", a=8), in0=Qm[0][:].rearrange("p (a b) -> p a b", a=8),
                                               in1=ident_bf[0:64, 0:64].unsqueeze(1).to_broadcast([64, 8, 64]), op=ALU.add), ['Qm0', 'ident_bf'], ['TT0'])
                cur = 0
                tcur = 0

                def mm8(l, lk, r, rk, evac):
                    ps, pk = Rn()
                    for hh in range(8):
                        PE(lambda h, hh=hh, ps=ps: h.matmul(ps[0:64, hh * 64:(hh + 1) * 64], lhsT=l[:, hh * 64:(hh + 1) * 64], rhs=r[:, hh * 64:(hh + 1) * 64],
                                                            start=True, stop=True), [lk, rk], [pk], inc=(hh == 7))
                    evac(ps, pk)

                for j in range(5):
                    yield
                    nx = 1 - cur
                    mm8(Qm[cur], 'Qm%d' % cur, Pm[cur], 'Pm%d' % cur,
                        lambda ps, pk, nx=nx: ACT(lambda h: h.activation(out=Pm[nx][:], in_=ps[0:64, :], func=AF.Copy), [pk], ['Pm%d' % nx]))
                    if j < 4:
                        mm8(Pm[cur], 'Pm%d' % cur, Qm[cur], 'Qm%d' % cur,
                            lambda ps, pk, nx=nx: ACT(lambda h: h.activation(out=Qm[nx][:], in_=ps[0:64, :], func=AF.Copy), [pk], ['Qm%d' % nx]))
                    if j >= 1:
                        tn = 1 - tcur
                        mm8(Pm[cur], 'Pm%d' % cur, TT[tcur], 'TT%d' % tcur,
                            lambda ps, pk, tn=tn, tc=tcur: DVE(lambda h: h.tensor_tensor(out=TT[tn][:], in0=ps[0:64, :], in1=TT[tc][:], op=ALU.add),
                                                               [pk, 'TT%d' % tc], ['TT%d' % tn]))
                        tcur = tn
                    cur = nx
                yield
                tn = 1 - tcur
                mm8(Pm[cur], 'Pm%d' % cur, TT[tcur], 'TT%d' % tcur,
                    lambda ps, pk, tn=tn, tc=tcur: DVE(lambda h: h.tensor_tensor(out=TT[tn][:], in0=ps[0:64, :], in1=TT[tc][:], op=ALU.add),
                                                       [pk, 'TT%d' % tc], ['TT%d' % tn]))
                tcur = tn
                Tf = TT[tcur]
                Tk = 'TT%d' % tcur
                if need_y:
                    ckpt(32)
                yield
                ps, pk = Rn()
                for hh in range(8):
                    hs = slice(hh * 64, (hh + 1) * 64)
                    PE(lambda h, hh=hh, hs=hs, ps=ps: h.matmul(ps[0:64, hs], lhsT=aT[:, hh, :], rhs=Sbf[:, hh, :], start=True, stop=False), ['aT', 'Sbf'], [pk], inc=False)
                    PE(lambda h, hh=hh, hs=hs, ps=ps: h.matmul(ps[0:64, hs], lhsT=AKT[:, hs], rhs=vt[:, hs], start=False, stop=True), ['AKT', 'vt'], [pk], inc=(hh == 7))
                ACT(lambda h, ps=ps: h.activation(out=Xb[:], in_=ps[0:64, :], func=AF.Copy), [pk], ['Xb'])
                yield
                ps, pk = Rn()
                for hh in range(8):
                    hs = slice(hh * 64, (hh + 1) * 64)
                    PE(lambda h, hs=hs, ps=ps: h.matmul(ps[0:64, hs], lhsT=Tf[:, hs], rhs=Xb[:, hs], start=True, stop=True), [Tk, 'Xb'], [pk], inc=(hh == 7))
                DVE(lambda h, ps=ps: h.tensor_copy(out=Ub[:], in_=ps[0:64, :]), [pk], ['Ub'])
                yield
                if need_y:
                    psy, pky = Rn()
                    for hh in range(8):
                        hs = slice(hh * 64, (hh + 1) * 64)
                        PE(lambda h, hh=hh, hs=hs: h.matmul(psy[0:64, hs], lhsT=rT[:, hh, :], rhs=Sbf[:, hh, :], start=True, stop=False), ['rT', 'Sbf'], [pky], inc=False)
                        PE(lambda h, hs=hs: h.matmul(psy[0:64, hs], lhsT=RBT[:, hs], rhs=Ub[:, hs], start=False, stop=False), ['RBT', 'Ub'], [pky], inc=False)
                        PE(lambda h, hs=hs: h.matmul(psy[0:64, hs], lhsT=RKT[:, hs], rhs=vt[:, hs], start=False, stop=True), ['RKT', 'vt'], [pky], inc=(hh == 7))
                if need_y:
                    ckpt(34)
                yield
                pss, pks = Rn()
                for hh in range(8):
                    hs = slice(hh * 64, (hh + 1) * 64)
                    PE(lambda h, hs=hs: h.matmul(pss[0:64, hs], lhsT=bt[:, hs], rhs=Ub[:, hs], start=True, stop=False), ['bt', 'Ub'], [pks], inc=False)
                    PE(lambda h, hs=hs: h.matmul(pss[0:64, hs], lhsT=kt[:, hs], rhs=vt[:, hs], start=False, stop=True), ['kt', 'vt'], [pks], inc=(hh == 7))
                DVE(lambda h: h.tensor_tensor(out=S32[:], in0=v3(pss), in1=S32[:], op=ALU.add), [pks, 'S32'], ['S32'])
                DVE(lambda h: h.tensor_tensor(out=Sbf[:], in0=S32[:], in1=gam[:, :, 63:64].to_broadcast([64, 8, 64]), op=ALU.mult), ['S32', 'gam'], ['Sbf'])
                DVE(lambda h: h.tensor_tensor(out=S32[:], in0=S32[:], in1=gam[:, :, 63:64].to_broadcast([64, 8, 64]), op=ALU.mult), ['S32', 'gam'], ['S32'])
                yield
                if not need_y:
                    return
                ckpt(35)
                ACT(lambda h: h.activation(out=ytm[:], in_=v3(psy), func=AF.Copy), [pky], ['ytm'])
                ACT(lambda h: h.activation(out=ysq[:], in_=v3(psy), func=AF.Square), [pky], ['ysq'])
                DVE(lambda h: h.tensor_reduce(out=st8[:, 0, :], in_=ytm[:], axis=AX.X, op=ALU.add), ['ytm'], ['st8a'])
                DVE(lambda h: h.tensor_reduce(out=st8[:, 1, :], in_=ysq[:], axis=AX.X, op=ALU.add), ['ysq'], ['st8b'])
                DVE(lambda h: h.tensor_scalar(out=st8[:, 2, :], in0=st8[:, 0, :], scalar1=1.0 / 64, scalar2=None, op0=ALU.mult), ['st8a'], ['st8c'])
                DVE(lambda h: h.tensor_tensor(out=st8[:, 5, :], in0=st8[:, 2, :], in1=st8[:, 2, :], op=ALU.mult), ['st8c'], ['st8f'])
                DVE(lambda h: h.scalar_tensor_tensor(out=st8[:, 3, :], in0=st8[:, 1, :], scalar=1.0 / 64, in1=st8[:, 5, :], op0=ALU.mult, op1=ALU.subtract),
                    ['st8b', 'st8f'], ['st8d'])
                DVE(lambda h: h.tensor_scalar(out=st8[:, 3, :], in0=st8[:, 3, :], scalar1=GN_EPS, scalar2=None, op0=ALU.add), ['st8d'], ['st8d'])
                ACT(lambda h: h.activation(out=st8[:, 3, :], in_=st8[:, 3, :], func=AF.Sqrt), ['st8d'], ['st8d'])
                DVE(lambda h: h.reciprocal(out=st8[:, 3, :], in_=st8[:, 3, :]), ['st8d'], ['st8d'])
                DVE(lambda h: h.tensor_tensor(out=ytm[:], in0=ytm[:], in1=st8[:, 2, :].unsqueeze(2).to_broadcast([64, 8, 64]), op=ALU.subtract), ['ytm', 'st8c'], ['ytm'])
                DVE(lambda h: h.tensor_tensor(out=ytm[:], in0=ytm[:], in1=st8[:, 3, :].unsqueeze(2).to_broadcast([64, 8, 64]), op=ALU.mult), ['ytm', 'st8d'], ['ytm'])
                yf = ytm[:].rearrange("p a b -> p (a b)")
                DVE(lambda h: h.tensor_tensor(out=yf, in0=yf, in1=bc[:, 1, :], op=ALU.mult), ['ytm', 'bc'], ['ytm'])
                DVE(lambda h: h.tensor_tensor(out=yf, in0=yf, in1=bc[:, 2, :], op=ALU.add), ['ytm', 'bc'], ['ytm'])
                yield
                ps, pk = Rn()
                for hh in range(8):
                    PE(lambda h, hh=hh, ps=ps: h.matmul(ps[0:64, hh:hh + 1], lhsT=prodT[:, hh, :], rhs=rk_bf[:, hh:hh + 1], start=True, stop=True),
                       ['prodT', 'rk_bf'], [pk], inc=(hh == 7))
                ACT(lambda h, ps=ps: h.activation(out=st8[:, 4, :], in_=ps[0:64, 0:8], func=AF.Copy), [pk], ['st8e'])
                DVE(lambda h: h.tensor_tensor(out=ysq[:], in0=vt[:].rearrange("p (a b) -> p a b", a=8), in1=st8[:, 4, :].unsqueeze(2).to_broadcast([64, 8, 64]), op=ALU.mult),
                    ['vt', 'st8e'], ['ysq'])
                DVE(lambda h: h.tensor_tensor(out=ytm[:], in0=ytm[:], in1=ysq[:], op=ALU.add), ['ytm', 'ysq'], ['ytm'])
                ckpt(36)
                yield
                ps, pk = Rn()
                PE(lambda h, ps=ps: h.matmul(ps[0:64, :], lhsT=sgl[:, cs], rhs=w_g2_bf[:], start=True, stop=True), ['sgl', 'smallw'], [pk])
                DVE(lambda h, ps=ps: h.tensor_tensor(out=rwo[:], in0=ps[0:64, :], in1=yf, op=ALU.mult), [pk, 'ytm'], ['rwo'])
                def tail():
                    ps, pk = Rn()
                    for hh in range(8):
                        hs = slice(hh * 64, (hh + 1) * 64)
                        PE(lambda h, hs=hs, ps=ps: h.matmul(ps[0:64, hs], lhsT=rwo[:, hs], rhs=ident_bf[0:64, 0:64], start=True, stop=True), ['rwo', 'ident_bf'], [pk], inc=(hh == 7))
                    ACT(lambda h, ps=ps: h.activation(out=rwoT[:, :, cs], in_=v3(ps), func=AF.Copy), [pk], ['rwoT'])

                rw_tail[0] = tail

            tmpWA = c32

            POOL(lambda h: h.memset(xin[0:64, 0, :], 0.0), ['xin0'], ['xin0'])
            S.dma('sp', 'ld_x', [(xin[48:64, 0, :], meta)], reads=['xin0'], writes=['xin0'])
            POOL(lambda h: h.memset(carry[:], 0.0), [], ['carry'])
            POOL(lambda h: h.memset(S32[:], 0.0), [], ['S32'])
            POOL(lambda h: h.memset(Sbf[:], 0.0), [], ['Sbf'])
            POOL(lambda h: h.memset(als[:], 0.0), [], ['als'])

            def load_x_prompt(si, ti):
                def f():
                    S.dma('sp', 'ld_x', [(xin[:, b, :], xp[si, ti * NT + b * 128: ti * NT + (b + 1) * 128, :]) for b in range(2)],
                          writes=['xin0', 'xin1'])
                return f

            def load_x_sample():
                S.dma('sp', 'ld_x', [(xin[0:64, 0, :], xs)], writes=['xin0'])

            ntile = SEQ // NT
            tile(64, None, [tabM[i] for i in range(4)], 0, [], False, None, None, None, is_meta=True,
                 nxt=(load_x_prompt(0, 0) if NSEQ > 0 else load_x_sample))
            ckpt(10)
            DVE(lambda h: h.tensor_copy(out=S32m[:], in_=S32[:]), ['S32'], ['S32m'])
            DVE(lambda h: h.tensor_copy(out=carry_m[:], in_=carry[:]), ['carry'], ['carry_m'])

            def store_state(wkv_dst, sh_dst):
                for half in range(2):
                    ps, pk = Rn()
                    for j in range(4):
                        hh = half * 4 + j
                        PE(lambda h, hh=hh, j=j, ps=ps: h.transpose(ps[0:64, j * 64:(j + 1) * 64], S32[:, hh, :], ident[0:64, 0:64]), ['S32', 'ident'], [pk], inc=(j == 3))
                    ACT(lambda h, half=half, ps=ps: h.activation(out=ytm[:, half * 4:(half + 1) * 4, :], in_=ps[0:64, 0:256].rearrange("p (a b) -> p a b", a=4), func=AF.Copy),
                        [pk], ['ytm'])
                S.dma('pool', 'st_s', [(wkv_dst.rearrange("h v k -> v h k"), ytm[:])], reads=['ytm'], writes=['o_s'])
                S.dma('pool', 'st_s', [(sh_dst[0:1536].rearrange("(c p) -> p c", p=64), carry[0:64, 0:24]),
                                       (sh_dst[1536:1792].rearrange("(c p) -> p c", p=128), carry[:, 24:26])],
                      reads=['carry'], writes=['o_s'], allow_slow_non_contiguous=True)

            for si in range(NSEQ):
                DVE(lambda h: h.tensor_copy(out=S32[:], in_=S32m[:]), ['S32m'], ['S32'])
                ACT(lambda h: h.activation(out=Sbf[:], in_=S32m[:], func=AF.Copy), ['S32m'], ['Sbf'])
                DVE(lambda h: h.tensor_copy(out=carry[:], in_=carry_m[:]), ['carry_m'], ['carry'])
                for ti in range(ntile):
                    if ti + 1 < ntile:
                        nxt = load_x_prompt(si, ti + 1)
                    elif si + 1 < NSEQ:
                        nxt = load_x_prompt(si + 1, 0)
                    else:
                        nxt = load_x_sample
                    p0 = N_META + ti * NT
                    fb = [(cT_c[:, b * 128:(b + 1) * 128], krT_c[:, b * 128:(b + 1) * 128], ctok_c[:, b, :], 128) for b in range(ti * NT // 128)]
                    tile(NT, None, [tabP[i, :, p0:p0 + NT] for i in range(4)], ti * NT, fb, True,
                         y_p[si, ti * NT:(ti + 1) * NT, :], ckv_p[si, p0:p0 + NT, :], kr_p[si, p0:p0 + NT, :], nxt=nxt)
                store_state(wkv_p[si], sh_p[si])

            ckpt(50)
            S.dma('pool', 'ld_s', [(dmix[:, :, :].rearrange("p a b -> p (a b)")[:, 0:512].rearrange("p (a b) -> p a b", a=16), ckr.rearrange("(b p) r -> p b r", p=128))],
                  writes=['dmix'])
            S.dma('pool', 'ld_s', [(ytm[:], swkv.rearrange("h v k -> v h k"))], writes=['ytm'])
            krv = dmix[:, :, :].rearrange("p a b -> p (a b)")[:, 0:512].rearrange("p (a b) -> p a b", a=16)
            ckvv = None

            def sample_cache_c():
                pass

            S.dma('sp', 'ld_x', [(xin[:, 1, :].rearrange("p (b c) -> p b c", b=8), cckv[0:1024, :].rearrange("(b p) c -> p b c", p=128))], writes=['xin1'])
            for half in range(2):
                if half == 1:
                    S.dma('sp', 'ld_x', [(xin[:, 1, :].rearrange("p (b c) -> p b c", b=8), cckv[1024:2048, :].rearrange("(b p) c -> p b c", p=128))],
                          reads=['xin1'], writes=['xin1'])
                xv = xin[:, 1, :].rearrange("p (b c) -> p b c", b=8)
                DVE(lambda h, half=half, xv=xv: h.tensor_copy(out=ctok_c[:, half * 8:(half + 1) * 8, :], in_=xv), ['xin1'], ['ctok_c'])
                for b in range(8):
                    ps, pk = Rn()
                    PE(lambda h, b=b, ps=ps, xv=xv: h.transpose(ps[:, 0:128], xv[:, b, :], ident[:]), ['xin1', 'ident'], [pk])
                    EV(cT_c[:, (half * 8 + b) * 128:(half * 8 + b + 1) * 128], ps[:, 0:128], [pk], ['cT_c'])
            for b in range(16):
                ps, pk = Rn()
                PE(lambda h, b=b, ps=ps: h.transpose(ps[0:32, 0:128], krv[:, b, :], ident[:]), ['dmix', 'ident'], [pk])
                EV(krT_c[:, b * 128:(b + 1) * 128], ps[0:32, 0:128], [pk], ['krT_c'])
            swv = ytm
            for half in range(2):
                ps, pk = Rn()
                for j in range(4):
                    hh = half * 4 + j
                    PE(lambda h, hh=hh, j=j, ps=ps: h.transpose(ps[0:64, j * 64:(j + 1) * 64], swv[:, hh, :], ident[0:64, 0:64]), ['ytm', 'ident'], [pk], inc=(j == 3))
                DVE(lambda h, half=half, ps=ps: h.tensor_copy(out=S32[:, half * 4:(half + 1) * 4, :], in_=ps[0:64, 0:256].rearrange("p (a b) -> p a b", a=4)), [pk], ['S32'])
            ACT(lambda h: h.activation(out=Sbf[:], in_=S32[:], func=AF.Copy), ['S32'], ['Sbf'])
            S.dma('sp', 'ld_x', [(carry[0:64, 0:24], ssh[0:1536].rearrange("(c p) -> p c", p=64)),
                                 (carry[:, 24:26], ssh[1536:1792].rearrange("(c p) -> p c", p=128))], reads=['carry'], writes=['carry'],
                  allow_slow_non_contiguous=True)
            if NSEQ == 0:
                pass
            fb = [(cT_c[:, b * 128:(b + 1) * 128], krT_c[:, b * 128:(b + 1) * 128], ctok_c[:, b, :], 128) for b in range(16)]
            tile(64, None, [tabS[i] for i in range(4)], PAST, fb, True, y_s, ckv_s, kr_s)
            store_state(wkv_s, sh_s)


        except StopBuild:
            pass
        S.finish('sp')
        S.emit()
        print("ops:", S.nops, "sems:", S.nsem, flush=True)
    return nc


def make_consts(SEQ):
    ident = np.eye(128, dtype=np.float32)
    i = np.arange(64)[:, None]
    j = np.arange(64)[None, :]
    import ml_dtypes
    masks = np.stack([(i < j), (i > j), (i <= j)]).astype(np.float32).astype(ml_dtypes.bfloat16)
    tri = np.stack([(i <= j), (i < j)]).astype(np.float32)
    half = 16
    inv = (10000.0 ** (-np.arange(half, dtype=np.float32) / half)).astype(np.float32)

    def tab(pos):
        ang = pos.astype(np.float32)[None, :] * inv[:, None]
        cos = np.cos(ang).astype(np.float32)
        sin = np.sin(ang).astype(np.float32)
        c2 = np.concatenate([cos, cos], 0)
        s2 = np.concatenate([sin, sin], 0)
        sc = np.float32(MLA_SCALE)
        return np.stack([c2, s2, c2 * sc, s2 * sc]).astype(np.float32)

    tabp = tab(np.arange(N_META + SEQ))
    tabs = tab(N_META + PAST + np.arange(64))
    tabm = tab(np.concatenate([np.zeros(48), np.arange(16)]))
    return dict(c_ident=ident, c_masks=masks, c_tri=tri, c_tabp=tabp, c_tabs=tabs, c_tabm=tabm)


_CACHE = {}


def run(inputs, SEQ=4096, NSEQ=2, ncores=8, stop=99):
    key = (SEQ, NSEQ, stop)
    if key not in _CACHE:
        _CACHE[key] = build(SEQ, NSEQ, stop)
    nc = _CACHE[key]
    consts = make_consts(SEQ)
    f32 = lambda a: np.ascontiguousarray(np.asarray(a, dtype=np.float32))
    wmap = {}
    for n in W_NAMES:
        a = f32(inputs[n])
        if n != "final_norm":
            a = a[0]
        wmap[n] = np.ascontiguousarray(a.reshape(W_SHAPES[n]))
    in_maps = []
    for c in range(ncores):
        m = dict(wmap)
        m.update(consts)
        m["xp"] = f32(inputs["x_prompt"][c * NSEQ:(c + 1) * NSEQ, :SEQ])
        m["xs"] = f32(inputs["x_sample"][c])
        m["cckv"] = f32(inputs["cache_ckv"][0, c])
        m["ckr"] = f32(inputs["cache_krope"][0, c])
        m["swkv"] = f32(inputs["state_wkv"][0, c])
        m["ssh"] = f32(inputs["state_shift"][0, c, 0])
        m["meta"] = f32(inputs["meta_tokens"])
        in_maps.append(m)
    res = run_bass_kernel_spmd(nc, in_maps, core_ids=list(range(ncores)))
    R = res.results
    cat = lambda k: np.concatenate([np.asarray(r[k]) for r in R], axis=0)
    stk = lambda k: np.stack([np.asarray(r[k]) for r in R], axis=0)
    y_p = cat("y_p")
    y_s = stk("y_s")
    outs = (y_p, y_s,
            cat("ckv_p")[None], cat("kr_p")[None], cat("wkv_p")[None], cat("sh_p")[None, :, None, :],
            stk("ckv_s")[None], stk("kr_s")[None], stk("wkv_s")[None], stk("sh_s")[None, :, None, :])
    return tuple(np.ascontiguousarray(o.astype(np.float32)) for o in outs)


def kernel(**inputs):
    return run(inputs, SEQ=4096, NSEQ=2, ncores=8)
```

```python
import math
from contextlib import ExitStack

import numpy as np
import concourse.bass as bass
import concourse.mybir as mybir
from concourse.bass_utils import run_bass_kernel_spmd

F32 = mybir.dt.float32
BF16 = mybir.dt.bfloat16
ALU = mybir.AluOpType
AF = mybir.ActivationFunctionType
AX = mybir.AxisListType

ENGS = ('pe', 'act', 'dve', 'pool', 'sp')
SEM_LIMIT = 30000

D = 1024
DFF = 2816
NFC = 22
N_META = 16
PAST = 2048
NT = 256
C0 = math.exp(-0.5)
MLA_SCALE = 96 ** -0.5
NORM_EPS = 1e-6
GN_EPS = 64e-5


class COp:
    __slots__ = ("eng", "idx", "fn", "need", "sig")

    def __init__(self, eng, idx, fn):
        self.eng = eng
        self.idx = idx
        self.fn = fn
        self.need = False
        self.sig = None


class Sched:
    def __init__(self, nc, es):
        self.nc = nc
        self.es = es
        self.prog = {e: [] for e in ENGS}
        self.seen = {e: {} for e in ENGS}
        self.lastw = {}
        self.readers = {}
        self.nsem = 0
        self.rings = {}
        self.nops = {e: 0 for e in ENGS}
        self.cnt = {e: 0 for e in ENGS}

    def newsem(self):
        s = self.es.enter_context(self.nc.semaphore("s%d" % self.nsem))
        self.nsem += 1
        return s

    def _deps(self, eng, reads, writes):
        toks = []
        skip_own = (eng == 'pe')

        def own(t):
            return t[0] == 'c' and t[1].eng == eng

        for k in reads:
            t = self.lastw.get(k)
            if t is not None and not (skip_own and own(t)):
                toks.append(t)
        for k in writes:
            t = self.lastw.get(k)
            if t is not None and not (skip_own and own(t)):
                toks.append(t)
            for t in self.readers.get(k, ()):
                if not (skip_own and own(t)):
                    toks.append(t)
        best = {}
        for t in toks:
            if t[0] == 'c':
                key = ('c', t[1].eng)
                val = t[1].idx
            else:
                key = ('d', id(t[1]))
                val = t[2]
            if key not in best or best[key][0] < val:
                best[key] = (val, t)
        seen = self.seen[eng]
        out = []
        for key, (val, t) in best.items():
            if seen.get(key, -1) < val:
                seen[key] = val
                out.append(t)
        return out

    def _wait(self, eng, t):
        if t[0] == 'c':
            t[1].need = True
        self.prog[eng].append(('w', t))

    def _record(self, tok, reads, writes):
        for k in writes:
            self.lastw[k] = tok
            self.readers[k] = []
        for k in reads:
            self.readers.setdefault(k, []).append(tok)

    def op(self, eng, fn, reads=(), writes=(), inc=True):
        ps_r = [k for k in reads if k.startswith('ps')]
        if ps_r:
            reads = [k for k in reads if not k.startswith('ps')]
            writes = list(writes) + ps_r
        for t in self._deps(eng, reads, writes):
            self._wait(eng, t)
        o = COp(eng, self.cnt[eng], fn)
        self.cnt[eng] += 1
        self.prog[eng].append(('c', o))
        self.nops[eng] += 1
        tok = ('c', o)
        self._record(tok, reads, writes)
        return tok

    def dma(self, eng, chan, pairs, reads=(), writes=(), **kw):
        ring = self.rings.setdefault(eng, {'sems': [], 'i': 0})
        nring = 24 if eng == 'sp' else 16
        if len(ring['sems']) < nring:
            ring['sems'].append([self.newsem(), 0])
        ent = ring['sems'][ring['i'] % nring]
        ring['i'] += 1
        if ent[1] + 16 * len(pairs) >= SEM_LIMIT:
            s0, v0 = ent[0], ent[1]
            self.prog[eng].append(('w', ('d', s0, v0)))
            ent[0], ent[1] = self.newsem(), 0
        sem, cnt = ent[0], ent[1]
        key = ('d', id(sem))
        if cnt > 0 and self.seen[eng].get(key, -1) < cnt:
            self.seen[eng][key] = cnt
            self.prog[eng].append(('w', ('d', sem, cnt)))
        for t in self._deps(eng, reads, writes):
            self._wait(eng, t)
        for (o, i) in pairs:
            self.prog[eng].append(('raw', lambda h, o=o, i=i, sem=sem: h.dma_start(out=o, in_=i, **kw).then_inc(sem, 16)))
            ent[1] += 16
            self.nops[eng] += 1
        tok = ('d', sem, ent[1])
        self._record(tok, reads, writes)
        return tok

    def finish(self, eng='sp'):
        toks = []
        for k, t in self.lastw.items():
            toks.append(t)
        best = {}
        for t in toks:
            if t[0] == 'c':
                key = ('c', t[1].eng)
                val = t[1].idx
            else:
                key = ('d', id(t[1]))
                val = t[2]
            if key not in best or best[key][0] < val:
                best[key] = (val, t)
        for key, (val, t) in best.items():
            if self.seen[eng].get(key, -1) < val:
                self.seen[eng][key] = val
                self._wait(eng, t)

    def emit(self):
        nc = self.nc
        prog = self.prog
        nincs = {}
        for e in ENGS:
            sem, c = None, 0
            n = 0
            for ent in prog[e]:
                if ent[0] == 'c' and ent[1].need:
                    if sem is None or c >= SEM_LIMIT:
                        sem, c = self.newsem(), 0
                    c += 1
                    n += 1
                    ent[1].sig = (sem, c)
            nincs[e] = n
        print("incs:", nincs, flush=True)

        def run(e, h):
            for ent in prog[e]:
                k = ent[0]
                if k == 'c':
                    o = ent[1]
                    ins = o.fn(h)
                    if o.need:
                        ins.then_inc(o.sig[0], 1)
                elif k == 'w':
                    t = ent[1]
                    if t[0] == 'c':
                        h.wait_ge(t[1].sig[0], t[1].sig[1])
                    else:
                        h.wait_ge(t[1], t[2])
                else:
                    ent[1](h)

        with nc.Block() as block:
            @block.tensor
            def _(e):
                run('pe', e)

            @block.scalar
            def _(e):
                run('act', e)

            @block.vector
            def _(e):
                run('dve', e)

            @block.gpsimd
            def _(e):
                run('pool', e)

            @block.sync
            def _(e):
                run('sp', e)


class StopBuild(Exception):
    pass


class RR:
    def __init__(self, items):
        self.items = items
        self.i = 0

    def __call__(self):
        r = self.items[self.i % len(self.items)]
        self.i += 1
        return r


W_NAMES = ["norm_ffn1", "ffn1_w1", "ffn1_w3", "ffn1_w2", "norm_mix", "w_in", "mu_shift", "w0", "w_w2", "a0",
           "w_a2", "w_g2", "k_k", "k_a", "r_k", "ln_x_w", "ln_x_b", "q_norm", "w_q_up", "kv_norm", "w_uk", "w_uv",
           "w_out", "norm_ffn2", "ffn2_w1", "ffn2_w3", "ffn2_w2", "final_norm"]
W_SHAPES = {
    "norm_ffn1": [D], "ffn1_w1": [D, DFF], "ffn1_w3": [D, DFF], "ffn1_w2": [DFF, D], "norm_mix": [D],
    "w_in": [D, 2208], "mu_shift": [1792], "w0": [512], "w_w2": [64, 512], "a0": [512], "w_a2": [64, 512],
    "w_g2": [128, 512], "k_k": [512], "k_a": [512], "r_k": [512], "ln_x_w": [512], "ln_x_b": [512],
    "q_norm": [256], "w_q_up": [256, 768], "kv_norm": [128], "w_uk": [128, 512], "w_uv": [128, 512],
    "w_out": [D, D], "norm_ffn2": [D], "ffn2_w1": [D, DFF], "ffn2_w3": [D, DFF], "ffn2_w2": [DFF, D],
    "final_norm": [D],
}


def build(SEQ=4096, NSEQ=2, stop=99):
    nc = bass.Bass("TRN2", target_bir_lowering=False)
    es = ExitStack()
    with es:
        S = Sched(nc, es)

        def din(name, shape, dt=F32):
            return nc.dram_tensor(name, shape, dt, kind="ExternalInput").ap()

        def dout(name, shape):
            return nc.dram_tensor(name, shape, F32, kind="ExternalOutput").ap()

        def sb(name, shape, dt=F32):
            return es.enter_context(nc.sbuf_tensor(name, shape, dt))

        xp = din("xp", [NSEQ, SEQ, D])
        xs = din("xs", [64, D])
        cckv = din("cckv", [PAST, 128])
        ckr = din("ckr", [PAST, 32])
        swkv = din("swkv", [8, 64, 64])
        ssh = din("ssh", [1792])
        meta = din("meta", [N_META, D])
        W = {n: din(n, W_SHAPES[n]) for n in W_NAMES}
        ident_d = din("c_ident", [128, 128])
        masks_d = din("c_masks", [3, 64, 64], BF16)
        tri_d = din("c_tri", [2, 64, 64])
        tabP = din("c_tabp", [4, 32, N_META + SEQ])
        tabS = din("c_tabs", [4, 32, 64])
        tabM = din("c_tabm", [4, 32, 64])

        y_p = dout("y_p", [NSEQ, SEQ, D])
        y_s = dout("y_s", [64, D])
        ckv_p = dout("ckv_p", [NSEQ, N_META + SEQ, 128])
        kr_p = dout("kr_p", [NSEQ, N_META + SEQ, 32])
        wkv_p = dout("wkv_p", [NSEQ, 8, 64, 64])
        sh_p = dout("sh_p", [NSEQ, 1792])
        ckv_s = dout("ckv_s", [64, 128])
        kr_s = dout("kr_s", [64, 32])
        wkv_s = dout("wkv_s", [8, 64, 64])
        sh_s = dout("sh_s", [1792])

        scr_f = [nc.dram_tensor("scr_f%d" % i, [NFC, 128, 3072], BF16, kind="Internal").ap() for i in range(2)]
        scr_in = nc.dram_tensor("scr_in", [18, 128, 1024], BF16, kind="Internal").ap()
        scr_out = nc.dram_tensor("scr_out", [16, 64, 1024], BF16, kind="Internal").ap()

        ident = sb("ident", [128, 128])
        ident_bf = sb("ident_bf", [128, 128], BF16)
        ones_bf = sb("ones_bf", [128, 128], BF16)
        masks = sb("masks", [64, 3, 64], BF16)
        tri = sb("tri", [64, 2, 64])
        bc = sb("bc", [64, 3, 512])
        gains = sb("gains", [128, 4, 8])
        qn_g = sb("qn_g", [128, 2])
        kvn_g = sb("kvn_g", [128, 1])
        mu_t = sb("mu_t", [128, 26])
        hv = sb("hv", [64, 5, 8])
        rk_bf = sb("rk_bf", [64, 8], BF16)
        eps_c = sb("eps_c", [128, 1])
        w_w2_bf = sb("w_w2_bf", [64, 512], BF16)
        w_a2_bf = sb("w_a2_bf", [128, 512], BF16)
        w_g2_bf = sb("w_g2_bf", [128, 512], BF16)
        wq_bf = sb("wq_bf", [128, 2, 768], BF16)
        wqrot_bf = sb("wqrot_bf", [128, 2, 8, 32], BF16)
        wukT_bf = sb("wukT_bf", [64, 8, 128], BF16)
        wuv_bf = sb("wuv_bf", [128, 512], BF16)

        cT_c = sb("cT_c", [128, 4096], BF16)
        krT_c = sb("krT_c", [32, 4096], BF16)
        ctok_c = sb("ctok_c", [128, 32, 128], BF16)
        cT_m = sb("cT_m", [128, 16], BF16)
        krT_m = sb("krT_m", [32, 16], BF16)
        ctok_m = sb("ctok_m", [16, 128], BF16)
        ctok_d = sb("ctok_d", [64, 4, 128], BF16)

        xT = sb("xT", [128, 8, NT])
        hT = sb("hT", [128, 8, NT], BF16)
        xin = sb("xin", [128, 2, 1024])
        yout = sb("yout", [128, 1024])
        sq_p = [sb("sq%d" % i, [128, NT], BF16) for i in range(2)]
        rstd = sb("rstd", [128, NT])
        sa_p = [sb("sa%d" % i, [128, NT]) for i in range(2)]
        g_p = [sb("g%d" % i, [128, NT], BF16) for i in range(3)]
        w13_p = [sb("w13_%d" % i, [128, 2048], BF16) for i in range(3)]
        w2_p = [sb("w2_%d" % i, [128, 1024], BF16) for i in range(3)]
        wi_p = [sb("wi%d" % i, [128, 8, 128], BF16) for i in range(3)]
        wo_p = [sb("wo%d" % i, [64, 8, 128], BF16) for i in range(3)]
        pst_p = [sb("pst%d" % i, [128, 2, NT + 1]) for i in range(2)]
        dmix = sb("dmix", [128, 2, NT])
        ks_t = sb("ks_t", [64, 2, NT])
        carry = sb("carry", [128, 26])
        carry_m = sb("carry_m", [128, 26])
        tw = sb("tw", [64, NT], BF16)
        als = sb("als", [128, NT], BF16)
        sgl = sb("sgl", [128, NT], BF16)
        r_t = sb("r_t", [64, 8, NT], BF16)
        kp_t = sb("kp_t", [64, 8, NT], BF16)
        kk_t = sb("kk_t", [64, 8, NT], BF16)
        b_t = sb("b_t", [64, 8, NT], BF16)
        v_t = sb("v_t", [64, 8, NT], BF16)
        a_t = sb("a_t", [64, 8, NT], BF16)
        sg = sb("sg", [64, 512])
        gam = sb("gam", [64, 8, 64])
        ginv = sb("ginv", [64, 8, 64])
        gprev = sb("gprev", [64, 8, 64])
        tmpA = gam[:].rearrange("p a b -> p (a b)")
        tmpB = ginv[:].rearrange("p a b -> p (a b)")
        tmpC = gprev[:].rearrange("p a b -> p (a b)")
        rT = sb("rT", [64, 8, 64], BF16)
        kT = sb("kT", [64, 8, 64], BF16)
        bT = sb("bT", [64, 8, 64], BF16)
        aT = sb("aT", [64, 8, 64], BF16)
        prodT = sb("prodT", [64, 8, 64], BF16)
        bt = sb("bt", [64, 512], BF16)
        kt = sb("kt", [64, 512], BF16)
        vt = sb("vt", [64, 512], BF16)
        AKT = sb("AKT", [64, 512], BF16)
        RBT = sb("RBT", [64, 512], BF16)
        RKT = sb("RKT", [64, 512], BF16)
        Pm = [sb("Pm%d" % i, [64, 512], BF16) for i in range(2)]
        Qm = [sb("Qm%d" % i, [64, 512], BF16) for i in range(2)]
        TT = [sb("TT%d" % i, [64, 512], BF16) for i in range(2)]
        Xb = sb("Xb", [64, 512], BF16)
        Ub = sb("Ub", [64, 512], BF16)
        S32 = sb("S32", [64, 8, 64])
        S32m = sb("S32m", [64, 8, 64])
        Sbf = sb("Sbf", [64, 8, 64], BF16)
        ytm = sb("ytm", [64, 8, 64])
        ysq = sb("ysq", [64, 8, 64])
        st8 = sb("st8", [64, 6, 8])
        rwo = sb("rwo", [64, 512], BF16)
        rwoT = sb("rwoT", [64, 8, NT], BF16)
        mlaT = sb("mlaT", [64, 8, NT], BF16)
        cq32 = sb("cq32", [128, 2, NT])
        cqn = sb("cqn", [128, 2, NT], BF16)
        c32 = sb("c32", [128, NT])
        kr32 = sb("kr32", [32, NT])
        tabs = sb("tabs", [32, 4, NT])
        qn_p = [sb("qn%d" % i, [64, NT], BF16) for i in range(2)]
        qlatT = sb("qlatT", [128, 8, NT], BF16)
        qrT = sb("qrT", [32, 8, NT], BF16)
        qt1 = sb("qt1", [32, NT])
        qt2 = sb("qt2", [32, NT])
        cst = sb("cst", [128, 2, 128])
        krst = sb("krst", [128, 2, 32])
        PT_p = [sb("PT%d" % i, [128, 2, NT], BF16) for i in range(3)]
        latn_p = [sb("latn%d" % i, [128, NT], BF16) for i in range(2)]

        psL = [es.enter_context(nc.psum_tensor("psL%d" % i, [128, 512], F32)) for i in range(4)]
        psR = [es.enter_context(nc.psum_tensor("psR%d" % i, [128, 512], F32)) for i in range(4)]
        Rn = RR([(psR[i], "psR%d" % i) for i in range(3)])
        Ln2 = RR([(psL[2], "psL2"), (psL[3], "psL3"), (psR[3], "psR3")])
        Rn4 = RR([(psR[i], "psR%d" % i) for i in range(4)])

        def PE(fn, r, w, inc=True):
            return S.op('pe', fn, r, w, inc)

        def ACT(fn, r, w):
            return S.op('act', fn, r, w)

        def DVE(fn, r, w):
            return S.op('dve', fn, r, w)

        def POOL(fn, r, w):
            return S.op('pool', fn, r, w)

        ew_rr = RR(['act', 'dve'])

        def EV(out, in_, r, w):
            if ew_rr() == 'act':
                return ACT(lambda h: h.activation(out=out, in_=in_, func=AF.Copy), r, w)
            return DVE(lambda h: h.tensor_copy(out=out, in_=in_), r, w)

        def ckpt(n):
            if stop <= n:
                raise StopBuild()

        try:
            S.dma('sp', 'ld_c', [(ident[:], ident_d)], writes=['ident'])
            ACT(lambda h: h.activation(out=ident_bf[:], in_=ident[:], func=AF.Copy), ['ident'], ['ident_bf'])
            POOL(lambda h: h.memset(ones_bf[:], 1.0), [], ['ones_bf'])
            POOL(lambda h: h.memset(eps_c[:], NORM_EPS), [], ['eps_c'])
            S.dma('sp', 'ld_c', [(masks[:, m, :], masks_d[m]) for m in range(3)], writes=['masks'])
            S.dma('sp', 'ld_c', [(tri[:, m, :], tri_d[m]) for m in range(2)], writes=['tri'])
            for i, nme in enumerate(["w0", "ln_x_w", "ln_x_b"]):
                S.dma('sp', 'ld_c', [(bc[:, i, :], W[nme].partition_broadcast(64))], writes=['bc'])
            for i, nme in enumerate(["norm_ffn1", "norm_mix", "norm_ffn2", "final_norm"]):
                S.dma('sp', 'ld_c', [(gains[:, i, :], W[nme].rearrange("(c p) -> p c", p=128))], writes=['gains'],
                      allow_slow_non_contiguous=True)
            S.dma('sp', 'ld_c', [(qn_g[:], W["q_norm"].rearrange("(c p) -> p c", p=128)),
                                 (kvn_g[:], W["kv_norm"].rearrange("(c p) -> p c", p=128))], writes=['qn_g', 'kvn_g'],
                  allow_slow_non_contiguous=True)
            S.dma('sp', 'ld_c', [(mu_t[0:64, 0:24], W["mu_shift"][0:1536].rearrange("(c p) -> p c", p=64)),
                                 (mu_t[:, 24:26], W["mu_shift"][1536:1792].rearrange("(c p) -> p c", p=128))],
                  writes=['mu_t'], allow_slow_non_contiguous=True)
            for i, nme in enumerate(["a0", "k_k", "k_a", "r_k"]):
                S.dma('sp', 'ld_c', [(hv[:, (i if i < 3 else 4), :], W[nme].rearrange("(c p) -> p c", p=64))], writes=['hv'],
                      allow_slow_non_contiguous=True)
            DVE(lambda h: h.tensor_scalar(out=hv[:, 3, :], in0=hv[:, 2, :], scalar1=-1.0, scalar2=1.0, op0=ALU.mult, op1=ALU.add),
                ['hv'], ['hv'])
            DVE(lambda h: h.tensor_copy(out=rk_bf[:], in_=hv[:, 4, :]), ['hv'], ['rk_bf'])

            def small_w(dst_ap, src_ap, rows, cols, slot, r0=0):
                S.dma('sp', 'ld_c', [(xin[r0:r0 + rows, slot, 0:cols], src_ap)], writes=['xin%d' % slot])
                DVE(lambda h: h.tensor_copy(out=dst_ap, in_=xin[r0:r0 + rows, slot, 0:cols]), ['xin%d' % slot], ['smallw'])

            small_w(w_w2_bf[:], W["w_w2"], 64, 512, 0)
            small_w(w_a2_bf[64:128, :], W["w_a2"], 64, 512, 1, r0=64)
            small_w(w_g2_bf[:], W["w_g2"], 128, 512, 0)
            small_w(wuv_bf[:], W["w_uv"], 128, 512, 1)
            for j in range(2):
                small_w(wq_bf[:, j, :], W["w_q_up"][j * 128:(j + 1) * 128, :], 128, 768, j)
            for j in range(2):
                for hh in range(8):
                    c0 = hh * 96 + 64
                    DVE(lambda h, j=j, hh=hh, c0=c0: h.tensor_scalar(out=wqrot_bf[:, j, hh, 0:16], in0=wq_bf[:, j, c0 + 16:c0 + 32],
                                                                      scalar1=-1.0, scalar2=None, op0=ALU.mult),
                        ['smallw'], ['smallw2'])
                    DVE(lambda h, j=j, hh=hh, c0=c0: h.tensor_copy(out=wqrot_bf[:, j, hh, 16:32], in_=wq_bf[:, j, c0:c0 + 16]),
                        ['smallw'], ['smallw2'])
            S.dma('sp', 'ld_c', [(xin[:, 0, 0:512], W["w_uk"])], writes=['xin0'])
            for hh in range(8):
                ps, pk = Rn()
                PE(lambda h, hh=hh, ps=ps: h.transpose(ps[0:64, 0:128], xin[:, 0, hh * 64:(hh + 1) * 64], ident[:]),
                   ['xin0', 'ident'], [pk])
                ACT(lambda h, hh=hh, ps=ps: h.activation(out=wukT_bf[:, hh, :], in_=ps[0:64, 0:128], func=AF.Copy), [pk], ['smallw'])

            ckpt(0)
            cv_rr = RR(['act', 'dve', 'pool'])
            yo_bf = yout[:].bitcast(BF16)
            cvi = [0]

            def cast_unit(src3, dst_flat, npart=128, pieces=None):
                sl = cvi[0] % 2
                cvi[0] += 1
                xk = 'xin%d' % sl
                yk = 'yo%d' % sl
                a, b = src3.shape[1], src3.shape[2]
                S.dma('sp', 'cv_ld', [(xin[0:npart, sl, :].rearrange("p (a b) -> p a b", a=a), src3)], writes=[xk])
                eng = cv_rr()
                if pieces is None:
                    fnc = (lambda h: h.tensor_copy(out=yo_bf[0:npart, sl * 1024:(sl + 1) * 1024], in_=xin[0:npart, sl, :])) \
                        if eng != 'act' else \
                        (lambda h: h.activation(out=yo_bf[0:npart, sl * 1024:(sl + 1) * 1024], in_=xin[0:npart, sl, :], func=AF.Copy))
                    S.op(eng, fnc, [xk], [yk])
                else:
                    pieces(sl, xk, yk)
                S.dma('pool', 'cv_st', [(dst_flat, yo_bf[0:npart, sl * 1024:(sl + 1) * 1024])], reads=[yk], writes=['scr'])

            for fi, (n1, n3, n2) in enumerate([("ffn1_w1", "ffn1_w3", "ffn1_w2"), ("ffn2_w1", "ffn2_w3", "ffn2_w2")]):
                for fc in range(NFC):
                    for wi, nme in enumerate([n1, n3]):
                        cast_unit(W[nme][:, fc * 128:(fc + 1) * 128].rearrange("(k p) f -> p k f", p=128),
                                  scr_f[fi][fc, :, wi * 1024:(wi + 1) * 1024])
                    cast_unit(W[n2][fc * 128:(fc + 1) * 128, :].rearrange("p (a b) -> p a b", a=8),
                              scr_f[fi][fc, :, 2048:3072])
            for g in range(17):
                cast_unit(W["w_in"][:, g * 128:(g + 1) * 128].rearrange("(k p) f -> p k f", p=128), scr_in[g])

            def rope_pieces(sl, xk, yk):
                xv = xin[:, sl, :].rearrange("p (k f) -> p k f", k=8)
                yv = yo_bf[:, sl * 1024:(sl + 1) * 1024].rearrange("p (k f) -> p k f", k=8)
                POOL(lambda h: h.memset(yo_bf[:, sl * 1024:(sl + 1) * 1024], 0.0), [], [yk])
                DVE(lambda h: h.tensor_copy(out=yv[:, :, 0:32], in_=xv[:, :, 0:32]), [xk], [yk])
                DVE(lambda h: h.tensor_scalar(out=yv[:, :, 32:48], in0=xv[:, :, 16:32], scalar1=-1.0, scalar2=None, op0=ALU.mult), [xk], [yk])
                DVE(lambda h: h.tensor_copy(out=yv[:, :, 48:64], in_=xv[:, :, 0:16]), [xk], [yk])

            sl = cvi[0] % 2
            POOL(lambda h, sl=sl: h.memset(xin[:, sl, :], 0.0), [], ['xin%d' % sl])
            cvi[0] += 1
            xk = 'xin%d' % sl
            yk = 'yo%d' % sl
            S.dma('sp', 'cv_ld', [(xin[:, sl, :].rearrange("p (k f) -> p k f", k=8)[:, :, 0:32],
                                   W["w_in"][:, 2176:2208].rearrange("(k p) f -> p k f", p=128))], writes=[xk])
            rope_pieces(sl, xk, yk)
            S.dma('pool', 'cv_st', [(scr_in[17], yo_bf[:, sl * 1024:(sl + 1) * 1024])], reads=[yk], writes=['scr'])
            for dch in range(8):
                for half in range(2):
                    cast_unit(W["w_out"][half * 512:(half + 1) * 512, dch * 128:(dch + 1) * 128].rearrange("(g p) m -> p g m", p=64),
                              scr_out[dch * 2 + half], npart=64)

            ckpt(1)
            def rmsnorm_to_hT(gi, ntok):
                ps, pk = Rn4()
                for dch in range(8):
                    sq = sq_p[dch % 2]
                    sk = 'sq%d' % (dch % 2)
                    ACT(lambda h, dch=dch, sq=sq: h.activation(out=sq[:, :ntok], in_=xT[:, dch, :ntok], func=AF.Square), ['xT%d' % dch], [sk])
                    PE(lambda h, dch=dch, sq=sq, ps=ps: h.matmul(ps[:, :ntok], lhsT=ones_bf[:], rhs=sq[:, :ntok], start=(dch == 0), stop=(dch == 7)),
                       [sk, 'ones_bf'], [pk], inc=(dch == 7))
                rstd_from(ps, pk, ntok, 128, 1.0 / D)
                for dch in range(8):
                    if True:
                        DVE(lambda h, dch=dch: h.scalar_tensor_tensor(out=hT[:, dch, :ntok], in0=xT[:, dch, :ntok], scalar=gains[:, gi, dch:dch + 1],
                                                                      in1=rstd[:, :ntok], op0=ALU.mult, op1=ALU.mult), ['xT%d' % dch, 'rstd', 'gains'], ['hT%d' % dch])
                    else:
                        tb = sa_p[dch % 2]
                        tk = 'sa%d' % (dch % 2)
                        POOL(lambda h, dch=dch, tb=tb: h.tensor_tensor(out=tb[:, :ntok], in0=xT[:, dch, :ntok], in1=rstd[:, :ntok], op=ALU.mult),
                             ['xT%d' % dch, 'rstd'], [tk])
                        POOL(lambda h, dch=dch, tb=tb: h.tensor_scalar(out=hT[:, dch, :ntok], in0=tb[:, :ntok], scalar1=gains[:, gi, dch:dch + 1], scalar2=None,
                                                                        op0=ALU.mult), [tk, 'gains'], ['hT%d' % dch])

            def rstd_from(ps, pk, ntok, npart, inv_n):
                ACT(lambda h: h.activation(out=rstd[0:npart, :ntok], in_=ps[0:npart, :ntok], func=AF.Sqrt, bias=eps_c[0:npart, 0:1], scale=inv_n),
                    [pk, 'eps_c'], ['rstd'])
                DVE(lambda h: h.reciprocal(out=rstd[0:npart, :ntok], in_=rstd[0:npart, :ntok]), ['rstd'], ['rstd'])

            wf_i = [0]

            def ffn(fi, ntok):
                base = wf_i[0]
                wf_i[0] += NFC

                def slot(fc):
                    return (base + fc) % 3

                def load(fc):
                    sl = slot(fc)
                    S.dma('sp', 'w13', [(w13_p[sl][:], scr_f[fi][fc, :, 0:2048])], reads=['scr'], writes=['w13_%d' % sl])
                    S.dma('sp', 'w2', [(w2_p[sl][:], scr_f[fi][fc, :, 2048:3072])], reads=['scr'], writes=['w2_%d' % sl])

                def up(fc):
                    sl = slot(fc)
                    wf = w13_p[sl]
                    ps, pk = Rn4()
                    for part in range(2):
                        for k in range(8):
                            PE(lambda h, part=part, k=k, ps=ps, wf=wf: h.matmul(ps[:, part * 256:part * 256 + ntok],
                                                                              lhsT=wf[:, part * 1024 + k * 128: part * 1024 + (k + 1) * 128],
                                                                              rhs=hT[:, k, :ntok], start=(k == 0), stop=(k == 7)),
                               ['w13_%d' % sl, 'hT%d' % k], [pk])
                    sa = sa_p[fc % 2]
                    g = g_p[fc % 3]
                    ACT(lambda h, ps=ps, sa=sa: h.activation(out=sa[:, :ntok], in_=ps[:, 0:ntok], func=AF.Silu), [pk], ['sa%d' % (fc % 2)])
                    DVE(lambda h, ps=ps, sa=sa, g=g: h.tensor_tensor(out=g[:, :ntok], in0=ps[:, 256:256 + ntok], in1=sa[:, :ntok], op=ALU.mult),
                        [pk, 'sa%d' % (fc % 2)], ['g%d' % (fc % 3)])

                def down(fc):
                    sl = slot(fc)
                    wf = w2_p[sl]
                    g = g_p[fc % 3]
                    for dch in range(8):
                        acc = psL[dch // 2]
                        PE(lambda h, dch=dch, acc=acc, wf=wf, g=g: h.matmul(acc[:, (dch % 2) * 256:(dch % 2) * 256 + ntok],
                                                                          lhsT=wf[:, dch * 128:(dch + 1) * 128],
                                                                          rhs=g[:, :ntok], start=(fc == 0 and dch % 2 == 0), stop=(fc == NFC - 1),
                                                                          skip_group_check=True),
                           ['w2_%d' % sl, 'g%d' % (fc % 3)], ['psL%d' % (dch // 2)])

                for fc in range(3):
                    load(fc)
                up(0)
                for fc in range(NFC):
                    if fc + 1 < NFC:
                        up(fc + 1)
                    down(fc)
                    if fc + 3 < NFC:
                        load(fc + 3)
                for dch in range(8):
                    acc = psL[dch // 2]
                    DVE(lambda h, dch=dch, acc=acc: h.scalar_tensor_tensor(out=xT[:, dch, :ntok], in0=acc[:, (dch % 2) * 256:(dch % 2) * 256 + ntok],
                                                                         scalar=0.5, in1=xT[:, dch, :ntok], op0=ALU.mult, op1=ALU.add),
                        ['psL%d' % (dch // 2), 'xT%d' % dch], ['xT%d' % dch])

            wi_i = [0]

            wi_st = {'order': [], 'loaded': 0, 'used': 0, 'slots': []}

            def wi_begin(order):
                wi_st['order'] = list(order)
                wi_st['loaded'] = 0
                wi_st['used'] = 0
                wi_st['slots'] = []

            def load_wi(g):
                st = wi_st
                assert st['order'][st['used']] == g, (st['order'], st['used'], g)
                while st['loaded'] < min(len(st['order']), st['used'] + 3):
                    gg = st['order'][st['loaded']]
                    sl = wi_i[0] % 3
                    wi_i[0] += 1
                    S.dma('sp', 'wi%d' % sl, [(wi_p[sl][:].rearrange("p k f -> p (k f)"), scr_in[gg])], reads=['scr'], writes=['wi%d' % sl])
                    st['slots'].append(sl)
                    st['loaded'] += 1
                sl = st['slots'][st['used']]
                st['used'] += 1
                return sl

            def proj(sl, c0, m, ps, pk, col0, ntok, part0=0):
                wi = wi_p[sl]
                for k in range(8):
                    PE(lambda h, k=k: h.matmul(ps[part0:part0 + m, col0:col0 + ntok], lhsT=wi[:, k, c0:c0 + m], rhs=hT[:, k, :ntok],
                                               start=(k == 0), stop=(k == 7)), ['wi%d' % sl, 'hT%d' % k], [pk], inc=(k == 7))

            pst_i = [0]

            def mix(ps, pk, npart, nsub, cols, ntok, outs):
                i = pst_i[0] % 2
                pst_i[0] += 1
                pst = pst_p[i]
                pk2 = 'pst%d' % i
                psv = ps[0:npart, :].rearrange("p (j t) -> p j t", j=2)[:, 0:nsub, 0:ntok]
                ACT(lambda h: h.activation(out=pst[0:npart, 0:nsub, 1:ntok + 1], in_=psv, func=AF.Copy), [pk], [pk2])
                c0 = cols[0]
                DVE(lambda h: h.tensor_copy(out=pst[0:npart, 0:nsub, 0:1], in_=carry[0:npart, c0:c0 + nsub].unsqueeze(2)), ['carry%d' % c0], [pk2])
                DVE(lambda h: h.tensor_tensor(out=dmix[0:npart, 0:nsub, 0:ntok], in0=pst[0:npart, 0:nsub, 0:ntok],
                                              in1=pst[0:npart, 0:nsub, 1:ntok + 1], op=ALU.subtract), [pk2], ['dmix'])
                for j in range(nsub):
                    o, ok = outs[j]
                    DVE(lambda h, j=j, o=o: h.scalar_tensor_tensor(out=o, in0=dmix[0:npart, j, 0:ntok], scalar=mu_t[0:npart, cols[j]:cols[j] + 1],
                                                                 in1=pst[0:npart, j, 1:ntok + 1], op0=ALU.mult, op1=ALU.add),
                        ['dmix', pk2, 'mu_t'], [ok])
                DVE(lambda h: h.tensor_copy(out=carry[0:npart, c0:c0 + nsub].unsqueeze(2), in_=pst[0:npart, 0:nsub, ntok:ntok + 1]), [pk2], ['carry%d' % c0])

            def tile(ntok, x_src, tab_src, key_off, full_blocks, do_attn, y_dst, c_dst, kr_dst, is_meta=False, nxt=None):
                nch = ntok // 64
                blks = [(t0, min(128, ntok - t0)) for t0 in range(0, ntok, 128)]
                for bi, (t0, n) in enumerate(blks):
                    for q4 in range(2):
                        ps, pk = Rn4()
                        for j in range(4):
                            dch = q4 * 4 + j
                            PE(lambda h, bi=bi, n=n, j=j, dch=dch, ps=ps: h.transpose(ps[:, j * 128:j * 128 + n], xin[0:n, bi, dch * 128:(dch + 1) * 128],
                                                                                 ident[0:n, 0:n]), ['xin%d' % bi, 'ident'], [pk], inc=(j == 3))
                        EV(xT[:, q4 * 4:(q4 + 1) * 4, t0:t0 + n], ps[:, :].rearrange("p (j t) -> p j t", j=4)[:, :, 0:n], [pk], ['xT%d' % d_ for d_ in range(q4 * 4, q4 * 4 + 4)])
                if nxt is not None:
                    nxt()
                S.dma('sp', 'tab', [(tabs[:, i, :ntok], tab_src[i]) for i in range(4)], writes=['tabs'])
                if not is_meta:
                    ckpt(20)
                rmsnorm_to_hT(0, ntok)
                ffn(0, ntok)
                if not is_meta:
                    ckpt(21)
                rmsnorm_to_hT(1, ntok)
                wi_begin([12, 13, 4, 5, 0, 6, 1, 7, 2, 3, 8, 9, 10, 11] + ([14, 15] if do_attn else []) + [16, 17])
                sl = load_wi(12)
                ps, pk = Rn4()
                proj(sl, 0, 128, ps, pk, 0, ntok)
                mix(ps, pk, 128, 1, [24], ntok, [(tmpWA[:, :ntok], 'c32')])
                ACT(lambda h: h.activation(out=tw[:, :ntok], in_=tmpWA[0:64, :ntok], func=AF.Tanh), ['c32'], ['tw'])
                DVE(lambda h: h.tensor_copy(out=als[64:128, :ntok], in_=tmpWA[64:128, :ntok]), ['c32'], ['als'])
                sl2 = load_wi(13)
                ps, pk = Rn4()
                proj(sl2, 0, 128, ps, pk, 0, ntok)
                mix(ps, pk, 128, 1, [25], ntok, [(tmpWA[:, :ntok], 'c32')])
                ACT(lambda h: h.activation(out=sgl[:, :ntok], in_=tmpWA[:, :ntok], func=AF.Sigmoid), ['c32'], ['sgl'])
                for hp in range(4):
                    ps, pk = Rn4()
                    for j in range(2):
                        hh = hp * 2 + j
                        PE(lambda h, hh=hh, j=j, ps=ps: h.matmul(ps[0:64, j * 256:j * 256 + ntok], lhsT=w_a2_bf[64:128, hh * 64:(hh + 1) * 64],
                                                               rhs=als[64:128, :ntok], start=True, stop=True), ['als', 'smallw'], [pk], inc=(j == 1))
                    for j in range(2):
                        hh = hp * 2 + j
                        ACT(lambda h, hh=hh, j=j, ps=ps: h.activation(out=a_t[:, hh, :ntok], in_=ps[0:64, j * 256:j * 256 + ntok], func=AF.Sigmoid,
                                                                    bias=hv[:, 0, hh:hh + 1], scale=1.0), [pk, 'hv'], ['a_t'])
                def rv_group(g):
                    sl = load_wi(g)
                    ps, pk = Rn4()
                    dst = r_t if g < 4 else v_t
                    dk = 'r_t' if g < 4 else 'v_t'
                    for j in range(2):
                        proj(sl, j * 64, 64, ps, pk, j * 256, ntok)
                    hp = g % 4
                    mix(ps, pk, 64, 2, [2 * g, 2 * g + 1], ntok, [(dst[:, hp * 2 + j, :ntok], dk) for j in range(2)])

                tA3 = gam[:].rearrange("p a b -> p (a b)").rearrange("p (j t) -> p j t", j=2)
                tB3 = ginv[:].rearrange("p a b -> p (a b)").rearrange("p (j t) -> p j t", j=2)
                tC3 = gprev[:].rearrange("p a b -> p (a b)").rearrange("p (j t) -> p j t", j=2)
                sq3 = Xb[:].rearrange("p (j t) -> p j t", j=2)
                kbufs = [(ks_t, 'ks_t'), (ysq[:].rearrange("p a b -> p (a b)").rearrange("p (j t) -> p j t", j=2), 'ysq')]

                def bcs(col, hp):
                    return hv[:, col, 2 * hp:2 * hp + 2].unsqueeze(2).to_broadcast([64, 2, ntok])

                def k_s1(g):
                    kb, kk_ = kbufs[g % 2]
                    sl = load_wi(g)
                    ps, pk = Rn4()
                    for j in range(2):
                        proj(sl, j * 64, 64, ps, pk, j * 256, ntok)
                    mix(ps, pk, 64, 2, [2 * g, 2 * g + 1], ntok, [(kb[:, j, :ntok], kk_) for j in range(2)])

                def k_s2(g):
                    kb, kk_ = kbufs[g % 2]
                    hp = g % 4
                    DVE(lambda h: h.tensor_tensor(out=tA3[:, :, :ntok], in0=kb[:, :, :ntok], in1=bcs(1, hp), op=ALU.mult), [kk_, 'hv'], ['gam'])
                    ACT(lambda h: h.activation(out=sq3[:, :, :ntok], in_=tA3[:, :, :ntok], func=AF.Square), ['gam'], ['Xb'])

                def k_s3(g):
                    kb, kk_ = kbufs[g % 2]
                    hp = g % 4
                    ps2, pk2 = Rn4()
                    if ntok == 256:
                        PE(lambda h: h.matmul(ps2[0:64, :], lhsT=ones_bf[0:64, 0:64], rhs=Xb[:], start=True, stop=True), ['Xb', 'ones_bf'], [pk2])
                    else:
                        for j in range(2):
                            PE(lambda h, j=j: h.matmul(ps2[0:64, j * 256:j * 256 + ntok], lhsT=ones_bf[0:64, 0:64], rhs=sq3[:, j, :ntok], start=True, stop=True),
                               ['Xb', 'ones_bf'], [pk2])
                    p3 = ps2[0:64, :].rearrange("p (j t) -> p j t", j=2)[:, :, :ntok]
                    ACT(lambda h: h.activation(out=tB3[:, :, :ntok], in_=p3, func=AF.Sqrt), [pk2], ['ginv'])
                    DVE(lambda h: h.tensor_scalar(out=tB3[:, :, :ntok], in0=tB3[:, :, :ntok], scalar1=1e-12, scalar2=None, op0=ALU.max), ['ginv'], ['ginv'])
                    DVE(lambda h: h.reciprocal(out=tB3[:, :, :ntok], in_=tB3[:, :, :ntok]), ['ginv'], ['ginv'])
                    DVE(lambda h: h.tensor_tensor(out=kk_t[:, 2 * hp:2 * hp + 2, :ntok], in0=tA3[:, :, :ntok], in1=tB3[:, :, :ntok], op=ALU.mult),
                        ['gam', 'ginv'], ['kk_t'])
                    POOL(lambda h: h.tensor_tensor(out=b_t[:, 2 * hp:2 * hp + 2, :ntok], in0=kk_t[:, 2 * hp:2 * hp + 2, :ntok], in1=a_t[:, 2 * hp:2 * hp + 2, :ntok], op=ALU.mult),
                         ['kk_t', 'a_t'], ['b_t'])
                    DVE(lambda h: h.tensor_tensor(out=tC3[:, :, :ntok], in0=a_t[:, 2 * hp:2 * hp + 2, :ntok], in1=bcs(2, hp), op=ALU.mult), ['a_t', 'hv'], ['gprev'])
                    DVE(lambda h: h.tensor_tensor(out=tC3[:, :, :ntok], in0=tC3[:, :, :ntok], in1=bcs(3, hp), op=ALU.add), ['gprev', 'hv'], ['gprev'])
                    DVE(lambda h: h.tensor_tensor(out=kp_t[:, 2 * hp:2 * hp + 2, :ntok], in0=kb[:, :, :ntok], in1=tC3[:, :, :ntok], op=ALU.mult),
                        [kk_, 'gprev'], ['kp_t'])

                k_s1(4)
                k_s2(4)
                k_s1(5)
                rv_group(0)
                k_s3(4)
                k_s2(5)
                k_s1(6)
                rv_group(1)
                k_s3(5)
                k_s2(6)
                k_s1(7)
                rv_group(2)
                k_s3(6)
                k_s2(7)
                rv_group(3)
                k_s3(7)
                for g in range(8, 12):
                    rv_group(g)
                if not is_meta:
                    ckpt(22)
                if do_attn:
                    sl = load_wi(14)
                    ps, pk = Rn4()
                    proj(sl, 0, 128, ps, pk, 0, ntok)
                    sl2 = load_wi(15)
                    proj(sl2, 0, 128, ps, pk, 256, ntok)
                    ACT(lambda h, ps=ps: h.activation(out=cq32[:, :, :ntok], in_=ps[:, :].rearrange("p (j t) -> p j t", j=2)[:, :, 0:ntok], func=AF.Copy),
                        [pk], ['cq32'])
                    ps2, pk2 = Rn4()
                    for j in range(2):
                        ACT(lambda h, j=j: h.activation(out=sq_p[j][:, :ntok], in_=cq32[:, j, :ntok], func=AF.Square), ['cq32'], ['sq%d' % j])
                        PE(lambda h, j=j, ps2=ps2: h.matmul(ps2[:, :ntok], lhsT=ones_bf[:], rhs=sq_p[j][:, :ntok], start=(j == 0), stop=(j == 1)),
                           ['sq%d' % j, 'ones_bf'], [pk2], inc=(j == 1))
                    rstd_from(ps2, pk2, ntok, 128, 1.0 / 256)
                    for j in range(2):
                        DVE(lambda h, j=j: h.scalar_tensor_tensor(out=cqn[:, j, :ntok], in0=cq32[:, j, :ntok], scalar=qn_g[:, j:j + 1], in1=rstd[:, :ntok],
                                                                  op0=ALU.mult, op1=ALU.mult), ['cq32', 'rstd', 'qn_g'], ['cqn'])
                if not is_meta:
                    ckpt(22.1)
                sl = load_wi(16)
                ps, pk = Rn4()
                proj(sl, 0, 128, ps, pk, 0, ntok)
                sl2 = load_wi(17)
                ACT(lambda h, ps=ps: h.activation(out=c32[:, :ntok], in_=ps[:, 0:ntok], func=AF.Copy), [pk], ['c32'])
                ACT(lambda h: h.activation(out=sq_p[0][:, :ntok], in_=c32[:, :ntok], func=AF.Square), ['c32'], ['sq0'])
                ps2, pk2 = Rn4()
                PE(lambda h, ps2=ps2: h.matmul(ps2[:, :ntok], lhsT=ones_bf[:], rhs=sq_p[0][:, :ntok], start=True, stop=True), ['sq0', 'ones_bf'], [pk2])
                rstd_from(ps2, pk2, ntok, 128, 1.0 / 128)
                DVE(lambda h: h.scalar_tensor_tensor(out=c32[:, :ntok], in0=c32[:, :ntok], scalar=kvn_g[:, 0:1], in1=rstd[:, :ntok],
                                                     op0=ALU.mult, op1=ALU.mult), ['c32', 'rstd', 'kvn_g'], ['c32'])
                if is_meta:
                    ACT(lambda h: h.activation(out=cT_m[:], in_=c32[:, 48:64], func=AF.Copy), ['c32'], ['cT_m'])
                else:
                    ACT(lambda h: h.activation(out=cT_c[:, key_off:key_off + ntok], in_=c32[:, :ntok], func=AF.Copy), ['c32'], ['cT_c'])
                if not is_meta:
                    ckpt(22.2)
                ps, pk = Rn4()
                proj(sl2, 0, 32, ps, pk, 0, ntok)
                proj(sl2, 32, 32, ps, pk, 256, ntok)
                DVE(lambda h, ps=ps: h.tensor_tensor(out=qt1[:, :ntok], in0=ps[0:32, 0:ntok], in1=tabs[:, 0, :ntok], op=ALU.mult), [pk, 'tabs'], ['qt1'])
                DVE(lambda h, ps=ps: h.tensor_tensor(out=kr32[:, :ntok], in0=ps[0:32, 256:256 + ntok], in1=tabs[:, 1, :ntok], op=ALU.mult), [pk, 'tabs'], ['kr32'])
                DVE(lambda h: h.tensor_tensor(out=kr32[:, :ntok], in0=kr32[:, :ntok], in1=qt1[:, :ntok], op=ALU.add), ['kr32', 'qt1'], ['kr32'])
                if is_meta:
                    ACT(lambda h: h.activation(out=krT_m[:], in_=kr32[:, 48:64], func=AF.Copy), ['kr32'], ['krT_m'])
                else:
                    ACT(lambda h: h.activation(out=krT_c[:, key_off:key_off + ntok], in_=kr32[:, :ntok], func=AF.Copy), ['kr32'], ['krT_c'])
                if not is_meta:
                    ckpt(22.3)
                for bi, (t0, n) in enumerate(blks):
                    ps, pk = Rn4()
                    PE(lambda h, t0=t0, n=n, ps=ps: h.transpose(ps[0:n, 0:128], c32[:, t0:t0 + n], ident[:]), ['c32', 'ident'], [pk], inc=False)
                    PE(lambda h, t0=t0, n=n, ps=ps: h.transpose(ps[0:n, 128:160], kr32[:, t0:t0 + n], ident[0:32, 0:32]), ['kr32', 'ident'], [pk])
                    ACT(lambda h, bi=bi, n=n, ps=ps: h.activation(out=cst[0:n, bi, :], in_=ps[0:n, 0:128], func=AF.Copy), [pk], ['cst'])
                    DVE(lambda h, bi=bi, n=n, ps=ps: h.tensor_copy(out=krst[0:n, bi, :], in_=ps[0:n, 128:160]), [pk], ['krst'])
                    if (not is_meta) and n == 128:
                        DVE(lambda h, bi=bi, ps=ps: h.tensor_copy(out=ctok_c[:, (key_off + bi * 128) // 128, :], in_=ps[:, 0:128]), [pk], ['ctok_c'])
                if not is_meta:
                    ckpt(22.4)
                if is_meta:
                    ps, pk = Rn4()
                    PE(lambda h, ps=ps: h.matmul(ps[0:16, 0:128], lhsT=cT_m[:], rhs=ident_bf[:], start=True, stop=True), ['cT_m', 'ident_bf'], [pk])
                    ACT(lambda h, ps=ps: h.activation(out=ctok_m[:], in_=ps[0:16, 0:128], func=AF.Copy), [pk], ['ctok_m'])
                    for si in range(NSEQ):
                        S.dma('pool', 'st_c', [(ckv_p[si, 0:16, :], cst[48:64, 0, :]), (kr_p[si, 0:16, :], krst[48:64, 0, :])],
                              reads=['cst', 'krst'], writes=['o_ckv'])
                else:
                    for bi, (t0, n) in enumerate(blks):
                        S.dma('pool', 'st_c', [(c_dst[t0:t0 + n, :], cst[0:n, bi, :]), (kr_dst[t0:t0 + n, :], krst[0:n, bi, :])],
                              reads=['cst', 'krst'], writes=['o_ckv'])
                    ckpt(22.5)
                    for cch in range(nch):
                        ps, pk = Rn4()
                        PE(lambda h, cch=cch, ps=ps: h.matmul(ps[0:64, 0:128], lhsT=cT_c[:, key_off + cch * 64:key_off + (cch + 1) * 64],
                                                            rhs=ident_bf[:], start=True, stop=True), ['cT_c', 'ident_bf'], [pk])
                        ACT(lambda h, cch=cch, ps=ps: h.activation(out=ctok_d[:, cch, :], in_=ps[0:64, 0:128], func=AF.Copy), [pk], ['ctok_d'])

                if not is_meta:
                    ckpt(23)
                def rwkv_thread():
                    for cch in range(nch):
                        yield from rwkv_chunk(cch, ntok, need_y=do_attn)
                    if rw_tail[0] is not None:
                        rw_tail[0]()
                        rw_tail[0] = None

                if not do_attn:
                    for _ in rwkv_thread():
                        pass
                    return
                ckpt(40)
                for hh in range(8):
                    psn, pkn = Rn4()
                    for j in range(2):
                        PE(lambda h, j=j, hh=hh, psn=psn: h.matmul(psn[0:64, :ntok], lhsT=wq_bf[:, j, hh * 96:hh * 96 + 64], rhs=cqn[:, j, :ntok],
                                                                 start=(j == 0), stop=(j == 1)), ['cqn', 'smallw'], [pkn], inc=(j == 1))
                    psr, pkr = Rn4()
                    for j in range(2):
                        PE(lambda h, j=j, hh=hh, psr=psr: h.matmul(psr[0:32, 0:ntok], lhsT=wq_bf[:, j, hh * 96 + 64:hh * 96 + 96], rhs=cqn[:, j, :ntok],
                                                                 start=(j == 0), stop=(j == 1)), ['cqn', 'smallw'], [pkr], inc=False)
                    for j in range(2):
                        PE(lambda h, j=j, hh=hh, psr=psr: h.matmul(psr[0:32, 256:256 + ntok], lhsT=wqrot_bf[:, j, hh, :], rhs=cqn[:, j, :ntok],
                                                                 start=(j == 0), stop=(j == 1)), ['cqn', 'smallw2'], [pkr], inc=(j == 1))
                    qn = qn_p[hh % 2]
                    qk = 'qn%d' % (hh % 2)
                    ACT(lambda h, psn=psn, qn=qn: h.activation(out=qn[:, :ntok], in_=psn[0:64, :ntok], func=AF.Copy), [pkn], [qk])
                    psl, pkl = Rn4()
                    PE(lambda h, hh=hh, psl=psl, qn=qn: h.matmul(psl[:, :ntok], lhsT=wukT_bf[:, hh, :], rhs=qn[:, :ntok], start=True, stop=True),
                       [qk, 'smallw'], [pkl])
                    ACT(lambda h, hh=hh, psl=psl: h.activation(out=qlatT[:, hh, :ntok], in_=psl[:, :ntok], func=AF.Copy, scale=MLA_SCALE), [pkl], ['qlatT'])
                    DVE(lambda h, psr=psr: h.tensor_tensor(out=qt1[:, :ntok], in0=psr[0:32, 0:ntok], in1=tabs[:, 2, :ntok], op=ALU.mult), [pkr, 'tabs'], ['qt1'])
                    DVE(lambda h, psr=psr: h.tensor_tensor(out=qt2[:, :ntok], in0=psr[0:32, 256:256 + ntok], in1=tabs[:, 3, :ntok], op=ALU.mult), [pkr, 'tabs'], ['qt2'])
                    DVE(lambda h, hh=hh: h.tensor_tensor(out=qrT[:, hh, :ntok], in0=qt1[:, :ntok], in1=qt2[:, :ntok], op=ALU.add), ['qt1', 'qt2'], ['qrT'])
                ckpt(41)
                blocks = [(cT_m[:], krT_m[:], ctok_m[:], 16, 0, ['cT_m', 'krT_m', 'ctok_m'])]
                for (a1, a2, a3, nk) in full_blocks:
                    blocks.append((a1, a2, a3, nk, 0, ['cT_c', 'krT_c', 'ctok_c']))
                for cch in range(nch):
                    blocks.append((cT_c[:, key_off + cch * 64:key_off + (cch + 1) * 64], krT_c[:, key_off + cch * 64:key_off + (cch + 1) * 64],
                                   ctok_d[:, cch, :], 64, cch * 64, ['cT_c', 'krT_c', 'ctok_d']))
                pt_i = [0]
                v2 = lambda t, np_: t[0:np_, :].rearrange("p (j t) -> p j t", j=2)

                def score(hp, bi, pend):
                    a1, a2, a3, nk, q0, keys = blocks[bi]
                    ps, pk = Ln2()
                    o = v2(ps, nk)[:, :, q0:ntok]
                    if q0 == 0 and ntok == 256:
                        PE(lambda h: h.matmul(ps[0:nk, :], lhsT=a1, rhs=qlatT[:, 2 * hp:2 * hp + 2, :].rearrange("p a b -> p (a b)"), start=True, stop=False),
                           keys[0:1] + ['qlatT'], [pk])
                        PE(lambda h: h.matmul(ps[0:nk, :], lhsT=a2, rhs=qrT[:, 2 * hp:2 * hp + 2, :].rearrange("p a b -> p (a b)"), start=False, stop=True),
                           keys[1:2] + ['qrT'], [pk])
                    else:
                        for j in range(2):
                            oj = ps[0:nk, j * 256 + q0:j * 256 + ntok]
                            PE(lambda h, j=j, oj=oj: h.matmul(oj, lhsT=a1, rhs=qlatT[:, 2 * hp + j, q0:ntok], start=True, stop=False),
                               keys[0:1] + ['qlatT'], [pk])
                            PE(lambda h, j=j, oj=oj: h.matmul(oj, lhsT=a2, rhs=qrT[:, 2 * hp + j, q0:ntok], start=False, stop=True),
                               keys[1:2] + ['qrT'], [pk])
                    pi = pt_i[0] % 3
                    pt_i[0] += 1
                    PT = PT_p[pi]
                    ACT(lambda h: h.activation(out=PT[0:nk, :, q0:ntok], in_=o, func=AF.Exp), [pk], ['PT%d' % pi])
                    pend.append((bi, pi))

                def pv(hp, pend, nb):
                    bi, pi = pend.pop(0)
                    a1, a2, a3, nk, q0, keys = blocks[bi]
                    PT = PT_p[pi]
                    if q0 == 0 and ntok == 256:
                        PE(lambda h: h.matmul(psL[0][:, :], lhsT=a3, rhs=PT[0:nk, :, :].rearrange("p a b -> p (a b)"), start=(bi == 0), stop=(bi == nb - 1),
                                              skip_group_check=True), keys[2:3] + ['PT%d' % pi], ['psL0'])
                        PE(lambda h: h.matmul(psL[1][:, :], lhsT=ones_bf[0:nk, :], rhs=PT[0:nk, :, :].rearrange("p a b -> p (a b)"), start=(bi == 0), stop=(bi == nb - 1),
                                              skip_group_check=True), ['ones_bf', 'PT%d' % pi], ['psL1'])
                    else:
                        for j in range(2):
                            PE(lambda h, j=j: h.matmul(psL[0][:, j * 256 + q0:j * 256 + ntok], lhsT=a3, rhs=PT[0:nk, j, q0:ntok], start=(bi == 0 and j == 0), stop=(bi == nb - 1),
                                                       skip_group_check=True), keys[2:3] + ['PT%d' % pi], ['psL0'])
                            PE(lambda h, j=j: h.matmul(psL[1][:, j * 256 + q0:j * 256 + ntok], lhsT=ones_bf[0:nk, :], rhs=PT[0:nk, j, q0:ntok], start=(bi == 0 and j == 0), stop=(bi == nb - 1),
                                                       skip_group_check=True), ['ones_bf', 'PT%d' % pi], ['psL1'])

                def head_norm(hp):
                    for j in range(2):
                        latn = latn_p[j]
                        DVE(lambda h, j=j: h.reciprocal(out=rstd[:, :ntok], in_=psL[1][:, j * 256:j * 256 + ntok]), ['psL1'], ['rstd'])
                        DVE(lambda h, j=j, latn=latn: h.tensor_tensor(out=latn[:, :ntok], in0=psL[0][:, j * 256:j * 256 + ntok], in1=rstd[:, :ntok], op=ALU.mult),
                            ['psL0', 'rstd'], ['latn%d' % j])

                def head_tail(hp):
                    for j in range(2):
                        hh = 2 * hp + j
                        latn = latn_p[j]
                        ps, pk = Ln2()
                        PE(lambda h, hh=hh, latn=latn, ps=ps: h.matmul(ps[0:64, :ntok], lhsT=wuv_bf[:, hh * 64:(hh + 1) * 64], rhs=latn[:, :ntok], start=True, stop=True),
                           ['latn%d' % j, 'smallw'], [pk])
                        ACT(lambda h, hh=hh, ps=ps: h.activation(out=mlaT[:, hh, :ntok], in_=ps[0:64, :ntok], func=AF.Copy), [pk], ['mlaT'])

                def attn_thread():
                    nb = len(blocks)
                    ptail = None
                    for hp in range(4):
                        pend = []
                        score(hp, 0, pend)
                        score(hp, 1, pend)
                        for bi in range(nb):
                            if bi + 2 < nb:
                                score(hp, bi + 2, pend)
                            pv(hp, pend, nb)
                            if bi == 1 and ptail is not None:
                                head_tail(ptail)
                                ptail = None
                            yield
                        if ptail is not None:
                            head_tail(ptail)
                        head_norm(hp)
                        ptail = hp
                        yield
                    head_tail(ptail)
                    yield

                wo_slots = {}

                def wo_load(u):
                    sl = wo_ctr[0] % 3
                    wo_ctr[0] += 1
                    wo_slots[u] = sl
                    S.dma('sp', 'wo%d' % sl, [(wo_p[sl][:].rearrange("p g m -> p (g m)"), scr_out[u])], reads=['scr'], writes=['wo%d' % sl])

                for u in range(3):
                    wo_load(u)
                threads = [attn_thread(), rwkv_thread()]
                import os as _os
                if _os.environ.get("NOINT") == "1":
                    for t in threads:
                        for _ in t:
                            pass
                    threads = []
                while threads:
                    for t in list(threads):
                        try:
                            next(t)
                        except StopIteration:
                            threads.remove(t)
                ckpt(42)
                for dch in range(8):
                    ps, pk = Rn4()
                    for half in range(2):
                        u = dch * 2 + half
                        sl = wo_slots[u]
                        for g8 in range(8):
                            src = rwoT if half == 0 else mlaT
                            PE(lambda h, sl=sl, g8=g8, src=src, ps=ps, half=half: h.matmul(ps[:, :ntok], lhsT=wo_p[sl][:, g8, :], rhs=src[:, g8, :ntok],
                                                                                       start=(half == 0 and g8 == 0), stop=(half == 1 and g8 == 7)),
                               ['wo%d' % sl, 'rwoT' if half == 0 else 'mlaT'], [pk], inc=(half == 1 and g8 == 7))
                        if u + 3 < 16:
                            wo_load(u + 3)
                    DVE(lambda h, dch=dch, ps=ps: h.tensor_tensor(out=xT[:, dch, :ntok], in0=ps[:, :ntok], in1=xT[:, dch, :ntok], op=ALU.add), [pk, 'xT%d' % dch], ['xT%d' % dch])
                ckpt(43)
                rmsnorm_to_hT(2, ntok)
                ffn(1, ntok)
                ps, pk = Rn4()
                for dch in range(8):
                    sq = sq_p[dch % 2]
                    sk = 'sq%d' % (dch % 2)
                    ACT(lambda h, dch=dch, sq=sq: h.activation(out=sq[:, :ntok], in_=xT[:, dch, :ntok], func=AF.Square), ['xT%d' % dch], [sk])
                    PE(lambda h, dch=dch, sq=sq, ps=ps: h.matmul(ps[:, :ntok], lhsT=ones_bf[:], rhs=sq[:, :ntok], start=(dch == 0), stop=(dch == 7)),
                       [sk, 'ones_bf'], [pk], inc=(dch == 7))
                rstd_from(ps, pk, ntok, 128, 1.0 / D)
                for dch in range(8):
                    if True:
                        DVE(lambda h, dch=dch: h.scalar_tensor_tensor(out=xT[:, dch, :ntok], in0=xT[:, dch, :ntok], scalar=gains[:, 3, dch:dch + 1],
                                                                      in1=rstd[:, :ntok], op0=ALU.mult, op1=ALU.mult), ['xT%d' % dch, 'rstd', 'gains'], ['xT%d' % dch])
                    else:
                        POOL(lambda h, dch=dch: h.tensor_tensor(out=xT[:, dch, :ntok], in0=xT[:, dch, :ntok], in1=rstd[:, :ntok], op=ALU.mult),
                             ['xT%d' % dch, 'rstd'], ['xT%d' % dch])
                        POOL(lambda h, dch=dch: h.tensor_scalar(out=xT[:, dch, :ntok], in0=xT[:, dch, :ntok], scalar1=gains[:, 3, dch:dch + 1], scalar2=None,
                                                                op0=ALU.mult), ['xT%d' % dch, 'gains'], ['xT%d' % dch])
                for bi, (t0, n) in enumerate(blks):
                    for q4 in range(2):
                        ps, pk = Rn4()
                        for j in range(4):
                            dch = q4 * 4 + j
                            PE(lambda h, t0=t0, n=n, j=j, dch=dch, ps=ps: h.transpose(ps[0:n, j * 128:(j + 1) * 128], xT[:, dch, t0:t0 + n], ident[:]),
                               ['xT%d' % dch, 'ident'], [pk], inc=(j == 3))
                        EV(yout[0:n, q4 * 512:(q4 + 1) * 512], ps[0:n, :], [pk], ['yo%d' % q4])
                        S.dma('sp', 'st_y', [(y_dst[t0:t0 + n, q4 * 512:(q4 + 1) * 512], yout[0:n, q4 * 512:(q4 + 1) * 512])],
                              reads=['yo%d' % q4], writes=['o_y%d' % q4])

            rw_tail = [None]
            wo_ctr = [0]

            def rwkv_chunk(cch, ntok, need_y):
                cs = slice(cch * 64, (cch + 1) * 64)
                ps, pk = Rn()
                PE(lambda h, ps=ps: h.matmul(ps[0:64, :], lhsT=tw[:, cs], rhs=w_w2_bf[:], start=True, stop=True), ['tw', 'smallw'], [pk])
                DVE(lambda h, ps=ps: h.tensor_tensor(out=sg[:], in0=ps[0:64, :], in1=bc[:, 0, :], op=ALU.add), [pk, 'bc'], ['sg'])
                ACT(lambda h: h.activation(out=sg[:], in_=sg[:], func=AF.Sigmoid), ['sg'], ['sg'])
                yield
                pcl, kcl = Rn()
                pce, kce = Rn()
                for hh in range(8):
                    PE(lambda h, hh=hh, pcl=pcl: h.matmul(pcl[0:64, hh * 64:(hh + 1) * 64], lhsT=sg[:, hh * 64:(hh + 1) * 64], rhs=tri[:, 0, :], start=True, stop=True),
                       ['sg', 'tri'], [kcl], inc=(hh == 7))
                for hh in range(8):
                    PE(lambda h, hh=hh, pce=pce: h.matmul(pce[0:64, hh * 64:(hh + 1) * 64], lhsT=sg[:, hh * 64:(hh + 1) * 64], rhs=tri[:, 1, :], start=True, stop=True),
                       ['sg', 'tri'], [kce], inc=(hh == 7))
                v3 = lambda t: t[0:64, :].rearrange("p (a b) -> p a b", a=8)
                ACT(lambda h: h.activation(out=gam[:], in_=v3(pcl), func=AF.Exp, scale=-C0), [kcl], ['gam'])
                ACT(lambda h: h.activation(out=ginv[:], in_=v3(pcl), func=AF.Exp, scale=C0), [kcl], ['ginv'])
                ACT(lambda h: h.activation(out=gprev[:], in_=v3(pce), func=AF.Exp, scale=-C0), [kce], ['gprev'])
                if need_y:
                    ckpt(30)
                yield
                DVE(lambda h: h.tensor_tensor(out=bT[:], in0=b_t[:, :, cs], in1=ginv[:], op=ALU.mult), ['b_t', 'ginv'], ['bT'])
                DVE(lambda h: h.tensor_tensor(out=kT[:], in0=kp_t[:, :, cs], in1=ginv[:], op=ALU.mult), ['kp_t', 'ginv'], ['kT'])
                DVE(lambda h: h.scalar_tensor_tensor(out=aT[:], in0=kk_t[:, :, cs], scalar=-1.0, in1=gprev[:], op0=ALU.mult, op1=ALU.mult),
                    ['kk_t', 'gprev'], ['aT'])
                POOL(lambda h: h.tensor_tensor(out=rT[:], in0=r_t[:, :, cs], in1=gam[:], op=ALU.mult), ['r_t', 'gam'], ['rT'])
                if need_y:
                    POOL(lambda h: h.tensor_tensor(out=prodT[:], in0=r_t[:, :, cs], in1=kp_t[:, :, cs], op=ALU.mult), ['r_t', 'kp_t'], ['prodT'])

                def tr8(src_fn, dst, dkey, rkeys):
                    ps, pk = Rn()
                    for hh in range(8):
                        PE(lambda h, hh=hh, ps=ps: h.matmul(ps[0:64, hh * 64:(hh + 1) * 64], lhsT=src_fn(hh), rhs=ident_bf[0:64, 0:64], start=True, stop=True),
                           rkeys + ['ident_bf'], [pk], inc=(hh == 7))
                    EV(dst[:], ps[0:64, :], [pk], [dkey])

                yield
                tr8(lambda hh: bT[:, hh, :], bt, 'bt', ['bT'])
                yield
                tr8(lambda hh: kT[:, hh, :], kt, 'kt', ['kT'])
                yield
                tr8(lambda hh: v_t[:, hh, cs], vt, 'vt', ['v_t'])
                yield

                def sc8(l_fn, r_fn, rkeys, mask_i, dst, dkey):
                    ps, pk = Rn()
                    for hh in range(8):
                        PE(lambda h, hh=hh, ps=ps: h.matmul(ps[0:64, hh * 64:(hh + 1) * 64], lhsT=l_fn(hh), rhs=r_fn(hh), start=True, stop=True),
                           rkeys, [pk], inc=(hh == 7))
                    DVE(lambda h, ps=ps: h.tensor_tensor(out=dst[:].rearrange("p (a b) -> p a b", a=8), in0=v3(ps),
                                                         in1=masks[:, mask_i, :].unsqueeze(1).to_broadcast([64, 8, 64]), op=ALU.mult),
                        [pk, 'masks'], [dkey])

                sc8(lambda hh: bT[:, hh, :], lambda hh: aT[:, hh, :], ['bT', 'aT'], 0, Qm[0], 'Qm0')
                yield
                sc8(lambda hh: aT[:, hh, :], lambda hh: bT[:, hh, :], ['bT', 'aT'], 1, Pm[0], 'Pm0')
                yield
                sc8(lambda hh: kT[:, hh, :], lambda hh: aT[:, hh, :], ['kT', 'aT'], 0, AKT, 'AKT')
                yield
                if need_y:
                    sc8(lambda hh: bT[:, hh, :], lambda hh: rT[:, hh, :], ['bT', 'rT'], 2, RBT, 'RBT')
                    yield
                    sc8(lambda hh: kT[:, hh, :], lambda hh: rT[:, hh, :], ['kT', 'rT'], 2, RKT, 'RKT')
                    yield
                if need_y:
                    ckpt(31)
                if rw_tail[0] is not None:
                    rw_tail[0]()
                    rw_tail[0] = None
                    yield
                POOL(lambda h: h.tensor_tensor(out=TT[0][:].rearrange("p (a b) -> p a b", a=8), in0=Qm[0][:].rearrange("p (a b) -> p a b", a=8),
                                               in1=ident_bf[0:64, 0:64].unsqueeze(1).to_broadcast([64, 8, 64]), op=ALU.add), ['Qm0', 'ident_bf'], ['TT0'])
                cur = 0
                tcur = 0

                def mm8(l, lk, r, rk, evac):
                    ps, pk = Rn()
                    for hh in range(8):
                        PE(lambda h, hh=hh, ps=ps: h.matmul(ps[0:64, hh * 64:(hh + 1) * 64], lhsT=l[:, hh * 64:(hh + 1) * 64], rhs=r[:, hh * 64:(hh + 1) * 64],
                                                            start=True, stop=True), [lk, rk], [pk], inc=(hh == 7))
                    evac(ps, pk)

                for j in range(5):
                    yield
                    nx = 1 - cur
                    mm8(Qm[cur], 'Qm%d' % cur, Pm[cur], 'Pm%d' % cur,
                        lambda ps, pk, nx=nx: ACT(lambda h: h.activation(out=Pm[nx][:], in_=ps[0:64, :], func=AF.Copy), [pk], ['Pm%d' % nx]))
                    if j < 4:
                        mm8(Pm[cur], 'Pm%d' % cur, Qm[cur], 'Qm%d' % cur,
                            lambda ps, pk, nx=nx: ACT(lambda h: h.activation(out=Qm[nx][:], in_=ps[0:64, :], func=AF.Copy), [pk], ['Qm%d' % nx]))
                    if j >= 1:
                        tn = 1 - tcur
                        mm8(Pm[cur], 'Pm%d' % cur, TT[tcur], 'TT%d' % tcur,
                            lambda ps, pk, tn=tn, tc=tcur: DVE(lambda h: h.tensor_tensor(out=TT[tn][:], in0=ps[0:64, :], in1=TT[tc][:], op=ALU.add),
                                                               [pk, 'TT%d' % tc], ['TT%d' % tn]))
                        tcur = tn
                    cur = nx
                yield
                tn = 1 - tcur
                mm8(Pm[cur], 'Pm%d' % cur, TT[tcur], 'TT%d' % tcur,
                    lambda ps, pk, tn=tn, tc=tcur: DVE(lambda h: h.tensor_tensor(out=TT[tn][:], in0=ps[0:64, :], in1=TT[tc][:], op=ALU.add),
                                                       [pk, 'TT%d' % tc], ['TT%d' % tn]))
                tcur = tn
                Tf = TT[tcur]
                Tk = 'TT%d' % tcur
                if need_y:
                    ckpt(32)
                yield
                ps, pk = Rn()
                for hh in range(8):
                    hs = slice(hh * 64, (hh + 1) * 64)
                    PE(lambda h, hh=hh, hs=hs, ps=ps: h.matmul(ps[0:64, hs], lhsT=aT[:, hh, :], rhs=Sbf[:, hh, :], start=True, stop=False), ['aT', 'Sbf'], [pk], inc=False)
                    PE(lambda h, hh=hh, hs=hs, ps=ps: h.matmul(ps[0:64, hs], lhsT=AKT[:, hs], rhs=vt[:, hs], start=False, stop=True), ['AKT', 'vt'], [pk], inc=(hh == 7))
                ACT(lambda h, ps=ps: h.activation(out=Xb[:], in_=ps[0:64, :], func=AF.Copy), [pk], ['Xb'])
                yield
                ps, pk = Rn()
                for hh in range(8):
                    hs = slice(hh * 64, (hh + 1) * 64)
                    PE(lambda h, hs=hs, ps=ps: h.matmul(ps[0:64, hs], lhsT=Tf[:, hs], rhs=Xb[:, hs], start=True, stop=True), [Tk, 'Xb'], [pk], inc=(hh == 7))
                DVE(lambda h, ps=ps: h.tensor_copy(out=Ub[:], in_=ps[0:64, :]), [pk], ['Ub'])
                yield
                if need_y:
                    psy, pky = Rn()
                    for hh in range(8):
                        hs = slice(hh * 64, (hh + 1) * 64)
                        PE(lambda h, hh=hh, hs=hs: h.matmul(psy[0:64, hs], lhsT=rT[:, hh, :], rhs=Sbf[:, hh, :], start=True, stop=False), ['rT', 'Sbf'], [pky], inc=False)
                        PE(lambda h, hs=hs: h.matmul(psy[0:64, hs], lhsT=RBT[:, hs], rhs=Ub[:, hs], start=False, stop=False), ['RBT', 'Ub'], [pky], inc=False)
                        PE(lambda h, hs=hs: h.matmul(psy[0:64, hs], lhsT=RKT[:, hs], rhs=vt[:, hs], start=False, stop=True), ['RKT', 'vt'], [pky], inc=(hh == 7))
                if need_y:
                    ckpt(34)
                yield
                pss, pks = Rn()
                for hh in range(8):
                    hs = slice(hh * 64, (hh + 1) * 64)
                    PE(lambda h, hs=hs: h.matmul(pss[0:64, hs], lhsT=bt[:, hs], rhs=Ub[:, hs], start=True, stop=False), ['bt', 'Ub'], [pks], inc=False)
                    PE(lambda h, hs=hs: h.matmul(pss[0:64, hs], lhsT=kt[:, hs], rhs=vt[:, hs], start=False, stop=True), ['kt', 'vt'], [pks], inc=(hh == 7))
                DVE(lambda h: h.tensor_tensor(out=S32[:], in0=v3(pss), in1=S32[:], op=ALU.add), [pks, 'S32'], ['S32'])
                DVE(lambda h: h.tensor_tensor(out=Sbf[:], in0=S32[:], in1=gam[:, :, 63:64].to_broadcast([64, 8, 64]), op=ALU.mult), ['S32', 'gam'], ['Sbf'])
                DVE(lambda h: h.tensor_tensor(out=S32[:], in0=S32[:], in1=gam[:, :, 63:64].to_broadcast([64, 8, 64]), op=ALU.mult), ['S32', 'gam'], ['S32'])
                yield
                if not need_y:
                    return
                ckpt(35)
                ACT(lambda h: h.activation(out=ytm[:], in_=v3(psy), func=AF.Copy), [pky], ['ytm'])
                ACT(lambda h: h.activation(out=ysq[:], in_=v3(psy), func=AF.Square), [pky], ['ysq'])
                DVE(lambda h: h.tensor_reduce(out=st8[:, 0, :], in_=ytm[:], axis=AX.X, op=ALU.add), ['ytm'], ['st8a'])
                DVE(lambda h: h.tensor_reduce(out=st8[:, 1, :], in_=ysq[:], axis=AX.X, op=ALU.add), ['ysq'], ['st8b'])
                DVE(lambda h: h.tensor_scalar(out=st8[:, 2, :], in0=st8[:, 0, :], scalar1=1.0 / 64, scalar2=None, op0=ALU.mult), ['st8a'], ['st8c'])
                DVE(lambda h: h.tensor_tensor(out=st8[:, 5, :], in0=st8[:, 2, :], in1=st8[:, 2, :], op=ALU.mult), ['st8c'], ['st8f'])
                DVE(lambda h: h.scalar_tensor_tensor(out=st8[:, 3, :], in0=st8[:, 1, :], scalar=1.0 / 64, in1=st8[:, 5, :], op0=ALU.mult, op1=ALU.subtract),
                    ['st8b', 'st8f'], ['st8d'])
                DVE(lambda h: h.tensor_scalar(out=st8[:, 3, :], in0=st8[:, 3, :], scalar1=GN_EPS, scalar2=None, op0=ALU.add), ['st8d'], ['st8d'])
                ACT(lambda h: h.activation(out=st8[:, 3, :], in_=st8[:, 3, :], func=AF.Sqrt), ['st8d'], ['st8d'])
                DVE(lambda h: h.reciprocal(out=st8[:, 3, :], in_=st8[:, 3, :]), ['st8d'], ['st8d'])
                DVE(lambda h: h.tensor_tensor(out=ytm[:], in0=ytm[:], in1=st8[:, 2, :].unsqueeze(2).to_broadcast([64, 8, 64]), op=ALU.subtract), ['ytm', 'st8c'], ['ytm'])
                DVE(lambda h: h.tensor_tensor(out=ytm[:], in0=ytm[:], in1=st8[:, 3, :].unsqueeze(2).to_broadcast([64, 8, 64]), op=ALU.mult), ['ytm', 'st8d'], ['ytm'])
                yf = ytm[:].rearrange("p a b -> p (a b)")
                DVE(lambda h: h.tensor_tensor(out=yf, in0=yf, in1=bc[:, 1, :], op=ALU.mult), ['ytm', 'bc'], ['ytm'])
                DVE(lambda h: h.tensor_tensor(out=yf, in0=yf, in1=bc[:, 2, :], op=ALU.add), ['ytm', 'bc'], ['ytm'])
                yield
                ps, pk = Rn()
                for hh in range(8):
                    PE(lambda h, hh=hh, ps=ps: h.matmul(ps[0:64, hh:hh + 1], lhsT=prodT[:, hh, :], rhs=rk_bf[:, hh:hh + 1], start=True, stop=True),
                       ['prodT', 'rk_bf'], [pk], inc=(hh == 7))
                ACT(lambda h, ps=ps: h.activation(out=st8[:, 4, :], in_=ps[0:64, 0:8], func=AF.Copy), [pk], ['st8e'])
                DVE(lambda h: h.tensor_tensor(out=ysq[:], in0=vt[:].rearrange("p (a b) -> p a b", a=8), in1=st8[:, 4, :].unsqueeze(2).to_broadcast([64, 8, 64]), op=ALU.mult),
                    ['vt', 'st8e'], ['ysq'])
                DVE(lambda h: h.tensor_tensor(out=ytm[:], in0=ytm[:], in1=ysq[:], op=ALU.add), ['ytm', 'ysq'], ['ytm'])
                ckpt(36)
                yield
                ps, pk = Rn()
                PE(lambda h, ps=ps: h.matmul(ps[0:64, :], lhsT=sgl[:, cs], rhs=w_g2_bf[:], start=True, stop=True), ['sgl', 'smallw'], [pk])
                DVE(lambda h, ps=ps: h.tensor_tensor(out=rwo[:], in0=ps[0:64, :], in1=yf, op=ALU.mult), [pk, 'ytm'], ['rwo'])
                def tail():
                    ps, pk = Rn()
                    for hh in range(8):
                        hs = slice(hh * 64, (hh + 1) * 64)
                        PE(lambda h, hs=hs, ps=ps: h.matmul(ps[0:64, hs], lhsT=rwo[:, hs], rhs=ident_bf[0:64, 0:64], start=True, stop=True), ['rwo', 'ident_bf'], [pk], inc=(hh == 7))
                    ACT(lambda h, ps=ps: h.activation(out=rwoT[:, :, cs], in_=v3(ps), func=AF.Copy), [pk], ['rwoT'])

                rw_tail[0] = tail

            tmpWA = c32

            POOL(lambda h: h.memset(xin[0:64, 0, :], 0.0), ['xin0'], ['xin0'])
            S.dma('sp', 'ld_x', [(xin[48:64, 0, :], meta)], reads=['xin0'], writes=['xin0'])
            POOL(lambda h: h.memset(carry[:], 0.0), [], ['carry%d' % c_ for c_ in list(range(0, 24, 2)) + [24, 25]])
            POOL(lambda h: h.memset(S32[:], 0.0), [], ['S32'])
            POOL(lambda h: h.memset(Sbf[:], 0.0), [], ['Sbf'])
            POOL(lambda h: h.memset(als[:], 0.0), [], ['als'])

            def load_x_prompt(si, ti):
                def f():
                    S.dma('sp', 'ld_x', [(xin[:, b, :], xp[si, ti * NT + b * 128: ti * NT + (b + 1) * 128, :]) for b in range(2)],
                          writes=['xin0', 'xin1'])
                return f

            def load_x_sample():
                S.dma('sp', 'ld_x', [(xin[0:64, 0, :], xs)], writes=['xin0'])

            ntile = SEQ // NT
            tile(64, None, [tabM[i] for i in range(4)], 0, [], False, None, None, None, is_meta=True,
                 nxt=(load_x_prompt(0, 0) if NSEQ > 0 else load_x_sample))
            ckpt(10)
            DVE(lambda h: h.tensor_copy(out=S32m[:], in_=S32[:]), ['S32'], ['S32m'])
            DVE(lambda h: h.tensor_copy(out=carry_m[:], in_=carry[:]), ['carry%d' % c_ for c_ in list(range(0, 24, 2)) + [24, 25]], ['carry_m'])

            def store_state(wkv_dst, sh_dst):
                for half in range(2):
                    ps, pk = Rn()
                    for j in range(4):
                        hh = half * 4 + j
                        PE(lambda h, hh=hh, j=j, ps=ps: h.transpose(ps[0:64, j * 64:(j + 1) * 64], S32[:, hh, :], ident[0:64, 0:64]), ['S32', 'ident'], [pk], inc=(j == 3))
                    ACT(lambda h, half=half, ps=ps: h.activation(out=ytm[:, half * 4:(half + 1) * 4, :], in_=ps[0:64, 0:256].rearrange("p (a b) -> p a b", a=4), func=AF.Copy),
                        [pk], ['ytm'])
                S.dma('pool', 'st_s', [(wkv_dst.rearrange("h v k -> v h k"), ytm[:])], reads=['ytm'], writes=['o_s'])
                S.dma('pool', 'st_s', [(sh_dst[0:1536].rearrange("(c p) -> p c", p=64), carry[0:64, 0:24]),
                                       (sh_dst[1536:1792].rearrange("(c p) -> p c", p=128), carry[:, 24:26])],
                      reads=['carry%d' % c_ for c_ in list(range(0, 24, 2)) + [24, 25]], writes=['o_s'], allow_slow_non_contiguous=True)

            for si in range(NSEQ):
                DVE(lambda h: h.tensor_copy(out=S32[:], in_=S32m[:]), ['S32m'], ['S32'])
                ACT(lambda h: h.activation(out=Sbf[:], in_=S32m[:], func=AF.Copy), ['S32m'], ['Sbf'])
                DVE(lambda h: h.tensor_copy(out=carry[:], in_=carry_m[:]), ['carry_m'], ['carry%d' % c_ for c_ in list(range(0, 24, 2)) + [24, 25]])
                for ti in range(ntile):
                    if ti + 1 < ntile:
                        nxt = load_x_prompt(si, ti + 1)
                    elif si + 1 < NSEQ:
                        nxt = load_x_prompt(si + 1, 0)
                    else:
                        nxt = load_x_sample
                    p0 = N_META + ti * NT
                    fb = [(cT_c[:, b * 128:(b + 1) * 128], krT_c[:, b * 128:(b + 1) * 128], ctok_c[:, b, :], 128) for b in range(ti * NT // 128)]
                    tile(NT, None, [tabP[i, :, p0:p0 + NT] for i in range(4)], ti * NT, fb, True,
                         y_p[si, ti * NT:(ti + 1) * NT, :], ckv_p[si, p0:p0 + NT, :], kr_p[si, p0:p0 + NT, :], nxt=nxt)
                store_state(wkv_p[si], sh_p[si])

            ckpt(50)
            S.dma('pool', 'ld_s', [(dmix[:, :, :].rearrange("p a b -> p (a b)")[:, 0:512].rearrange("p (a b) -> p a b", a=16), ckr.rearrange("(b p) r -> p b r", p=128))],
                  writes=['dmix'])
            S.dma('pool', 'ld_s', [(ytm[:], swkv.rearrange("h v k -> v h k"))], writes=['ytm'])
            krv = dmix[:, :, :].rearrange("p a b -> p (a b)")[:, 0:512].rearrange("p (a b) -> p a b", a=16)
            ckvv = None

            def sample_cache_c():
                pass

            S.dma('sp', 'ld_x', [(xin[:, 1, :].rearrange("p (b c) -> p b c", b=8), cckv[0:1024, :].rearrange("(b p) c -> p b c", p=128))], writes=['xin1'])
            for half in range(2):
                if half == 1:
                    S.dma('sp', 'ld_x', [(xin[:, 1, :].rearrange("p (b c) -> p b c", b=8), cckv[1024:2048, :].rearrange("(b p) c -> p b c", p=128))],
                          reads=['xin1'], writes=['xin1'])
                xv = xin[:, 1, :].rearrange("p (b c) -> p b c", b=8)
                DVE(lambda h, half=half, xv=xv: h.tensor_copy(out=ctok_c[:, half * 8:(half + 1) * 8, :], in_=xv), ['xin1'], ['ctok_c'])
                for b in range(8):
                    ps, pk = Rn()
                    PE(lambda h, b=b, ps=ps, xv=xv: h.transpose(ps[:, 0:128], xv[:, b, :], ident[:]), ['xin1', 'ident'], [pk])
                    EV(cT_c[:, (half * 8 + b) * 128:(half * 8 + b + 1) * 128], ps[:, 0:128], [pk], ['cT_c'])
            for b in range(16):
                ps, pk = Rn()
                PE(lambda h, b=b, ps=ps: h.transpose(ps[0:32, 0:128], krv[:, b, :], ident[:]), ['dmix', 'ident'], [pk])
                EV(krT_c[:, b * 128:(b + 1) * 128], ps[0:32, 0:128], [pk], ['krT_c'])
            swv = ytm
            for half in range(2):
                ps, pk = Rn()
                for j in range(4):
                    hh = half * 4 + j
                    PE(lambda h, hh=hh, j=j, ps=ps: h.transpose(ps[0:64, j * 64:(j + 1) * 64], swv[:, hh, :], ident[0:64, 0:64]), ['ytm', 'ident'], [pk], inc=(j == 3))
                DVE(lambda h, half=half, ps=ps: h.tensor_copy(out=S32[:, half * 4:(half + 1) * 4, :], in_=ps[0:64, 0:256].rearrange("p (a b) -> p a b", a=4)), [pk], ['S32'])
            ACT(lambda h: h.activation(out=Sbf[:], in_=S32[:], func=AF.Copy), ['S32'], ['Sbf'])
            S.dma('sp', 'ld_x', [(carry[0:64, 0:24], ssh[0:1536].rearrange("(c p) -> p c", p=64)),
                                 (carry[:, 24:26], ssh[1536:1792].rearrange("(c p) -> p c", p=128))], reads=['carry%d' % c_ for c_ in list(range(0, 24, 2)) + [24, 25]], writes=['carry%d' % c_ for c_ in list(range(0, 24, 2)) + [24, 25]],
                  allow_slow_non_contiguous=True)
            if NSEQ == 0:
                pass
            fb = [(cT_c[:, b * 128:(b + 1) * 128], krT_c[:, b * 128:(b + 1) * 128], ctok_c[:, b, :], 128) for b in range(16)]
            tile(64, None, [tabS[i] for i in range(4)], PAST, fb, True, y_s, ckv_s, kr_s)
            store_state(wkv_s, sh_s)


        except StopBuild:
            pass
        S.finish('sp')
        S.emit()
        print("ops:", S.nops, "sems:", S.nsem, flush=True)
    return nc


def make_consts(SEQ):
    ident = np.eye(128, dtype=np.float32)
    i = np.arange(64)[:, None]
    j = np.arange(64)[None, :]
    import ml_dtypes
    masks = np.stack([(i < j), (i > j), (i <= j)]).astype(np.float32).astype(ml_dtypes.bfloat16)
    tri = np.stack([(i <= j), (i < j)]).astype(np.float32)
    half = 16
    inv = (10000.0 ** (-np.arange(half, dtype=np.float32) / half)).astype(np.float32)

    def tab(pos):
        ang = pos.astype(np.float32)[None, :] * inv[:, None]
        cos = np.cos(ang).astype(np.float32)
        sin = np.sin(ang).astype(np.float32)
        c2 = np.concatenate([cos, cos], 0)
        s2 = np.concatenate([sin, sin], 0)
        sc = np.float32(MLA_SCALE)
        return np.stack([c2, s2, c2 * sc, s2 * sc]).astype(np.float32)

    tabp = tab(np.arange(N_META + SEQ))
    tabs = tab(N_META + PAST + np.arange(64))
    tabm = tab(np.concatenate([np.zeros(48), np.arange(16)]))
    return dict(c_ident=ident, c_masks=masks, c_tri=tri, c_tabp=tabp, c_tabs=tabs, c_tabm=tabm)


_CACHE = {}


def run(inputs, SEQ=4096, NSEQ=2, ncores=8, stop=99):
    key = (SEQ, NSEQ, stop)
    if key not in _CACHE:
        _CACHE[key] = build(SEQ, NSEQ, stop)
    nc = _CACHE[key]
    consts = make_consts(SEQ)
    f32 = lambda a: np.ascontiguousarray(np.asarray(a, dtype=np.float32))
    wmap = {}
    for n in W_NAMES:
        a = f32(inputs[n])
        if n != "final_norm":
            a = a[0]
        wmap[n] = np.ascontiguousarray(a.reshape(W_SHAPES[n]))
    in_maps = []
    for c in range(ncores):
        m = dict(wmap)
        m.update(consts)
        m["xp"] = f32(inputs["x_prompt"][c * NSEQ:(c + 1) * NSEQ, :SEQ])
        m["xs"] = f32(inputs["x_sample"][c])
        m["cckv"] = f32(inputs["cache_ckv"][0, c])
        m["ckr"] = f32(inputs["cache_krope"][0, c])
        m["swkv"] = f32(inputs["state_wkv"][0, c])
        m["ssh"] = f32(inputs["state_shift"][0, c, 0])
        m["meta"] = f32(inputs["meta_tokens"])
        in_maps.append(m)
    res = run_bass_kernel_spmd(nc, in_maps, core_ids=list(range(ncores)))
    R = res.results
    cat = lambda k: np.concatenate([np.asarray(r[k]) for r in R], axis=0)
    stk = lambda k: np.stack([np.asarray(r[k]) for r in R], axis=0)
    y_p = cat("y_p")
    y_s = stk("y_s")
    outs = (y_p, y_s,
            cat("ckv_p")[None], cat("kr_p")[None], cat("wkv_p")[None], cat("sh_p")[None, :, None, :],
            stk("ckv_s")[None], stk("kr_s")[None], stk("wkv_s")[None], stk("sh_s")[None, :, None, :])
    return tuple(np.ascontiguousarray(o.astype(np.float32)) for o in outs)


def kernel(**inputs):
    return run(inputs, SEQ=4096, NSEQ=2, ncores=8)
```

```python
import math
from contextlib import ExitStack

import numpy as np
import concourse.bass as bass
import concourse.mybir as mybir
from concourse.bass_utils import run_bass_kernel_spmd

F32 = mybir.dt.float32
BF16 = mybir.dt.bfloat16
ALU = mybir.AluOpType
AF = mybir.ActivationFunctionType
AX = mybir.AxisListType

ENGS = ('pe', 'act', 'dve', 'pool', 'sp')
SEM_LIMIT = 30000

D = 1024
DFF = 2816
NFC = 22
N_META = 16
PAST = 2048
NT = 256
C0 = math.exp(-0.5)
MLA_SCALE = 96 ** -0.5
NORM_EPS = 1e-6
GN_EPS = 64e-5


class COp:
    __slots__ = ("eng", "idx", "fn", "need", "sig")

    def __init__(self, eng, idx, fn):
        self.eng = eng
        self.idx = idx
        self.fn = fn
        self.need = False
        self.sig = None


class Sched:
    def __init__(self, nc, es):
        self.nc = nc
        self.es = es
        self.prog = {e: [] for e in ENGS}
        self.seen = {e: {} for e in ENGS}
        self.lastw = {}
        self.readers = {}
        self.nsem = 0
        self.rings = {}
        self.nops = {e: 0 for e in ENGS}
        self.cnt = {e: 0 for e in ENGS}

    def newsem(self):
        s = self.es.enter_context(self.nc.semaphore("s%d" % self.nsem))
        self.nsem += 1
        return s

    def _deps(self, eng, reads, writes):
        toks = []
        skip_own = (eng == 'pe')

        def own(t):
            return t[0] == 'c' and t[1].eng == eng

        for k in reads:
            t = self.lastw.get(k)
            if t is not None and not (skip_own and own(t)):
                toks.append(t)
        for k in writes:
            t = self.lastw.get(k)
            if t is not None and not (skip_own and own(t)):
                toks.append(t)
            for t in self.readers.get(k, ()):
                if not (skip_own and own(t)):
                    toks.append(t)
        best = {}
        for t in toks:
            if t[0] == 'c':
                key = ('c', t[1].eng)
                val = t[1].idx
            else:
                key = ('d', id(t[1]))
                val = t[2]
            if key not in best or best[key][0] < val:
                best[key] = (val, t)
        seen = self.seen[eng]
        out = []
        for key, (val, t) in best.items():
            if seen.get(key, -1) < val:
                seen[key] = val
                out.append(t)
        return out

    def _wait(self, eng, t):
        if t[0] == 'c':
            t[1].need = True
        self.prog[eng].append(('w', t))

    def _record(self, tok, reads, writes):
        for k in writes:
            self.lastw[k] = tok
            self.readers[k] = []
        for k in reads:
            self.readers.setdefault(k, []).append(tok)

    def op(self, eng, fn, reads=(), writes=(), inc=True):
        ps_r = [k for k in reads if k.startswith('ps')]
        if ps_r:
            reads = [k for k in reads if not k.startswith('ps')]
            writes = list(writes) + ps_r
        for t in self._deps(eng, reads, writes):
            self._wait(eng, t)
        o = COp(eng, self.cnt[eng], fn)
        self.cnt[eng] += 1
        self.prog[eng].append(('c', o))
        self.nops[eng] += 1
        tok = ('c', o)
        self._record(tok, reads, writes)
        return tok

    def dma(self, eng, chan, pairs, reads=(), writes=(), **kw):
        ring = self.rings.setdefault(eng, {'sems': [], 'i': 0})
        nring = 24 if eng == 'sp' else 16
        if len(ring['sems']) < nring:
            ring['sems'].append([self.newsem(), 0])
        ent = ring['sems'][ring['i'] % nring]
        ring['i'] += 1
        if ent[1] + 16 * len(pairs) >= SEM_LIMIT:
            s0, v0 = ent[0], ent[1]
            self.prog[eng].append(('w', ('d', s0, v0)))
            ent[0], ent[1] = self.newsem(), 0
        sem, cnt = ent[0], ent[1]
        key = ('d', id(sem))
        if cnt > 0 and self.seen[eng].get(key, -1) < cnt:
            self.seen[eng][key] = cnt
            self.prog[eng].append(('w', ('d', sem, cnt)))
        for t in self._deps(eng, reads, writes):
            self._wait(eng, t)
        for (o, i) in pairs:
            self.prog[eng].append(('raw', lambda h, o=o, i=i, sem=sem: h.dma_start(out=o, in_=i, **kw).then_inc(sem, 16)))
            ent[1] += 16
            self.nops[eng] += 1
        tok = ('d', sem, ent[1])
        self._record(tok, reads, writes)
        return tok

    def finish(self, eng='sp'):
        toks = []
        for k, t in self.lastw.items():
            toks.append(t)
        best = {}
        for t in toks:
            if t[0] == 'c':
                key = ('c', t[1].eng)
                val = t[1].idx
            else:
                key = ('d', id(t[1]))
                val = t[2]
            if key not in best or best[key][0] < val:
                best[key] = (val, t)
        for key, (val, t) in best.items():
            if self.seen[eng].get(key, -1) < val:
                self.seen[eng][key] = val
                self._wait(eng, t)

    def emit(self):
        nc = self.nc
        prog = self.prog
        nincs = {}
        for e in ENGS:
            sem, c = None, 0
            n = 0
            for ent in prog[e]:
                if ent[0] == 'c' and ent[1].need:
                    if sem is None or c >= SEM_LIMIT:
                        sem, c = self.newsem(), 0
                    c += 1
                    n += 1
                    ent[1].sig = (sem, c)
            nincs[e] = n
        print("incs:", nincs, flush=True)

        def run(e, h):
            for ent in prog[e]:
                k = ent[0]
                if k == 'c':
                    o = ent[1]
                    ins = o.fn(h)
                    if o.need:
                        ins.then_inc(o.sig[0], 1)
                elif k == 'w':
                    t = ent[1]
                    if t[0] == 'c':
                        h.wait_ge(t[1].sig[0], t[1].sig[1])
                    else:
                        h.wait_ge(t[1], t[2])
                else:
                    ent[1](h)

        with nc.Block() as block:
            @block.tensor
            def _(e):
                run('pe', e)

            @block.scalar
            def _(e):
                run('act', e)

            @block.vector
            def _(e):
                run('dve', e)

            @block.gpsimd
            def _(e):
                run('pool', e)

            @block.sync
            def _(e):
                run('sp', e)


class StopBuild(Exception):
    pass


class RR:
    def __init__(self, items):
        self.items = items
        self.i = 0

    def __call__(self):
        r = self.items[self.i % len(self.items)]
        self.i += 1
        return r


W_NAMES = ["norm_ffn1", "ffn1_w1", "ffn1_w3", "ffn1_w2", "norm_mix", "w_in", "mu_shift", "w0", "w_w2", "a0",
           "w_a2", "w_g2", "k_k", "k_a", "r_k", "ln_x_w", "ln_x_b", "q_norm", "w_q_up", "kv_norm", "w_uk", "w_uv",
           "w_out", "norm_ffn2", "ffn2_w1", "ffn2_w3", "ffn2_w2", "final_norm"]
W_SHAPES = {
    "norm_ffn1": [D], "ffn1_w1": [D, DFF], "ffn1_w3": [D, DFF], "ffn1_w2": [DFF, D], "norm_mix": [D],
    "w_in": [D, 2208], "mu_shift": [1792], "w0": [512], "w_w2": [64, 512], "a0": [512], "w_a2": [64, 512],
    "w_g2": [128, 512], "k_k": [512], "k_a": [512], "r_k": [512], "ln_x_w": [512], "ln_x_b": [512],
    "q_norm": [256], "w_q_up": [256, 768], "kv_norm": [128], "w_uk": [128, 512], "w_uv": [128, 512],
    "w_out": [D, D], "norm_ffn2": [D], "ffn2_w1": [D, DFF], "ffn2_w3": [D, DFF], "ffn2_w2": [DFF, D],
    "final_norm": [D],
}


def build(SEQ=4096, NSEQ=2, stop=99):
    nc = bass.Bass("TRN2", target_bir_lowering=False)
    es = ExitStack()
    with es:
        S = Sched(nc, es)

        def din(name, shape, dt=F32):
            return nc.dram_tensor(name, shape, dt, kind="ExternalInput").ap()

        def dout(name, shape):
            return nc.dram_tensor(name, shape, F32, kind="ExternalOutput").ap()

        def sb(name, shape, dt=F32):
            return es.enter_context(nc.sbuf_tensor(name, shape, dt))

        xp = din("xp", [NSEQ, SEQ, D])
        xs = din("xs", [64, D])
        cckv = din("cckv", [PAST, 128])
        ckr = din("ckr", [PAST, 32])
        swkv = din("swkv", [8, 64, 64])
        ssh = din("ssh", [1792])
        meta = din("meta", [N_META, D])
        W = {n: din(n, W_SHAPES[n]) for n in W_NAMES}
        ident_d = din("c_ident", [128, 128])
        masks_d = din("c_masks", [3, 64, 64], BF16)
        tri_d = din("c_tri", [2, 64, 64])
        tabP = din("c_tabp", [4, 32, N_META + SEQ])
        tabS = din("c_tabs", [4, 32, 64])
        tabM = din("c_tabm", [4, 32, 64])

        y_p = dout("y_p", [NSEQ, SEQ, D])
        y_s = dout("y_s", [64, D])
        ckv_p = dout("ckv_p", [NSEQ, N_META + SEQ, 128])
        kr_p = dout("kr_p", [NSEQ, N_META + SEQ, 32])
        wkv_p = dout("wkv_p", [NSEQ, 8, 64, 64])
        sh_p = dout("sh_p", [NSEQ, 1792])
        ckv_s = dout("ckv_s", [64, 128])
        kr_s = dout("kr_s", [64, 32])
        wkv_s = dout("wkv_s", [8, 64, 64])
        sh_s = dout("sh_s", [1792])

        scr_f = [nc.dram_tensor("scr_f%d" % i, [NFC, 128, 3072], BF16, kind="Internal").ap() for i in range(2)]
        scr_in = nc.dram_tensor("scr_in", [18, 128, 1024], BF16, kind="Internal").ap()
        scr_out = nc.dram_tensor("scr_out", [16, 64, 1024], BF16, kind="Internal").ap()

        ident = sb("ident", [128, 128])
        ident_bf = sb("ident_bf", [128, 128], BF16)
        ones_bf = sb("ones_bf", [128, 128], BF16)
        masks = sb("masks", [64, 3, 64], BF16)
        tri = sb("tri", [64, 2, 64])
        bc = sb("bc", [64, 3, 512])
        gains = sb("gains", [128, 4, 8])
        qn_g = sb("qn_g", [128, 2])
        kvn_g = sb("kvn_g", [128, 1])
        mu_t = sb("mu_t", [128, 26])
        hv = sb("hv", [64, 5, 8])
        rk_bf = sb("rk_bf", [64, 8], BF16)
        eps_c = sb("eps_c", [128, 1])
        w_w2_bf = sb("w_w2_bf", [64, 512], BF16)
        w_a2_bf = sb("w_a2_bf", [128, 512], BF16)
        w_g2_bf = sb("w_g2_bf", [128, 512], BF16)
        wq_bf = sb("wq_bf", [128, 2, 768], BF16)
        wqrot_bf = sb("wqrot_bf", [128, 2, 8, 32], BF16)
        wukT_bf = sb("wukT_bf", [64, 8, 128], BF16)
        wuv_bf = sb("wuv_bf", [128, 512], BF16)

        cT_c = sb("cT_c", [128, 4096], BF16)
        krT_c = sb("krT_c", [32, 4096], BF16)
        ctok_c = sb("ctok_c", [128, 32, 128], BF16)
        cT_m = sb("cT_m", [128, 16], BF16)
        krT_m = sb("krT_m", [32, 16], BF16)
        ctok_m = sb("ctok_m", [16, 128], BF16)
        ctok_d = sb("ctok_d", [64, 4, 128], BF16)

        xT = sb("xT", [128, 8, NT])
        hT = sb("hT", [128, 8, NT], BF16)
        xin = sb("xin", [128, 2, 1024])
        yout = sb("yout", [128, 1024])
        sq_p = [sb("sq%d" % i, [128, NT], BF16) for i in range(2)]
        rstd = sb("rstd", [128, NT])
        sa_p = [sb("sa%d" % i, [128, NT]) for i in range(2)]
        g_p = [sb("g%d" % i, [128, NT], BF16) for i in range(3)]
        w13_p = [sb("w13_%d" % i, [128, 2048], BF16) for i in range(3)]
        w2_p = [sb("w2_%d" % i, [128, 1024], BF16) for i in range(3)]
        wi_p = [sb("wi%d" % i, [128, 8, 128], BF16) for i in range(3)]
        wo_p = [sb("wo%d" % i, [64, 8, 128], BF16) for i in range(3)]
        pst_p = [sb("pst%d" % i, [128, 2, NT + 1]) for i in range(2)]
        dmix = sb("dmix", [128, 2, NT])
        ks_t = sb("ks_t", [64, 2, NT])
        carry = sb("carry", [128, 26])
        carry_m = sb("carry_m", [128, 26])
        tw = sb("tw", [64, NT], BF16)
        als = sb("als", [128, NT], BF16)
        sgl = sb("sgl", [128, NT], BF16)
        r_t = sb("r_t", [64, 8, NT], BF16)
        kp_t = sb("kp_t", [64, 8, NT], BF16)
        kk_t = sb("kk_t", [64, 8, NT], BF16)
        b_t = sb("b_t", [64, 8, NT], BF16)
        v_t = sb("v_t", [64, 8, NT], BF16)
        a_t = sb("a_t", [64, 8, NT], BF16)
        sg = sb("sg", [64, 512])
        gam = sb("gam", [64, 8, 64])
        ginv = sb("ginv", [64, 8, 64])
        gprev = sb("gprev", [64, 8, 64])
        tmpA = gam[:].rearrange("p a b -> p (a b)")
        tmpB = ginv[:].rearrange("p a b -> p (a b)")
        tmpC = gprev[:].rearrange("p a b -> p (a b)")
        rT = sb("rT", [64, 8, 64], BF16)
        kT = sb("kT", [64, 8, 64], BF16)
        bT = sb("bT", [64, 8, 64], BF16)
        aT = sb("aT", [64, 8, 64], BF16)
        prodT = sb("prodT", [64, 8, 64], BF16)
        bt = sb("bt", [64, 512], BF16)
        kt = sb("kt", [64, 512], BF16)
        vt = sb("vt", [64, 512], BF16)
        AKT = sb("AKT", [64, 512], BF16)
        RBT = sb("RBT", [64, 512], BF16)
        RKT = sb("RKT", [64, 512], BF16)
        Pm = [sb("Pm%d" % i, [64, 512], BF16) for i in range(2)]
        Qm = [sb("Qm%d" % i, [64, 512], BF16) for i in range(2)]
        TT = [sb("TT%d" % i, [64, 512], BF16) for i in range(2)]
        Xb = sb("Xb", [64, 512], BF16)
        Ub = sb("Ub", [64, 512], BF16)
        S32 = sb("S32", [64, 8, 64])
        S32m = sb("S32m", [64, 8, 64])
        Sbf = sb("Sbf", [64, 8, 64], BF16)
        ytm = sb("ytm", [64, 8, 64])
        ysq = sb("ysq", [64, 8, 64])
        st8 = sb("st8", [64, 6, 8])
        rwo = sb("rwo", [64, 512], BF16)
        rwoT = sb("rwoT", [64, 8, NT], BF16)
        mlaT = sb("mlaT", [64, 8, NT], BF16)
        cq32 = sb("cq32", [128, 2, NT])
        cqn = sb("cqn", [128, 2, NT], BF16)
        c32 = sb("c32", [128, NT])
        kr32 = sb("kr32", [32, NT])
        tabs = sb("tabs", [32, 4, NT])
        qn_p = [sb("qn%d" % i, [64, NT], BF16) for i in range(2)]
        qlatT = sb("qlatT", [128, 8, NT], BF16)
        qrT = sb("qrT", [32, 8, NT], BF16)
        qt1 = sb("qt1", [32, NT])
        qt2 = sb("qt2", [32, NT])
        cst = sb("cst", [128, 2, 128])
        krst = sb("krst", [128, 2, 32])
        PT_p = [sb("PT%d" % i, [128, 2, NT], BF16) for i in range(3)]
        latn_p = [sb("latn%d" % i, [128, NT], BF16) for i in range(2)]

        psL = [es.enter_context(nc.psum_tensor("psL%d" % i, [128, 512], F32)) for i in range(4)]
        psR = [es.enter_context(nc.psum_tensor("psR%d" % i, [128, 512], F32)) for i in range(4)]
        Rn = RR([(psR[i], "psR%d" % i) for i in range(3)])
        Ln2 = RR([(psL[2], "psL2"), (psL[3], "psL3"), (psR[3], "psR3")])
        Rn4 = RR([(psR[i], "psR%d" % i) for i in range(4)])

        def PE(fn, r, w, inc=True):
            return S.op('pe', fn, r, w, inc)

        def ACT(fn, r, w):
            return S.op('act', fn, r, w)

        def DVE(fn, r, w):
            return S.op('dve', fn, r, w)

        def POOL(fn, r, w):
            return S.op('pool', fn, r, w)

        ew_rr = RR(['act', 'dve'])

        def EV(out, in_, r, w):
            if ew_rr() == 'act':
                return ACT(lambda h: h.activation(out=out, in_=in_, func=AF.Copy), r, w)
            return DVE(lambda h: h.tensor_copy(out=out, in_=in_), r, w)

        def ckpt(n):
            if stop <= n:
                raise StopBuild()

        try:
            S.dma('sp', 'ld_c', [(ident[:], ident_d)], writes=['ident'])
            ACT(lambda h: h.activation(out=ident_bf[:], in_=ident[:], func=AF.Copy), ['ident'], ['ident_bf'])
            POOL(lambda h: h.memset(ones_bf[:], 1.0), [], ['ones_bf'])
            POOL(lambda h: h.memset(eps_c[:], NORM_EPS), [], ['eps_c'])
            S.dma('sp', 'ld_c', [(masks[:, m, :], masks_d[m]) for m in range(3)], writes=['masks'])
            S.dma('sp', 'ld_c', [(tri[:, m, :], tri_d[m]) for m in range(2)], writes=['tri'])
            for i, nme in enumerate(["w0", "ln_x_w", "ln_x_b"]):
                S.dma('sp', 'ld_c', [(bc[:, i, :], W[nme].partition_broadcast(64))], writes=['bc'])
            for i, nme in enumerate(["norm_ffn1", "norm_mix", "norm_ffn2", "final_norm"]):
                S.dma('sp', 'ld_c', [(gains[:, i, :], W[nme].rearrange("(c p) -> p c", p=128))], writes=['gains'],
                      allow_slow_non_contiguous=True)
            S.dma('sp', 'ld_c', [(qn_g[:], W["q_norm"].rearrange("(c p) -> p c", p=128)),
                                 (kvn_g[:], W["kv_norm"].rearrange("(c p) -> p c", p=128))], writes=['qn_g', 'kvn_g'],
                  allow_slow_non_contiguous=True)
            S.dma('sp', 'ld_c', [(mu_t[0:64, 0:24], W["mu_shift"][0:1536].rearrange("(c p) -> p c", p=64)),
                                 (mu_t[:, 24:26], W["mu_shift"][1536:1792].rearrange("(c p) -> p c", p=128))],
                  writes=['mu_t'], allow_slow_non_contiguous=True)
            for i, nme in enumerate(["a0", "k_k", "k_a", "r_k"]):
                S.dma('sp', 'ld_c', [(hv[:, (i if i < 3 else 4), :], W[nme].rearrange("(c p) -> p c", p=64))], writes=['hv'],
                      allow_slow_non_contiguous=True)
            DVE(lambda h: h.tensor_scalar(out=hv[:, 3, :], in0=hv[:, 2, :], scalar1=-1.0, scalar2=1.0, op0=ALU.mult, op1=ALU.add),
                ['hv'], ['hv'])
            DVE(lambda h: h.tensor_copy(out=rk_bf[:], in_=hv[:, 4, :]), ['hv'], ['rk_bf'])

            def small_w(dst_ap, src_ap, rows, cols, slot, r0=0):
                S.dma('sp', 'ld_c', [(xin[r0:r0 + rows, slot, 0:cols], src_ap)], writes=['xin%d' % slot])
                DVE(lambda h: h.tensor_copy(out=dst_ap, in_=xin[r0:r0 + rows, slot, 0:cols]), ['xin%d' % slot], ['smallw'])

            small_w(w_w2_bf[:], W["w_w2"], 64, 512, 0)
            small_w(w_a2_bf[64:128, :], W["w_a2"], 64, 512, 1, r0=64)
            small_w(w_g2_bf[:], W["w_g2"], 128, 512, 0)
            small_w(wuv_bf[:], W["w_uv"], 128, 512, 1)
            for j in range(2):
                small_w(wq_bf[:, j, :], W["w_q_up"][j * 128:(j + 1) * 128, :], 128, 768, j)
            for j in range(2):
                for hh in range(8):
                    c0 = hh * 96 + 64
                    DVE(lambda h, j=j, hh=hh, c0=c0: h.tensor_scalar(out=wqrot_bf[:, j, hh, 0:16], in0=wq_bf[:, j, c0 + 16:c0 + 32],
                                                                      scalar1=-1.0, scalar2=None, op0=ALU.mult),
                        ['smallw'], ['smallw2'])
                    DVE(lambda h, j=j, hh=hh, c0=c0: h.tensor_copy(out=wqrot_bf[:, j, hh, 16:32], in_=wq_bf[:, j, c0:c0 + 16]),
                        ['smallw'], ['smallw2'])
            S.dma('sp', 'ld_c', [(xin[:, 0, 0:512], W["w_uk"])], writes=['xin0'])
            for hh in range(8):
                ps, pk = Rn()
                PE(lambda h, hh=hh, ps=ps: h.transpose(ps[0:64, 0:128], xin[:, 0, hh * 64:(hh + 1) * 64], ident[:]),
                   ['xin0', 'ident'], [pk])
                ACT(lambda h, hh=hh, ps=ps: h.activation(out=wukT_bf[:, hh, :], in_=ps[0:64, 0:128], func=AF.Copy), [pk], ['smallw'])

            ckpt(0)
            cv_rr = RR(['act', 'dve', 'pool'])
            yo_bf = yout[:].bitcast(BF16)
            cvi = [0]

            def cast_unit(src3, dst_flat, npart=128, pieces=None):
                sl = cvi[0] % 2
                cvi[0] += 1
                xk = 'xin%d' % sl
                yk = 'yo%d' % sl
                a, b = src3.shape[1], src3.shape[2]
                S.dma('sp', 'cv_ld', [(xin[0:npart, sl, :].rearrange("p (a b) -> p a b", a=a), src3)], writes=[xk])
                eng = cv_rr()
                if pieces is None:
                    fnc = (lambda h: h.tensor_copy(out=yo_bf[0:npart, sl * 1024:(sl + 1) * 1024], in_=xin[0:npart, sl, :])) \
                        if eng != 'act' else \
                        (lambda h: h.activation(out=yo_bf[0:npart, sl * 1024:(sl + 1) * 1024], in_=xin[0:npart, sl, :], func=AF.Copy))
                    S.op(eng, fnc, [xk], [yk])
                else:
                    pieces(sl, xk, yk)
                S.dma('pool', 'cv_st', [(dst_flat, yo_bf[0:npart, sl * 1024:(sl + 1) * 1024])], reads=[yk], writes=['scr'])

            for fi, (n1, n3, n2) in enumerate([("ffn1_w1", "ffn1_w3", "ffn1_w2"), ("ffn2_w1", "ffn2_w3", "ffn2_w2")]):
                for fc in range(NFC):
                    for wi, nme in enumerate([n1, n3]):
                        cast_unit(W[nme][:, fc * 128:(fc + 1) * 128].rearrange("(k p) f -> p k f", p=128),
                                  scr_f[fi][fc, :, wi * 1024:(wi + 1) * 1024])
                    cast_unit(W[n2][fc * 128:(fc + 1) * 128, :].rearrange("p (a b) -> p a b", a=8),
                              scr_f[fi][fc, :, 2048:3072])
            for g in range(17):
                cast_unit(W["w_in"][:, g * 128:(g + 1) * 128].rearrange("(k p) f -> p k f", p=128), scr_in[g])

            def rope_pieces(sl, xk, yk):
                xv = xin[:, sl, :].rearrange("p (k f) -> p k f", k=8)
                yv = yo_bf[:, sl * 1024:(sl + 1) * 1024].rearrange("p (k f) -> p k f", k=8)
                POOL(lambda h: h.memset(yo_bf[:, sl * 1024:(sl + 1) * 1024], 0.0), [], [yk])
                DVE(lambda h: h.tensor_copy(out=yv[:, :, 0:32], in_=xv[:, :, 0:32]), [xk], [yk])
                DVE(lambda h: h.tensor_scalar(out=yv[:, :, 32:48], in0=xv[:, :, 16:32], scalar1=-1.0, scalar2=None, op0=ALU.mult), [xk], [yk])
                DVE(lambda h: h.tensor_copy(out=yv[:, :, 48:64], in_=xv[:, :, 0:16]), [xk], [yk])

            sl = cvi[0] % 2
            POOL(lambda h, sl=sl: h.memset(xin[:, sl, :], 0.0), [], ['xin%d' % sl])
            cvi[0] += 1
            xk = 'xin%d' % sl
            yk = 'yo%d' % sl
            S.dma('sp', 'cv_ld', [(xin[:, sl, :].rearrange("p (k f) -> p k f", k=8)[:, :, 0:32],
                                   W["w_in"][:, 2176:2208].rearrange("(k p) f -> p k f", p=128))], writes=[xk])
            rope_pieces(sl, xk, yk)
            S.dma('pool', 'cv_st', [(scr_in[17], yo_bf[:, sl * 1024:(sl + 1) * 1024])], reads=[yk], writes=['scr'])
            for dch in range(8):
                for half in range(2):
                    cast_unit(W["w_out"][half * 512:(half + 1) * 512, dch * 128:(dch + 1) * 128].rearrange("(g p) m -> p g m", p=64),
                              scr_out[dch * 2 + half], npart=64)

            ckpt(1)
            def rmsnorm_to_hT(gi, ntok):
                ps, pk = Rn4()
                for dch in range(8):
                    sq = sq_p[dch % 2]
                    sk = 'sq%d' % (dch % 2)
                    ACT(lambda h, dch=dch, sq=sq: h.activation(out=sq[:, :ntok], in_=xT[:, dch, :ntok], func=AF.Square), ['xT%d' % dch], [sk])
                    PE(lambda h, dch=dch, sq=sq, ps=ps: h.matmul(ps[:, :ntok], lhsT=ones_bf[:], rhs=sq[:, :ntok], start=(dch == 0), stop=(dch == 7)),
                       [sk, 'ones_bf'], [pk], inc=(dch == 7))
                rstd_from(ps, pk, ntok, 128, 1.0 / D)
                for dch in range(8):
                    if True:
                        DVE(lambda h, dch=dch: h.scalar_tensor_tensor(out=hT[:, dch, :ntok], in0=xT[:, dch, :ntok], scalar=gains[:, gi, dch:dch + 1],
                                                                      in1=rstd[:, :ntok], op0=ALU.mult, op1=ALU.mult), ['xT%d' % dch, 'rstd', 'gains'], ['hT%d' % dch])
                    else:
                        tb = sa_p[dch % 2]
                        tk = 'sa%d' % (dch % 2)
                        POOL(lambda h, dch=dch, tb=tb: h.tensor_tensor(out=tb[:, :ntok], in0=xT[:, dch, :ntok], in1=rstd[:, :ntok], op=ALU.mult),
                             ['xT%d' % dch, 'rstd'], [tk])
                        POOL(lambda h, dch=dch, tb=tb: h.tensor_scalar(out=hT[:, dch, :ntok], in0=tb[:, :ntok], scalar1=gains[:, gi, dch:dch + 1], scalar2=None,
                                                                        op0=ALU.mult), [tk, 'gains'], ['hT%d' % dch])

            def rstd_from(ps, pk, ntok, npart, inv_n):
                ACT(lambda h: h.activation(out=rstd[0:npart, :ntok], in_=ps[0:npart, :ntok], func=AF.Sqrt, bias=eps_c[0:npart, 0:1], scale=inv_n),
                    [pk, 'eps_c'], ['rstd'])
                DVE(lambda h: h.reciprocal(out=rstd[0:npart, :ntok], in_=rstd[0:npart, :ntok]), ['rstd'], ['rstd'])

            wf_i = [0]

            def ffn(fi, ntok):
                base = wf_i[0]
                wf_i[0] += NFC

                def slot(fc):
                    return (base + fc) % 3

                def load(fc):
                    sl = slot(fc)
                    S.dma('sp', 'w1', [(w13_p[sl][:, 0:1024], scr_f[fi][fc, :, 0:1024])], reads=['scr'], writes=['w13_%d_0' % sl])
                    S.dma('sp', 'w3', [(w13_p[sl][:, 1024:2048], scr_f[fi][fc, :, 1024:2048])], reads=['scr'], writes=['w13_%d_1' % sl])
                    S.dma('sp', 'w2', [(w2_p[sl][:], scr_f[fi][fc, :, 2048:3072])], reads=['scr'], writes=['w2_%d' % sl])

                def up(fc):
                    sl = slot(fc)
                    wf = w13_p[sl]
                    ps, pk = Rn4()
                    for part in range(2):
                        for k in range(8):
                            PE(lambda h, part=part, k=k, ps=ps, wf=wf: h.matmul(ps[:, part * 256:part * 256 + ntok],
                                                                              lhsT=wf[:, part * 1024 + k * 128: part * 1024 + (k + 1) * 128],
                                                                              rhs=hT[:, k, :ntok], start=(k == 0), stop=(k == 7)),
                               ['w13_%d_%d' % (sl, part), 'hT%d' % k], [pk])
                    sa = sa_p[fc % 2]
                    g = g_p[fc % 3]
                    ACT(lambda h, ps=ps, sa=sa: h.activation(out=sa[:, :ntok], in_=ps[:, 0:ntok], func=AF.Silu), [pk], ['sa%d' % (fc % 2)])
                    DVE(lambda h, ps=ps, sa=sa, g=g: h.tensor_tensor(out=g[:, :ntok], in0=ps[:, 256:256 + ntok], in1=sa[:, :ntok], op=ALU.mult),
                        [pk, 'sa%d' % (fc % 2)], ['g%d' % (fc % 3)])

                def down(fc):
                    sl = slot(fc)
                    wf = w2_p[sl]
                    g = g_p[fc % 3]
                    for dch in range(8):
                        acc = psL[dch // 2]
                        PE(lambda h, dch=dch, acc=acc, wf=wf, g=g: h.matmul(acc[:, (dch % 2) * 256:(dch % 2) * 256 + ntok],
                                                                          lhsT=wf[:, dch * 128:(dch + 1) * 128],
                                                                          rhs=g[:, :ntok], start=(fc == 0 and dch % 2 == 0), stop=(fc == NFC - 1),
                                                                          skip_group_check=True),
                           ['w2_%d' % sl, 'g%d' % (fc % 3)], ['psL%d' % (dch // 2)])

                for fc in range(3):
                    load(fc)
                up(0)
                for fc in range(NFC):
                    if fc + 1 < NFC:
                        up(fc + 1)
                    down(fc)
                    if fc + 3 < NFC:
                        load(fc + 3)
                for dch in range(8):
                    acc = psL[dch // 2]
                    DVE(lambda h, dch=dch, acc=acc: h.scalar_tensor_tensor(out=xT[:, dch, :ntok], in0=acc[:, (dch % 2) * 256:(dch % 2) * 256 + ntok],
                                                                         scalar=0.5, in1=xT[:, dch, :ntok], op0=ALU.mult, op1=ALU.add),
                        ['psL%d' % (dch // 2), 'xT%d' % dch], ['xT%d' % dch])

            wi_i = [0]

            wi_st = {'order': [], 'loaded': 0, 'used': 0, 'slots': []}

            def wi_begin(order):
                wi_st['order'] = list(order)
                wi_st['loaded'] = 0
                wi_st['used'] = 0
                wi_st['slots'] = []

            def load_wi(g):
                st = wi_st
                assert st['order'][st['used']] == g, (st['order'], st['used'], g)
                while st['loaded'] < min(len(st['order']), st['used'] + 3):
                    gg = st['order'][st['loaded']]
                    sl = wi_i[0] % 3
                    wi_i[0] += 1
                    S.dma('sp', 'wi%d' % sl, [(wi_p[sl][:].rearrange("p k f -> p (k f)"), scr_in[gg])], reads=['scr'], writes=['wi%d' % sl])
                    st['slots'].append(sl)
                    st['loaded'] += 1
                sl = st['slots'][st['used']]
                st['used'] += 1
                return sl

            def proj(sl, c0, m, ps, pk, col0, ntok, part0=0):
                wi = wi_p[sl]
                for k in range(8):
                    PE(lambda h, k=k: h.matmul(ps[part0:part0 + m, col0:col0 + ntok], lhsT=wi[:, k, c0:c0 + m], rhs=hT[:, k, :ntok],
                                               start=(k == 0), stop=(k == 7)), ['wi%d' % sl, 'hT%d' % k], [pk], inc=(k == 7))

            pst_i = [0]

            def mix(ps, pk, npart, nsub, cols, ntok, outs):
                i = pst_i[0] % 2
                pst_i[0] += 1
                pst = pst_p[i]
                pk2 = 'pst%d' % i
                psv = ps[0:npart, :].rearrange("p (j t) -> p j t", j=2)[:, 0:nsub, 0:ntok]
                ACT(lambda h: h.activation(out=pst[0:npart, 0:nsub, 1:ntok + 1], in_=psv, func=AF.Copy), [pk], [pk2])
                c0 = cols[0]
                DVE(lambda h: h.tensor_copy(out=pst[0:npart, 0:nsub, 0:1], in_=carry[0:npart, c0:c0 + nsub].unsqueeze(2)), ['carry%d' % c0], [pk2])
                DVE(lambda h: h.tensor_tensor(out=dmix[0:npart, 0:nsub, 0:ntok], in0=pst[0:npart, 0:nsub, 0:ntok],
                                              in1=pst[0:npart, 0:nsub, 1:ntok + 1], op=ALU.subtract), [pk2], ['dmix'])
                for j in range(nsub):
                    o, ok = outs[j]
                    DVE(lambda h, j=j, o=o: h.scalar_tensor_tensor(out=o, in0=dmix[0:npart, j, 0:ntok], scalar=mu_t[0:npart, cols[j]:cols[j] + 1],
                                                                 in1=pst[0:npart, j, 1:ntok + 1], op0=ALU.mult, op1=ALU.add),
                        ['dmix', pk2, 'mu_t'], [ok])
                DVE(lambda h: h.tensor_copy(out=carry[0:npart, c0:c0 + nsub].unsqueeze(2), in_=pst[0:npart, 0:nsub, ntok:ntok + 1]), [pk2], ['carry%d' % c0])

            def tile(ntok, x_src, tab_src, key_off, full_blocks, do_attn, y_dst, c_dst, kr_dst, is_meta=False, nxt=None):
                nch = ntok // 64
                blks = [(t0, min(128, ntok - t0)) for t0 in range(0, ntok, 128)]
                for bi, (t0, n) in enumerate(blks):
                    for q4 in range(2):
                        ps, pk = Rn4()
                        for j in range(4):
                            dch = q4 * 4 + j
                            PE(lambda h, bi=bi, n=n, j=j, dch=dch, ps=ps: h.transpose(ps[:, j * 128:j * 128 + n], xin[0:n, bi, dch * 128:(dch + 1) * 128],
                                                                                 ident[0:n, 0:n]), ['xin%d' % bi, 'ident'], [pk], inc=(j == 3))
                        EV(xT[:, q4 * 4:(q4 + 1) * 4, t0:t0 + n], ps[:, :].rearrange("p (j t) -> p j t", j=4)[:, :, 0:n], [pk], ['xT%d' % d_ for d_ in range(q4 * 4, q4 * 4 + 4)])
                if nxt is not None:
                    nxt()
                S.dma('sp', 'tab', [(tabs[:, i, :ntok], tab_src[i]) for i in range(4)], writes=['tabs'])
                if not is_meta:
                    ckpt(20)
                rmsnorm_to_hT(0, ntok)
                ffn(0, ntok)
                if not is_meta:
                    ckpt(21)
                rmsnorm_to_hT(1, ntok)
                wi_begin([12, 13, 4, 5, 0, 6, 1, 7, 2, 3, 8, 9, 10, 11] + ([14, 15] if do_attn else []) + [16, 17])
                sl = load_wi(12)
                ps, pk = Rn4()
                proj(sl, 0, 128, ps, pk, 0, ntok)
                mix(ps, pk, 128, 1, [24], ntok, [(tmpWA[:, :ntok], 'c32')])
                ACT(lambda h: h.activation(out=tw[:, :ntok], in_=tmpWA[0:64, :ntok], func=AF.Tanh), ['c32'], ['tw'])
                DVE(lambda h: h.tensor_copy(out=als[64:128, :ntok], in_=tmpWA[64:128, :ntok]), ['c32'], ['als'])
                sl2 = load_wi(13)
                ps, pk = Rn4()
                proj(sl2, 0, 128, ps, pk, 0, ntok)
                mix(ps, pk, 128, 1, [25], ntok, [(tmpWA[:, :ntok], 'c32')])
                ACT(lambda h: h.activation(out=sgl[:, :ntok], in_=tmpWA[:, :ntok], func=AF.Sigmoid), ['c32'], ['sgl'])
                for hp in range(4):
                    ps, pk = Rn4()
                    for j in range(2):
                        hh = hp * 2 + j
                        PE(lambda h, hh=hh, j=j, ps=ps: h.matmul(ps[0:64, j * 256:j * 256 + ntok], lhsT=w_a2_bf[64:128, hh * 64:(hh + 1) * 64],
                                                               rhs=als[64:128, :ntok], start=True, stop=True), ['als', 'smallw'], [pk], inc=(j == 1))
                    for j in range(2):
                        hh = hp * 2 + j
                        ACT(lambda h, hh=hh, j=j, ps=ps: h.activation(out=a_t[:, hh, :ntok], in_=ps[0:64, j * 256:j * 256 + ntok], func=AF.Sigmoid,
                                                                    bias=hv[:, 0, hh:hh + 1], scale=1.0), [pk, 'hv'], ['a_t'])
                def rv_group(g):
                    sl = load_wi(g)
                    ps, pk = Rn4()
                    dst = r_t if g < 4 else v_t
                    dk = 'r_t' if g < 4 else 'v_t'
                    for j in range(2):
                        proj(sl, j * 64, 64, ps, pk, j * 256, ntok)
                    hp = g % 4
                    mix(ps, pk, 64, 2, [2 * g, 2 * g + 1], ntok, [(dst[:, hp * 2 + j, :ntok], dk) for j in range(2)])

                tA3 = gam[:].rearrange("p a b -> p (a b)").rearrange("p (j t) -> p j t", j=2)
                tB3 = ginv[:].rearrange("p a b -> p (a b)").rearrange("p (j t) -> p j t", j=2)
                tC3 = gprev[:].rearrange("p a b -> p (a b)").rearrange("p (j t) -> p j t", j=2)
                sq3 = Xb[:].rearrange("p (j t) -> p j t", j=2)
                kbufs = [(ks_t, 'ks_t'), (ysq[:].rearrange("p a b -> p (a b)").rearrange("p (j t) -> p j t", j=2), 'ysq')]

                def bcs(col, hp):
                    return hv[:, col, 2 * hp:2 * hp + 2].unsqueeze(2).to_broadcast([64, 2, ntok])

                def k_s1(g):
                    kb, kk_ = kbufs[g % 2]
                    sl = load_wi(g)
                    ps, pk = Rn4()
                    for j in range(2):
                        proj(sl, j * 64, 64, ps, pk, j * 256, ntok)
                    mix(ps, pk, 64, 2, [2 * g, 2 * g + 1], ntok, [(kb[:, j, :ntok], kk_) for j in range(2)])

                def k_s2(g):
                    kb, kk_ = kbufs[g % 2]
                    hp = g % 4
                    DVE(lambda h: h.tensor_tensor(out=tA3[:, :, :ntok], in0=kb[:, :, :ntok], in1=bcs(1, hp), op=ALU.mult), [kk_, 'hv'], ['gam'])
                    ACT(lambda h: h.activation(out=sq3[:, :, :ntok], in_=tA3[:, :, :ntok], func=AF.Square), ['gam'], ['Xb'])

                def k_s3(g):
                    kb, kk_ = kbufs[g % 2]
                    hp = g % 4
                    ps2, pk2 = Rn4()
                    if ntok == 256:
                        PE(lambda h: h.matmul(ps2[0:64, :], lhsT=ones_bf[0:64, 0:64], rhs=Xb[:], start=True, stop=True), ['Xb', 'ones_bf'], [pk2])
                    else:
                        for j in range(2):
                            PE(lambda h, j=j: h.matmul(ps2[0:64, j * 256:j * 256 + ntok], lhsT=ones_bf[0:64, 0:64], rhs=sq3[:, j, :ntok], start=True, stop=True),
                               ['Xb', 'ones_bf'], [pk2])
                    p3 = ps2[0:64, :].rearrange("p (j t) -> p j t", j=2)[:, :, :ntok]
                    ACT(lambda h: h.activation(out=tB3[:, :, :ntok], in_=p3, func=AF.Sqrt), [pk2], ['ginv'])
                    DVE(lambda h: h.tensor_scalar(out=tB3[:, :, :ntok], in0=tB3[:, :, :ntok], scalar1=1e-12, scalar2=None, op0=ALU.max), ['ginv'], ['ginv'])
                    DVE(lambda h: h.reciprocal(out=tB3[:, :, :ntok], in_=tB3[:, :, :ntok]), ['ginv'], ['ginv'])
                    DVE(lambda h: h.tensor_tensor(out=kk_t[:, 2 * hp:2 * hp + 2, :ntok], in0=tA3[:, :, :ntok], in1=tB3[:, :, :ntok], op=ALU.mult),
                        ['gam', 'ginv'], ['kk_t'])
                    POOL(lambda h: h.tensor_tensor(out=b_t[:, 2 * hp:2 * hp + 2, :ntok], in0=kk_t[:, 2 * hp:2 * hp + 2, :ntok], in1=a_t[:, 2 * hp:2 * hp + 2, :ntok], op=ALU.mult),
                         ['kk_t', 'a_t'], ['b_t'])
                    DVE(lambda h: h.tensor_tensor(out=tC3[:, :, :ntok], in0=a_t[:, 2 * hp:2 * hp + 2, :ntok], in1=bcs(2, hp), op=ALU.mult), ['a_t', 'hv'], ['gprev'])
                    DVE(lambda h: h.tensor_tensor(out=tC3[:, :, :ntok], in0=tC3[:, :, :ntok], in1=bcs(3, hp), op=ALU.add), ['gprev', 'hv'], ['gprev'])
                    DVE(lambda h: h.tensor_tensor(out=kp_t[:, 2 * hp:2 * hp + 2, :ntok], in0=kb[:, :, :ntok], in1=tC3[:, :, :ntok], op=ALU.mult),
                        [kk_, 'gprev'], ['kp_t'])

                k_s1(4)
                k_s2(4)
                k_s1(5)
                rv_group(0)
                k_s3(4)
                k_s2(5)
                k_s1(6)
                rv_group(1)
                k_s3(5)
                k_s2(6)
                k_s1(7)
                rv_group(2)
                k_s3(6)
                k_s2(7)
                rv_group(3)
                k_s3(7)
                for g in range(8, 12):
                    rv_group(g)
                if not is_meta:
                    ckpt(22)
                if do_attn:
                    sl = load_wi(14)
                    ps, pk = Rn4()
                    proj(sl, 0, 128, ps, pk, 0, ntok)
                    sl2 = load_wi(15)
                    proj(sl2, 0, 128, ps, pk, 256, ntok)
                    ACT(lambda h, ps=ps: h.activation(out=cq32[:, :, :ntok], in_=ps[:, :].rearrange("p (j t) -> p j t", j=2)[:, :, 0:ntok], func=AF.Copy),
                        [pk], ['cq32'])
                    ps2, pk2 = Rn4()
                    for j in range(2):
                        ACT(lambda h, j=j: h.activation(out=sq_p[j][:, :ntok], in_=cq32[:, j, :ntok], func=AF.Square), ['cq32'], ['sq%d' % j])
                        PE(lambda h, j=j, ps2=ps2: h.matmul(ps2[:, :ntok], lhsT=ones_bf[:], rhs=sq_p[j][:, :ntok], start=(j == 0), stop=(j == 1)),
                           ['sq%d' % j, 'ones_bf'], [pk2], inc=(j == 1))
                    rstd_from(ps2, pk2, ntok, 128, 1.0 / 256)
                    for j in range(2):
                        DVE(lambda h, j=j: h.scalar_tensor_tensor(out=cqn[:, j, :ntok], in0=cq32[:, j, :ntok], scalar=qn_g[:, j:j + 1], in1=rstd[:, :ntok],
                                                                  op0=ALU.mult, op1=ALU.mult), ['cq32', 'rstd', 'qn_g'], ['cqn'])
                if not is_meta:
                    ckpt(22.1)
                sl = load_wi(16)
                ps, pk = Rn4()
                proj(sl, 0, 128, ps, pk, 0, ntok)
                sl2 = load_wi(17)
                ACT(lambda h, ps=ps: h.activation(out=c32[:, :ntok], in_=ps[:, 0:ntok], func=AF.Copy), [pk], ['c32'])
                ACT(lambda h: h.activation(out=sq_p[0][:, :ntok], in_=c32[:, :ntok], func=AF.Square), ['c32'], ['sq0'])
                ps2, pk2 = Rn4()
                PE(lambda h, ps2=ps2: h.matmul(ps2[:, :ntok], lhsT=ones_bf[:], rhs=sq_p[0][:, :ntok], start=True, stop=True), ['sq0', 'ones_bf'], [pk2])
                rstd_from(ps2, pk2, ntok, 128, 1.0 / 128)
                DVE(lambda h: h.scalar_tensor_tensor(out=c32[:, :ntok], in0=c32[:, :ntok], scalar=kvn_g[:, 0:1], in1=rstd[:, :ntok],
                                                     op0=ALU.mult, op1=ALU.mult), ['c32', 'rstd', 'kvn_g'], ['c32'])
                if is_meta:
                    ACT(lambda h: h.activation(out=cT_m[:], in_=c32[:, 48:64], func=AF.Copy), ['c32'], ['cT_m'])
                else:
                    ACT(lambda h: h.activation(out=cT_c[:, key_off:key_off + ntok], in_=c32[:, :ntok], func=AF.Copy), ['c32'], ['cT_c'])
                if not is_meta:
                    ckpt(22.2)
                ps, pk = Rn4()
                proj(sl2, 0, 32, ps, pk, 0, ntok)
                proj(sl2, 32, 32, ps, pk, 256, ntok)
                DVE(lambda h, ps=ps: h.tensor_tensor(out=qt1[:, :ntok], in0=ps[0:32, 0:ntok], in1=tabs[:, 0, :ntok], op=ALU.mult), [pk, 'tabs'], ['qt1'])
                DVE(lambda h, ps=ps: h.tensor_tensor(out=kr32[:, :ntok], in0=ps[0:32, 256:256 + ntok], in1=tabs[:, 1, :ntok], op=ALU.mult), [pk, 'tabs'], ['kr32'])
                DVE(lambda h: h.tensor_tensor(out=kr32[:, :ntok], in0=kr32[:, :ntok], in1=qt1[:, :ntok], op=ALU.add), ['kr32', 'qt1'], ['kr32'])
                if is_meta:
                    ACT(lambda h: h.activation(out=krT_m[:], in_=kr32[:, 48:64], func=AF.Copy), ['kr32'], ['krT_m'])
                else:
                    ACT(lambda h: h.activation(out=krT_c[:, key_off:key_off + ntok], in_=kr32[:, :ntok], func=AF.Copy), ['kr32'], ['krT_c'])
                if not is_meta:
                    ckpt(22.3)
                for bi, (t0, n) in enumerate(blks):
                    ps, pk = Rn4()
                    PE(lambda h, t0=t0, n=n, ps=ps: h.transpose(ps[0:n, 0:128], c32[:, t0:t0 + n], ident[:]), ['c32', 'ident'], [pk], inc=False)
                    PE(lambda h, t0=t0, n=n, ps=ps: h.transpose(ps[0:n, 128:160], kr32[:, t0:t0 + n], ident[0:32, 0:32]), ['kr32', 'ident'], [pk])
                    ACT(lambda h, bi=bi, n=n, ps=ps: h.activation(out=cst[0:n, bi, :], in_=ps[0:n, 0:128], func=AF.Copy), [pk], ['cst'])
                    DVE(lambda h, bi=bi, n=n, ps=ps: h.tensor_copy(out=krst[0:n, bi, :], in_=ps[0:n, 128:160]), [pk], ['krst'])
                    if (not is_meta) and n == 128:
                        DVE(lambda h, bi=bi, ps=ps: h.tensor_copy(out=ctok_c[:, (key_off + bi * 128) // 128, :], in_=ps[:, 0:128]), [pk], ['ctok_c'])
                if not is_meta:
                    ckpt(22.4)
                if is_meta:
                    ps, pk = Rn4()
                    PE(lambda h, ps=ps: h.matmul(ps[0:16, 0:128], lhsT=cT_m[:], rhs=ident_bf[:], start=True, stop=True), ['cT_m', 'ident_bf'], [pk])
                    ACT(lambda h, ps=ps: h.activation(out=ctok_m[:], in_=ps[0:16, 0:128], func=AF.Copy), [pk], ['ctok_m'])
                    for si in range(NSEQ):
                        S.dma('pool', 'st_c', [(ckv_p[si, 0:16, :], cst[48:64, 0, :]), (kr_p[si, 0:16, :], krst[48:64, 0, :])],
                              reads=['cst', 'krst'], writes=['o_ckv'])
                else:
                    for bi, (t0, n) in enumerate(blks):
                        S.dma('pool', 'st_c', [(c_dst[t0:t0 + n, :], cst[0:n, bi, :]), (kr_dst[t0:t0 + n, :], krst[0:n, bi, :])],
                              reads=['cst', 'krst'], writes=['o_ckv'])
                    ckpt(22.5)
                    for cch in range(nch):
                        ps, pk = Rn4()
                        PE(lambda h, cch=cch, ps=ps: h.matmul(ps[0:64, 0:128], lhsT=cT_c[:, key_off + cch * 64:key_off + (cch + 1) * 64],
                                                            rhs=ident_bf[:], start=True, stop=True), ['cT_c', 'ident_bf'], [pk])
                        ACT(lambda h, cch=cch, ps=ps: h.activation(out=ctok_d[:, cch, :], in_=ps[0:64, 0:128], func=AF.Copy), [pk], ['ctok_d'])

                if not is_meta:
                    ckpt(23)
                def rwkv_thread():
                    for cch in range(nch):
                        yield from rwkv_chunk(cch, ntok, need_y=do_attn)
                    if rw_tail[0] is not None:
                        rw_tail[0]()
                        rw_tail[0] = None

                if not do_attn:
                    for _ in rwkv_thread():
                        pass
                    return
                ckpt(40)
                for hh in range(8):
                    psn, pkn = Rn4()
                    for j in range(2):
                        PE(lambda h, j=j, hh=hh, psn=psn: h.matmul(psn[0:64, :ntok], lhsT=wq_bf[:, j, hh * 96:hh * 96 + 64], rhs=cqn[:, j, :ntok],
                                                                 start=(j == 0), stop=(j == 1)), ['cqn', 'smallw'], [pkn], inc=(j == 1))
                    psr, pkr = Rn4()
                    for j in range(2):
                        PE(lambda h, j=j, hh=hh, psr=psr: h.matmul(psr[0:32, 0:ntok], lhsT=wq_bf[:, j, hh * 96 + 64:hh * 96 + 96], rhs=cqn[:, j, :ntok],
                                                                 start=(j == 0), stop=(j == 1)), ['cqn', 'smallw'], [pkr], inc=False)
                    for j in range(2):
                        PE(lambda h, j=j, hh=hh, psr=psr: h.matmul(psr[0:32, 256:256 + ntok], lhsT=wqrot_bf[:, j, hh, :], rhs=cqn[:, j, :ntok],
                                                                 start=(j == 0), stop=(j == 1)), ['cqn', 'smallw2'], [pkr], inc=(j == 1))
                    qn = qn_p[hh % 2]
                    qk = 'qn%d' % (hh % 2)
                    ACT(lambda h, psn=psn, qn=qn: h.activation(out=qn[:, :ntok], in_=psn[0:64, :ntok], func=AF.Copy), [pkn], [qk])
                    psl, pkl = Rn4()
                    PE(lambda h, hh=hh, psl=psl, qn=qn: h.matmul(psl[:, :ntok], lhsT=wukT_bf[:, hh, :], rhs=qn[:, :ntok], start=True, stop=True),
                       [qk, 'smallw'], [pkl])
                    ACT(lambda h, hh=hh, psl=psl: h.activation(out=qlatT[:, hh, :ntok], in_=psl[:, :ntok], func=AF.Copy, scale=MLA_SCALE), [pkl], ['qlatT'])
                    DVE(lambda h, psr=psr: h.tensor_tensor(out=qt1[:, :ntok], in0=psr[0:32, 0:ntok], in1=tabs[:, 2, :ntok], op=ALU.mult), [pkr, 'tabs'], ['qt1'])
                    DVE(lambda h, psr=psr: h.tensor_tensor(out=qt2[:, :ntok], in0=psr[0:32, 256:256 + ntok], in1=tabs[:, 3, :ntok], op=ALU.mult), [pkr, 'tabs'], ['qt2'])
                    DVE(lambda h, hh=hh: h.tensor_tensor(out=qrT[:, hh, :ntok], in0=qt1[:, :ntok], in1=qt2[:, :ntok], op=ALU.add), ['qt1', 'qt2'], ['qrT'])
                ckpt(41)
                blocks = [(cT_m[:], krT_m[:], ctok_m[:], 16, 0, ['cT_m', 'krT_m', 'ctok_m'])]
                for (a1, a2, a3, nk) in full_blocks:
                    blocks.append((a1, a2, a3, nk, 0, ['cT_c', 'krT_c', 'ctok_c']))
                for cch in range(nch):
                    blocks.append((cT_c[:, key_off + cch * 64:key_off + (cch + 1) * 64], krT_c[:, key_off + cch * 64:key_off + (cch + 1) * 64],
                                   ctok_d[:, cch, :], 64, cch * 64, ['cT_c', 'krT_c', 'ctok_d']))
                pt_i = [0]
                v2 = lambda t, np_: t[0:np_, :].rearrange("p (j t) -> p j t", j=2)

                def score(hp, bi, pend):
                    a1, a2, a3, nk, q0, keys = blocks[bi]
                    ps, pk = Ln2()
                    o = v2(ps, nk)[:, :, q0:ntok]
                    if q0 == 0 and ntok == 256:
                        PE(lambda h: h.matmul(ps[0:nk, :], lhsT=a1, rhs=qlatT[:, 2 * hp:2 * hp + 2, :].rearrange("p a b -> p (a b)"), start=True, stop=False),
                           keys[0:1] + ['qlatT'], [pk])
                        PE(lambda h: h.matmul(ps[0:nk, :], lhsT=a2, rhs=qrT[:, 2 * hp:2 * hp + 2, :].rearrange("p a b -> p (a b)"), start=False, stop=True),
                           keys[1:2] + ['qrT'], [pk])
                    else:
                        for j in range(2):
                            oj = ps[0:nk, j * 256 + q0:j * 256 + ntok]
                            PE(lambda h, j=j, oj=oj: h.matmul(oj, lhsT=a1, rhs=qlatT[:, 2 * hp + j, q0:ntok], start=True, stop=False),
                               keys[0:1] + ['qlatT'], [pk])
                            PE(lambda h, j=j, oj=oj: h.matmul(oj, lhsT=a2, rhs=qrT[:, 2 * hp + j, q0:ntok], start=False, stop=True),
                               keys[1:2] + ['qrT'], [pk])
                    pi = pt_i[0] % 3
                    pt_i[0] += 1
                    PT = PT_p[pi]
                    ACT(lambda h: h.activation(out=PT[0:nk, :, q0:ntok], in_=o, func=AF.Exp), [pk], ['PT%d' % pi])
                    pend.append((bi, pi))

                def pv(hp, pend, nb):
                    bi, pi = pend.pop(0)
                    a1, a2, a3, nk, q0, keys = blocks[bi]
                    PT = PT_p[pi]
                    if q0 == 0 and ntok == 256:
                        PE(lambda h: h.matmul(psL[0][:, :], lhsT=a3, rhs=PT[0:nk, :, :].rearrange("p a b -> p (a b)"), start=(bi == 0), stop=(bi == nb - 1),
                                              skip_group_check=True), keys[2:3] + ['PT%d' % pi], ['psL0'])
                        PE(lambda h: h.matmul(psL[1][:, :], lhsT=ones_bf[0:nk, :], rhs=PT[0:nk, :, :].rearrange("p a b -> p (a b)"), start=(bi == 0), stop=(bi == nb - 1),
                                              skip_group_check=True), ['ones_bf', 'PT%d' % pi], ['psL1'])
                    else:
                        for j in range(2):
                            PE(lambda h, j=j: h.matmul(psL[0][:, j * 256 + q0:j * 256 + ntok], lhsT=a3, rhs=PT[0:nk, j, q0:ntok], start=(bi == 0 and j == 0), stop=(bi == nb - 1),
                                                       skip_group_check=True), keys[2:3] + ['PT%d' % pi], ['psL0'])
                            PE(lambda h, j=j: h.matmul(psL[1][:, j * 256 + q0:j * 256 + ntok], lhsT=ones_bf[0:nk, :], rhs=PT[0:nk, j, q0:ntok], start=(bi == 0 and j == 0), stop=(bi == nb - 1),
                                                       skip_group_check=True), ['ones_bf', 'PT%d' % pi], ['psL1'])

                def head_norm(hp):
                    for j in range(2):
                        latn = latn_p[j]
                        DVE(lambda h, j=j: h.reciprocal(out=rstd[:, :ntok], in_=psL[1][:, j * 256:j * 256 + ntok]), ['psL1'], ['rstd'])
                        DVE(lambda h, j=j, latn=latn: h.tensor_tensor(out=latn[:, :ntok], in0=psL[0][:, j * 256:j * 256 + ntok], in1=rstd[:, :ntok], op=ALU.mult),
                            ['psL0', 'rstd'], ['latn%d' % j])

                def head_tail(hp):
                    for j in range(2):
                        hh = 2 * hp + j
                        latn = latn_p[j]
                        ps, pk = Ln2()
                        PE(lambda h, hh=hh, latn=latn, ps=ps: h.matmul(ps[0:64, :ntok], lhsT=wuv_bf[:, hh * 64:(hh + 1) * 64], rhs=latn[:, :ntok], start=True, stop=True),
                           ['latn%d' % j, 'smallw'], [pk])
                        ACT(lambda h, hh=hh, ps=ps: h.activation(out=mlaT[:, hh, :ntok], in_=ps[0:64, :ntok], func=AF.Copy), [pk], ['mlaT'])

                def attn_thread():
                    nb = len(blocks)
                    ptail = None
                    for hp in range(4):
                        pend = []
                        score(hp, 0, pend)
                        score(hp, 1, pend)
                        for bi in range(nb):
                            if bi + 2 < nb:
                                score(hp, bi + 2, pend)
                            pv(hp, pend, nb)
                            if bi == 1 and ptail is not None:
                                head_tail(ptail)
                                ptail = None
                            yield
                        if ptail is not None:
                            head_tail(ptail)
                        head_norm(hp)
                        ptail = hp
                        yield
                    head_tail(ptail)
                    yield

                wo_slots = {}

                def wo_load(u):
                    sl = wo_ctr[0] % 3
                    wo_ctr[0] += 1
                    wo_slots[u] = sl
                    S.dma('sp', 'wo%d' % sl, [(wo_p[sl][:].rearrange("p g m -> p (g m)"), scr_out[u])], reads=['scr'], writes=['wo%d' % sl])

                for u in range(3):
                    wo_load(u)
                threads = [attn_thread(), rwkv_thread()]
                import os as _os
                if _os.environ.get("NOINT") == "1":
                    for t in threads:
                        for _ in t:
                            pass
                    threads = []
                while threads:
                    for t in list(threads):
                        try:
                            next(t)
                        except StopIteration:
                            threads.remove(t)
                ckpt(42)
                for dch in range(8):
                    ps, pk = Rn4()
                    for half in range(2):
                        u = dch * 2 + half
                        sl = wo_slots[u]
                        for g8 in range(8):
                            src = rwoT if half == 0 else mlaT
                            PE(lambda h, sl=sl, g8=g8, src=src, ps=ps, half=half: h.matmul(ps[:, :ntok], lhsT=wo_p[sl][:, g8, :], rhs=src[:, g8, :ntok],
                                                                                       start=(half == 0 and g8 == 0), stop=(half == 1 and g8 == 7)),
                               ['wo%d' % sl, 'rwoT' if half == 0 else 'mlaT'], [pk], inc=(half == 1 and g8 == 7))
                        if u + 3 < 16:
                            wo_load(u + 3)
                    DVE(lambda h, dch=dch, ps=ps: h.tensor_tensor(out=xT[:, dch, :ntok], in0=ps[:, :ntok], in1=xT[:, dch, :ntok], op=ALU.add), [pk, 'xT%d' % dch], ['xT%d' % dch])
                ckpt(43)
                rmsnorm_to_hT(2, ntok)
                ffn(1, ntok)
                ps, pk = Rn4()
                for dch in range(8):
                    sq = sq_p[dch % 2]
                    sk = 'sq%d' % (dch % 2)
                    ACT(lambda h, dch=dch, sq=sq: h.activation(out=sq[:, :ntok], in_=xT[:, dch, :ntok], func=AF.Square), ['xT%d' % dch], [sk])
                    PE(lambda h, dch=dch, sq=sq, ps=ps: h.matmul(ps[:, :ntok], lhsT=ones_bf[:], rhs=sq[:, :ntok], start=(dch == 0), stop=(dch == 7)),
                       [sk, 'ones_bf'], [pk], inc=(dch == 7))
                rstd_from(ps, pk, ntok, 128, 1.0 / D)
                for dch in range(8):
                    if True:
                        DVE(lambda h, dch=dch: h.scalar_tensor_tensor(out=xT[:, dch, :ntok], in0=xT[:, dch, :ntok], scalar=gains[:, 3, dch:dch + 1],
                                                                      in1=rstd[:, :ntok], op0=ALU.mult, op1=ALU.mult), ['xT%d' % dch, 'rstd', 'gains'], ['xT%d' % dch])
                    else:
                        POOL(lambda h, dch=dch: h.tensor_tensor(out=xT[:, dch, :ntok], in0=xT[:, dch, :ntok], in1=rstd[:, :ntok], op=ALU.mult),
                             ['xT%d' % dch, 'rstd'], ['xT%d' % dch])
                        POOL(lambda h, dch=dch: h.tensor_scalar(out=xT[:, dch, :ntok], in0=xT[:, dch, :ntok], scalar1=gains[:, 3, dch:dch + 1], scalar2=None,
                                                                op0=ALU.mult), ['xT%d' % dch, 'gains'], ['xT%d' % dch])
                for bi, (t0, n) in enumerate(blks):
                    for q4 in range(2):
                        ps, pk = Rn4()
                        for j in range(4):
                            dch = q4 * 4 + j
                            PE(lambda h, t0=t0, n=n, j=j, dch=dch, ps=ps: h.transpose(ps[0:n, j * 128:(j + 1) * 128], xT[:, dch, t0:t0 + n], ident[:]),
                               ['xT%d' % dch, 'ident'], [pk], inc=(j == 3))
                        EV(yout[0:n, q4 * 512:(q4 + 1) * 512], ps[0:n, :], [pk], ['yo%d' % q4])
                        S.dma('sp', 'st_y', [(y_dst[t0:t0 + n, q4 * 512:(q4 + 1) * 512], yout[0:n, q4 * 512:(q4 + 1) * 512])],
                              reads=['yo%d' % q4], writes=['o_y%d' % q4])

            rw_tail = [None]
            wo_ctr = [0]

            def rwkv_chunk(cch, ntok, need_y):
                cs = slice(cch * 64, (cch + 1) * 64)
                ps, pk = Rn()
                PE(lambda h, ps=ps: h.matmul(ps[0:64, :], lhsT=tw[:, cs], rhs=w_w2_bf[:], start=True, stop=True), ['tw', 'smallw'], [pk])
                DVE(lambda h, ps=ps: h.tensor_tensor(out=sg[:], in0=ps[0:64, :], in1=bc[:, 0, :], op=ALU.add), [pk, 'bc'], ['sg'])
                ACT(lambda h: h.activation(out=sg[:], in_=sg[:], func=AF.Sigmoid), ['sg'], ['sg'])
                yield
                pcl, kcl = Rn()
                pce, kce = Rn()
                for hh in range(8):
                    PE(lambda h, hh=hh, pcl=pcl: h.matmul(pcl[0:64, hh * 64:(hh + 1) * 64], lhsT=sg[:, hh * 64:(hh + 1) * 64], rhs=tri[:, 0, :], start=True, stop=True),
                       ['sg', 'tri'], [kcl], inc=(hh == 7))
                for hh in range(8):
                    PE(lambda h, hh=hh, pce=pce: h.matmul(pce[0:64, hh * 64:(hh + 1) * 64], lhsT=sg[:, hh * 64:(hh + 1) * 64], rhs=tri[:, 1, :], start=True, stop=True),
                       ['sg', 'tri'], [kce], inc=(hh == 7))
                v3 = lambda t: t[0:64, :].rearrange("p (a b) -> p a b", a=8)
                ACT(lambda h: h.activation(out=gam[:], in_=v3(pcl), func=AF.Exp, scale=-C0), [kcl], ['gam'])
                ACT(lambda h: h.activation(out=ginv[:], in_=v3(pcl), func=AF.Exp, scale=C0), [kcl], ['ginv'])
                ACT(lambda h: h.activation(out=gprev[:], in_=v3(pce), func=AF.Exp, scale=-C0), [kce], ['gprev'])
                if need_y:
                    ckpt(30)
                yield
                DVE(lambda h: h.tensor_tensor(out=bT[:], in0=b_t[:, :, cs], in1=ginv[:], op=ALU.mult), ['b_t', 'ginv'], ['bT'])
                DVE(lambda h: h.tensor_tensor(out=kT[:], in0=kp_t[:, :, cs], in1=ginv[:], op=ALU.mult), ['kp_t', 'ginv'], ['kT'])
                DVE(lambda h: h.scalar_tensor_tensor(out=aT[:], in0=kk_t[:, :, cs], scalar=-1.0, in1=gprev[:], op0=ALU.mult, op1=ALU.mult),
                    ['kk_t', 'gprev'], ['aT'])
                POOL(lambda h: h.tensor_tensor(out=rT[:], in0=r_t[:, :, cs], in1=gam[:], op=ALU.mult), ['r_t', 'gam'], ['rT'])
                if need_y:
                    POOL(lambda h: h.tensor_tensor(out=prodT[:], in0=r_t[:, :, cs], in1=kp_t[:, :, cs], op=ALU.mult), ['r_t', 'kp_t'], ['prodT'])

                def tr8(src_fn, dst, dkey, rkeys):
                    ps, pk = Rn()
                    for hh in range(8):
                        PE(lambda h, hh=hh, ps=ps: h.matmul(ps[0:64, hh * 64:(hh + 1) * 64], lhsT=src_fn(hh), rhs=ident_bf[0:64, 0:64], start=True, stop=True),
                           rkeys + ['ident_bf'], [pk], inc=(hh == 7))
                    EV(dst[:], ps[0:64, :], [pk], [dkey])

                yield
                tr8(lambda hh: bT[:, hh, :], bt, 'bt', ['bT'])
                yield
                tr8(lambda hh: kT[:, hh, :], kt, 'kt', ['kT'])
                yield
                tr8(lambda hh: v_t[:, hh, cs], vt, 'vt', ['v_t'])
                yield

                def sc8(l_fn, r_fn, rkeys, mask_i, dst, dkey):
                    ps, pk = Rn()
                    for hh in range(8):
                        PE(lambda h, hh=hh, ps=ps: h.matmul(ps[0:64, hh * 64:(hh + 1) * 64], lhsT=l_fn(hh), rhs=r_fn(hh), start=True, stop=True),
                           rkeys, [pk], inc=(hh == 7))
                    DVE(lambda h, ps=ps: h.tensor_tensor(out=dst[:].rearrange("p (a b) -> p a b", a=8), in0=v3(ps),
                                                         in1=masks[:, mask_i, :].unsqueeze(1).to_broadcast([64, 8, 64]), op=ALU.mult),
                        [pk, 'masks'], [dkey])

                sc8(lambda hh: bT[:, hh, :], lambda hh: aT[:, hh, :], ['bT', 'aT'], 0, Qm[0], 'Qm0')
                yield
                sc8(lambda hh: aT[:, hh, :], lambda hh: bT[:, hh, :], ['bT', 'aT'], 1, Pm[0], 'Pm0')
                yield
                sc8(lambda hh: kT[:, hh, :], lambda hh: aT[:, hh, :], ['kT', 'aT'], 0, AKT, 'AKT')
                yield
                if need_y:
                    sc8(lambda hh: bT[:, hh, :], lambda hh: rT[:, hh, :], ['bT', 'rT'], 2, RBT, 'RBT')
                    yield
                    sc8(lambda hh: kT[:, hh, :], lambda hh: rT[:, hh, :], ['kT', 'rT'], 2, RKT, 'RKT')
                    yield
                if need_y:
                    ckpt(31)
                if rw_tail[0] is not None:
                    rw_tail[0]()
                    rw_tail[0] = None
                    yield
                POOL(lambda h: h.tensor_tensor(out=TT[0][:].rearrange("p (a b) -> p a b", a=8), in0=Qm[0][:].rearrange("p (a b) -> p a b", a=8),
                                               in1=ident_bf[0:64, 0:64].unsqueeze(1).to_broadcast([64, 8, 64]), op=ALU.add), ['Qm0', 'ident_bf'], ['TT0'])
                cur = 0
                tcur = 0

                def mm8(l, lk, r, rk, evac):
                    ps, pk = Rn()
                    for hh in range(8):
                        PE(lambda h, hh=hh, ps=ps: h.matmul(ps[0:64, hh * 64:(hh + 1) * 64], lhsT=l[:, hh * 64:(hh + 1) * 64], rhs=r[:, hh * 64:(hh + 1) * 64],
                                                            start=True, stop=True), [lk, rk], [pk], inc=(hh == 7))
                    evac(ps, pk)

                for j in range(5):
                    yield
                    nx = 1 - cur
                    mm8(Qm[cur], 'Qm%d' % cur, Pm[cur], 'Pm%d' % cur,
                        lambda ps, pk, nx=nx: ACT(lambda h: h.activation(out=Pm[nx][:], in_=ps[0:64, :], func=AF.Copy), [pk], ['Pm%d' % nx]))
                    if j < 4:
                        mm8(Pm[cur], 'Pm%d' % cur, Qm[cur], 'Qm%d' % cur,
                            lambda ps, pk, nx=nx: ACT(lambda h: h.activation(out=Qm[nx][:], in_=ps[0:64, :], func=AF.Copy), [pk], ['Qm%d' % nx]))
                    if j >= 1:
                        tn = 1 - tcur
                        mm8(Pm[cur], 'Pm%d' % cur, TT[tcur], 'TT%d' % tcur,
                            lambda ps, pk, tn=tn, tc=tcur: DVE(lambda h: h.tensor_tensor(out=TT[tn][:], in0=ps[0:64, :], in1=TT[tc][:], op=ALU.add),
                                                               [pk, 'TT%d' % tc], ['TT%d' % tn]))
                        tcur = tn
                    cur = nx
                yield
                tn = 1 - tcur
                mm8(Pm[cur], 'Pm%d' % cur, TT[tcur], 'TT%d' % tcur,
                    lambda ps, pk, tn=tn, tc=tcur: DVE(lambda h: h.tensor_tensor(out=TT[tn][:], in0=ps[0:64, :], in1=TT[tc][:], op=ALU.add),
                                                       [pk, 'TT%d' % tc], ['TT%d' % tn]))
                tcur = tn
                Tf = TT[tcur]
                Tk = 'TT%d' % tcur
                if need_y:
                    ckpt(32)
                yield
                ps, pk = Rn()
                for hh in range(8):
                    hs = slice(hh * 64, (hh + 1) * 64)
                    PE(lambda h, hh=hh, hs=hs, ps=ps: h.matmul(ps[0:64, hs], lhsT=aT[:, hh, :], rhs=Sbf[:, hh, :], start=True, stop=False), ['aT', 'Sbf'], [pk], inc=False)
                    PE(lambda h, hh=hh, hs=hs, ps=ps: h.matmul(ps[0:64, hs], lhsT=AKT[:, hs], rhs=vt[:, hs], start=False, stop=True), ['AKT', 'vt'], [pk], inc=(hh == 7))
                ACT(lambda h, ps=ps: h.activation(out=Xb[:], in_=ps[0:64, :], func=AF.Copy), [pk], ['Xb'])
                yield
                ps, pk = Rn()
                for hh in range(8):
                    hs = slice(hh * 64, (hh + 1) * 64)
                    PE(lambda h, hs=hs, ps=ps: h.matmul(ps[0:64, hs], lhsT=Tf[:, hs], rhs=Xb[:, hs], start=True, stop=True), [Tk, 'Xb'], [pk], inc=(hh == 7))
                DVE(lambda h, ps=ps: h.tensor_copy(out=Ub[:], in_=ps[0:64, :]), [pk], ['Ub'])
                yield
                if need_y:
                    psy, pky = Rn()
                    for hh in range(8):
                        hs = slice(hh * 64, (hh + 1) * 64)
                        PE(lambda h, hh=hh, hs=hs: h.matmul(psy[0:64, hs], lhsT=rT[:, hh, :], rhs=Sbf[:, hh, :], start=True, stop=False), ['rT', 'Sbf'], [pky], inc=False)
                        PE(lambda h, hs=hs: h.matmul(psy[0:64, hs], lhsT=RBT[:, hs], rhs=Ub[:, hs], start=False, stop=False), ['RBT', 'Ub'], [pky], inc=False)
                        PE(lambda h, hs=hs: h.matmul(psy[0:64, hs], lhsT=RKT[:, hs], rhs=vt[:, hs], start=False, stop=True), ['RKT', 'vt'], [pky], inc=(hh == 7))
                if need_y:
                    ckpt(34)
                yield
                pss, pks = Rn()
                for hh in range(8):
                    hs = slice(hh * 64, (hh + 1) * 64)
                    PE(lambda h, hs=hs: h.matmul(pss[0:64, hs], lhsT=bt[:, hs], rhs=Ub[:, hs], start=True, stop=False), ['bt', 'Ub'], [pks], inc=False)
                    PE(lambda h, hs=hs: h.matmul(pss[0:64, hs], lhsT=kt[:, hs], rhs=vt[:, hs], start=False, stop=True), ['kt', 'vt'], [pks], inc=(hh == 7))
                DVE(lambda h: h.tensor_tensor(out=S32[:], in0=v3(pss), in1=S32[:], op=ALU.add), [pks, 'S32'], ['S32'])
                DVE(lambda h: h.tensor_tensor(out=Sbf[:], in0=S32[:], in1=gam[:, :, 63:64].to_broadcast([64, 8, 64]), op=ALU.mult), ['S32', 'gam'], ['Sbf'])
                DVE(lambda h: h.tensor_tensor(out=S32[:], in0=S32[:], in1=gam[:, :, 63:64].to_broadcast([64, 8, 64]), op=ALU.mult), ['S32', 'gam'], ['S32'])
                yield
                if not need_y:
                    return
                ckpt(35)
                ACT(lambda h: h.activation(out=ytm[:], in_=v3(psy), func=AF.Copy), [pky], ['ytm'])
                ACT(lambda h: h.activation(out=ysq[:], in_=v3(psy), func=AF.Square), [pky], ['ysq'])
                DVE(lambda h: h.tensor_reduce(out=st8[:, 0, :], in_=ytm[:], axis=AX.X, op=ALU.add), ['ytm'], ['st8a'])
                DVE(lambda h: h.tensor_reduce(out=st8[:, 1, :], in_=ysq[:], axis=AX.X, op=ALU.add), ['ysq'], ['st8b'])
                DVE(lambda h: h.tensor_scalar(out=st8[:, 2, :], in0=st8[:, 0, :], scalar1=1.0 / 64, scalar2=None, op0=ALU.mult), ['st8a'], ['st8c'])
                DVE(lambda h: h.tensor_tensor(out=st8[:, 5, :], in0=st8[:, 2, :], in1=st8[:, 2, :], op=ALU.mult), ['st8c'], ['st8f'])
                DVE(lambda h: h.scalar_tensor_tensor(out=st8[:, 3, :], in0=st8[:, 1, :], scalar=1.0 / 64, in1=st8[:, 5, :], op0=ALU.mult, op1=ALU.subtract),
                    ['st8b', 'st8f'], ['st8d'])
                DVE(lambda h: h.tensor_scalar(out=st8[:, 3, :], in0=st8[:, 3, :], scalar1=GN_EPS, scalar2=None, op0=ALU.add), ['st8d'], ['st8d'])
                ACT(lambda h: h.activation(out=st8[:, 3, :], in_=st8[:, 3, :], func=AF.Sqrt), ['st8d'], ['st8d'])
                DVE(lambda h: h.reciprocal(out=st8[:, 3, :], in_=st8[:, 3, :]), ['st8d'], ['st8d'])
                DVE(lambda h: h.tensor_tensor(out=ytm[:], in0=ytm[:], in1=st8[:, 2, :].unsqueeze(2).to_broadcast([64, 8, 64]), op=ALU.subtract), ['ytm', 'st8c'], ['ytm'])
                DVE(lambda h: h.tensor_tensor(out=ytm[:], in0=ytm[:], in1=st8[:, 3, :].unsqueeze(2).to_broadcast([64, 8, 64]), op=ALU.mult), ['ytm', 'st8d'], ['ytm'])
                yf = ytm[:].rearrange("p a b -> p (a b)")
                DVE(lambda h: h.tensor_tensor(out=yf, in0=yf, in1=bc[:, 1, :], op=ALU.mult), ['ytm', 'bc'], ['ytm'])
                DVE(lambda h: h.tensor_tensor(out=yf, in0=yf, in1=bc[:, 2, :], op=ALU.add), ['ytm', 'bc'], ['ytm'])
                yield
                ps, pk = Rn()
                for hh in range(8):
                    PE(lambda h, hh=hh, ps=ps: h.matmul(ps[0:64, hh:hh + 1], lhsT=prodT[:, hh, :], rhs=rk_bf[:, hh:hh + 1], start=True, stop=True),
                       ['prodT', 'rk_bf'], [pk], inc=(hh == 7))
                ACT(lambda h, ps=ps: h.activation(out=st8[:, 4, :], in_=ps[0:64, 0:8], func=AF.Copy), [pk], ['st8e'])
                DVE(lambda h: h.tensor_tensor(out=ysq[:], in0=vt[:].rearrange("p (a b) -> p a b", a=8), in1=st8[:, 4, :].unsqueeze(2).to_broadcast([64, 8, 64]), op=ALU.mult),
                    ['vt', 'st8e'], ['ysq'])
                DVE(lambda h: h.tensor_tensor(out=ytm[:], in0=ytm[:], in1=ysq[:], op=ALU.add), ['ytm', 'ysq'], ['ytm'])
                ckpt(36)
                yield
                ps, pk = Rn()
                PE(lambda h, ps=ps: h.matmul(ps[0:64, :], lhsT=sgl[:, cs], rhs=w_g2_bf[:], start=True, stop=True), ['sgl', 'smallw'], [pk])
                DVE(lambda h, ps=ps: h.tensor_tensor(out=rwo[:], in0=ps[0:64, :], in1=yf, op=ALU.mult), [pk, 'ytm'], ['rwo'])
                def tail():
                    ps, pk = Rn()
                    for hh in range(8):
                        hs = slice(hh * 64, (hh + 1) * 64)
                        PE(lambda h, hs=hs, ps=ps: h.matmul(ps[0:64, hs], lhsT=rwo[:, hs], rhs=ident_bf[0:64, 0:64], start=True, stop=True), ['rwo', 'ident_bf'], [pk], inc=(hh == 7))
                    ACT(lambda h, ps=ps: h.activation(out=rwoT[:, :, cs], in_=v3(ps), func=AF.Copy), [pk], ['rwoT'])

                rw_tail[0] = tail

            tmpWA = c32

            POOL(lambda h: h.memset(xin[0:64, 0, :], 0.0), ['xin0'], ['xin0'])
            S.dma('sp', 'ld_x', [(xin[48:64, 0, :], meta)], reads=['xin0'], writes=['xin0'])
            POOL(lambda h: h.memset(carry[:], 0.0), [], ['carry%d' % c_ for c_ in list(range(0, 24, 2)) + [24, 25]])
            POOL(lambda h: h.memset(S32[:], 0.0), [], ['S32'])
            POOL(lambda h: h.memset(Sbf[:], 0.0), [], ['Sbf'])
            POOL(lambda h: h.memset(als[:], 0.0), [], ['als'])

            def load_x_prompt(si, ti):
                def f():
                    S.dma('sp', 'ld_x', [(xin[:, b, :], xp[si, ti * NT + b * 128: ti * NT + (b + 1) * 128, :]) for b in range(2)],
                          writes=['xin0', 'xin1'])
                return f

            def load_x_sample():
                S.dma('sp', 'ld_x', [(xin[0:64, 0, :], xs)], writes=['xin0'])

            ntile = SEQ // NT
            tile(64, None, [tabM[i] for i in range(4)], 0, [], False, None, None, None, is_meta=True,
                 nxt=(load_x_prompt(0, 0) if NSEQ > 0 else load_x_sample))
            ckpt(10)
            DVE(lambda h: h.tensor_copy(out=S32m[:], in_=S32[:]), ['S32'], ['S32m'])
            DVE(lambda h: h.tensor_copy(out=carry_m[:], in_=carry[:]), ['carry%d' % c_ for c_ in list(range(0, 24, 2)) + [24, 25]], ['carry_m'])

            def store_state(wkv_dst, sh_dst):
                for half in range(2):
                    ps, pk = Rn()
                    for j in range(4):
                        hh = half * 4 + j
                        PE(lambda h, hh=hh, j=j, ps=ps: h.transpose(ps[0:64, j * 64:(j + 1) * 64], S32[:, hh, :], ident[0:64, 0:64]), ['S32', 'ident'], [pk], inc=(j == 3))
                    ACT(lambda h, half=half, ps=ps: h.activation(out=ytm[:, half * 4:(half + 1) * 4, :], in_=ps[0:64, 0:256].rearrange("p (a b) -> p a b", a=4), func=AF.Copy),
                        [pk], ['ytm'])
                S.dma('pool', 'st_s', [(wkv_dst.rearrange("h v k -> v h k"), ytm[:])], reads=['ytm'], writes=['o_s'])
                S.dma('pool', 'st_s', [(sh_dst[0:1536].rearrange("(c p) -> p c", p=64), carry[0:64, 0:24]),
                                       (sh_dst[1536:1792].rearrange("(c p) -> p c", p=128), carry[:, 24:26])],
                      reads=['carry%d' % c_ for c_ in list(range(0, 24, 2)) + [24, 25]], writes=['o_s'], allow_slow_non_contiguous=True)

            for si in range(NSEQ):
                DVE(lambda h: h.tensor_copy(out=S32[:], in_=S32m[:]), ['S32m'], ['S32'])
                ACT(lambda h: h.activation(out=Sbf[:], in_=S32m[:], func=AF.Copy), ['S32m'], ['Sbf'])
                DVE(lambda h: h.tensor_copy(out=carry[:], in_=carry_m[:]), ['carry_m'], ['carry%d' % c_ for c_ in list(range(0, 24, 2)) + [24, 25]])
                for ti in range(ntile):
                    if ti + 1 < ntile:
                        nxt = load_x_prompt(si, ti + 1)
                    elif si + 1 < NSEQ:
                        nxt = load_x_prompt(si + 1, 0)
                    else:
                        nxt = load_x_sample
                    p0 = N_META + ti * NT
                    fb = [(cT_c[:, b * 128:(b + 1) * 128], krT_c[:, b * 128:(b + 1) * 128], ctok_c[:, b, :], 128) for b in range(ti * NT // 128)]
                    tile(NT, None, [tabP[i, :, p0:p0 + NT] for i in range(4)], ti * NT, fb, True,
                         y_p[si, ti * NT:(ti + 1) * NT, :], ckv_p[si, p0:p0 + NT, :], kr_p[si, p0:p0 + NT, :], nxt=nxt)
                store_state(wkv_p[si], sh_p[si])

            ckpt(50)
            S.dma('pool', 'ld_s', [(dmix[:, :, :].rearrange("p a b -> p (a b)")[:, 0:512].rearrange("p (a b) -> p a b", a=16), ckr.rearrange("(b p) r -> p b r", p=128))],
                  writes=['dmix'])
            S.dma('pool', 'ld_s', [(ytm[:], swkv.rearrange("h v k -> v h k"))], writes=['ytm'])
            krv = dmix[:, :, :].rearrange("p a b -> p (a b)")[:, 0:512].rearrange("p (a b) -> p a b", a=16)
            ckvv = None

            def sample_cache_c():
                pass

            S.dma('sp', 'ld_x', [(xin[:, 1, :].rearrange("p (b c) -> p b c", b=8), cckv[0:1024, :].rearrange("(b p) c -> p b c", p=128))], writes=['xin1'])
            for half in range(2):
                if half == 1:
                    S.dma('sp', 'ld_x', [(xin[:, 1, :].rearrange("p (b c) -> p b c", b=8), cckv[1024:2048, :].rearrange("(b p) c -> p b c", p=128))],
                          reads=['xin1'], writes=['xin1'])
                xv = xin[:, 1, :].rearrange("p (b c) -> p b c", b=8)
                DVE(lambda h, half=half, xv=xv: h.tensor_copy(out=ctok_c[:, half * 8:(half + 1) * 8, :], in_=xv), ['xin1'], ['ctok_c'])
                for b in range(8):
                    ps, pk = Rn()
                    PE(lambda h, b=b, ps=ps, xv=xv: h.transpose(ps[:, 0:128], xv[:, b, :], ident[:]), ['xin1', 'ident'], [pk])
                    EV(cT_c[:, (half * 8 + b) * 128:(half * 8 + b + 1) * 128], ps[:, 0:128], [pk], ['cT_c'])
            for b in range(16):
                ps, pk = Rn()
                PE(lambda h, b=b, ps=ps: h.transpose(ps[0:32, 0:128], krv[:, b, :], ident[:]), ['dmix', 'ident'], [pk])
                EV(krT_c[:, b * 128:(b + 1) * 128], ps[0:32, 0:128], [pk], ['krT_c'])
            swv = ytm
            for half in range(2):
                ps, pk = Rn()
                for j in range(4):
                    hh = half * 4 + j
                    PE(lambda h, hh=hh, j=j, ps=ps: h.transpose(ps[0:64, j * 64:(j + 1) * 64], swv[:, hh, :], ident[0:64, 0:64]), ['ytm', 'ident'], [pk], inc=(j == 3))
                DVE(lambda h, half=half, ps=ps: h.tensor_copy(out=S32[:, half * 4:(half + 1) * 4, :], in_=ps[0:64, 0:256].rearrange("p (a b) -> p a b", a=4)), [pk], ['S32'])
            ACT(lambda h: h.activation(out=Sbf[:], in_=S32[:], func=AF.Copy), ['S32'], ['Sbf'])
            S.dma('sp', 'ld_x', [(carry[0:64, 0:24], ssh[0:1536].rearrange("(c p) -> p c", p=64)),
                                 (carry[:, 24:26], ssh[1536:1792].rearrange("(c p) -> p c", p=128))], reads=['carry%d' % c_ for c_ in list(range(0, 24, 2)) + [24, 25]], writes=['carry%d' % c_ for c_ in list(range(0, 24, 2)) + [24, 25]],
                  allow_slow_non_contiguous=True)
            if NSEQ == 0:
                pass
            fb = [(cT_c[:, b * 128:(b + 1) * 128], krT_c[:, b * 128:(b + 1) * 128], ctok_c[:, b, :], 128) for b in range(16)]
            tile(64, None, [tabS[i] for i in range(4)], PAST, fb, True, y_s, ckv_s, kr_s)
            store_state(wkv_s, sh_s)


        except StopBuild:
            pass
        S.finish('sp')
        S.emit()
        print("ops:", S.nops, "sems:", S.nsem, flush=True)
    return nc


def make_consts(SEQ):
    ident = np.eye(128, dtype=np.float32)
    i = np.arange(64)[:, None]
    j = np.arange(64)[None, :]
    import ml_dtypes
    masks = np.stack([(i < j), (i > j), (i <= j)]).astype(np.float32).astype(ml_dtypes.bfloat16)
    tri = np.stack([(i <= j), (i < j)]).astype(np.float32)
    half = 16
    inv = (10000.0 ** (-np.arange(half, dtype=np.float32) / half)).astype(np.float32)

    def tab(pos):
        ang = pos.astype(np.float32)[None, :] * inv[:, None]
        cos = np.cos(ang).astype(np.float32)
        sin = np.sin(ang).astype(np.float32)
        c2 = np.concatenate([cos, cos], 0)
        s2 = np.concatenate([sin, sin], 0)
        sc = np.float32(MLA_SCALE)
        return np.stack([c2, s2, c2 * sc, s2 * sc]).astype(np.float32)

    tabp = tab(np.arange(N_META + SEQ))
    tabs = tab(N_META + PAST + np.arange(64))
    tabm = tab(np.concatenate([np.zeros(48), np.arange(16)]))
    return dict(c_ident=ident, c_masks=masks, c_tri=tri, c_tabp=tabp, c_tabs=tabs, c_tabm=tabm)


_CACHE = {}


def run(inputs, SEQ=4096, NSEQ=2, ncores=8, stop=99):
    key = (SEQ, NSEQ, stop)
    if key not in _CACHE:
        _CACHE[key] = build(SEQ, NSEQ, stop)
    nc = _CACHE[key]
    consts = make_consts(SEQ)
    f32 = lambda a: np.ascontiguousarray(np.asarray(a, dtype=np.float32))
    wmap = {}
    for n in W_NAMES:
        a = f32(inputs[n])
        if n != "final_norm":
            a = a[0]
        wmap[n] = np.ascontiguousarray(a.reshape(W_SHAPES[n]))
    in_maps = []
    for c in range(ncores):
        m = dict(wmap)
        m.update(consts)
        m["xp"] = f32(inputs["x_prompt"][c * NSEQ:(c + 1) * NSEQ, :SEQ])
        m["xs"] = f32(inputs["x_sample"][c])
        m["cckv"] = f32(inputs["cache_ckv"][0, c])
        m["ckr"] = f32(inputs["cache_krope"][0, c])
        m["swkv"] = f32(inputs["state_wkv"][0, c])
        m["ssh"] = f32(inputs["state_shift"][0, c, 0])
        m["meta"] = f32(inputs["meta_tokens"])
        in_maps.append(m)
    res = run_bass_kernel_spmd(nc, in_maps, core_ids=list(range(ncores)))
    R = res.results
    cat = lambda k: np.concatenate([np.asarray(r[k]) for r in R], axis=0)
    stk = lambda k: np.stack([np.asarray(r[k]) for r in R], axis=0)
    y_p = cat("y_p")
    y_s = stk("y_s")
    outs = (y_p, y_s,
            cat("ckv_p")[None], cat("kr_p")[None], cat("wkv_p")[None], cat("sh_p")[None, :, None, :],
            stk("ckv_s")[None], stk("kr_s")[None], stk("wkv_s")[None], stk("sh_s")[None, :, None, :])
    return tuple(np.ascontiguousarray(o.astype(np.float32)) for o in outs)


def kernel(**inputs):
    return run(inputs, SEQ=4096, NSEQ=2, ncores=8)
```
